# Optimizing a Trainium2 kernel written in Bass

```python
import math
import jax
import jax.numpy as jnp
from jax import lax
import numpy as np

D_MODEL = 1024
BATCH = 8
SEQ = 4096
DEPTH = 1

HEAD_DIM = 64
SSD_HEADS = 16
SSD_GROUPS = 2
SSD_STATE = 128
SSD_CONV = 4
SSD_CHUNK = 128
SSD_WIDTH = SSD_HEADS * HEAD_DIM
SSD_BC_WIDTH = SSD_GROUPS * SSD_STATE
SSD_XBC_WIDTH = SSD_WIDTH + 2 * SSD_BC_WIDTH
FOX_HEADS = 16
FOX_WIDTH = FOX_HEADS * HEAD_DIM
Q_BLOCK = 128
MIX_WIDTH = SSD_WIDTH + FOX_WIDTH
D_FF = 2816
FFN_CONV = 3
NORM_EPS = 1e-6
Z_END = SSD_WIDTH
XBC_END = Z_END + SSD_XBC_WIDTH
DT_END = XBC_END + SSD_HEADS
Q_END = DT_END + FOX_WIDTH
K_END = Q_END + FOX_WIDTH
V_END = K_END + FOX_WIDTH
IN_COLS = V_END + FOX_HEADS

kernel_name = 'hybrid_ssd_fox_convffn_layer'


def rms_norm(x, w):
    xf = x.astype(jnp.float32)
    y = xf * lax.rsqrt(jnp.mean(xf * xf, axis=-1, keepdims=True) + NORM_EPS)
    return (y * w.astype(jnp.float32)).astype(x.dtype)


def causal_depthwise_conv(x, w, b):
    k = w.shape[0]
    y = lax.conv_general_dilated(
        x, w[:, None, :].astype(x.dtype), window_strides=(1,), padding=[(k - 1, 0)],
        dimension_numbers=('NWC', 'WIO', 'NWC'), feature_group_count=x.shape[-1])
    return y + b.astype(x.dtype)


def ssd_mixer(xbc_raw, z, dt_raw, conv_w, conv_b, dt_bias, a_log, d_skip, norm_w):
    b, s, _ = xbc_raw.shape
    nc = s // SSD_CHUNK
    e = SSD_HEADS // SSD_GROUPS
    xbc = jax.nn.silu(causal_depthwise_conv(xbc_raw, conv_w, conv_b))
    xs, bm, cm = jnp.split(xbc, [SSD_WIDTH, SSD_WIDTH + SSD_BC_WIDTH], axis=-1)
    x6 = xs.reshape(b, nc, SSD_CHUNK, SSD_GROUPS, e, HEAD_DIM)
    bm = bm.reshape(b, nc, SSD_CHUNK, SSD_GROUPS, SSD_STATE)
    cm = cm.reshape(b, nc, SSD_CHUNK, SSD_GROUPS, SSD_STATE)
    dt = jax.nn.softplus(dt_raw.astype(jnp.float32) + dt_bias.astype(jnp.float32))
    a = -jnp.exp(a_log.astype(jnp.float32))
    dt5 = dt.reshape(b, nc, SSD_CHUNK, SSD_GROUPS, e)
    a_cs = jnp.cumsum(dt5 * a.reshape(SSD_GROUPS, e), axis=2)
    xdt = x6 * dt5[..., None]
    at = jnp.moveaxis(a_cs, 2, -1)
    seg = at[..., :, None] - at[..., None, :]
    tri = jnp.tril(jnp.ones((SSD_CHUNK, SSD_CHUNK), dtype=bool))
    decay = jnp.exp(jnp.where(tri, seg, -jnp.inf))
    cb = jnp.einsum('bclgn,bcsgn->bcgls', cm, bm)
    scores = cb[:, :, :, None] * decay
    y_diag = jnp.einsum('bcgels,bcsgep->bclgep', scores, xdt)
    decay_to_end = jnp.exp(a_cs[:, :, -1:] - a_cs)
    chunk_states = jnp.einsum('bclgn,bclgep->bcgepn', bm, xdt * decay_to_end[..., None])
    chunk_decay = jnp.exp(a_cs[:, :, -1])

    def step(h, inp):
        dec, st = inp
        return h * dec[..., None, None] + st, h

    h0 = jnp.zeros((b, SSD_GROUPS, e, HEAD_DIM, SSD_STATE), jnp.float32)
    _, states_in = lax.scan(step, h0, (jnp.moveaxis(chunk_decay, 1, 0), jnp.moveaxis(chunk_states, 1, 0)))
    states_in = jnp.moveaxis(states_in, 0, 1)
    y_off = jnp.einsum('bclgn,bcgepn->bclgep', cm, states_in) * jnp.exp(a_cs)[..., None]
    y = y_diag + y_off + d_skip.astype(jnp.float32).reshape(SSD_GROUPS, e)[:, :, None] * x6
    y = y.reshape(b, s, SSD_WIDTH)
    yg = (y * jax.nn.silu(z.astype(jnp.float32))).reshape(b, s, SSD_GROUPS, SSD_WIDTH // SSD_GROUPS)
    yg = yg * lax.rsqrt(jnp.mean(yg * yg, axis=-1, keepdims=True) + NORM_EPS)
    return (yg.reshape(b, s, SSD_WIDTH) * norm_w.astype(jnp.float32)).astype(xbc_raw.dtype)


def fox_mixer(q, k, v, f_raw, f_bias, q_norm_w, k_norm_w):
    b, s, _ = q.shape

    def heads(t):
        return t.reshape(b, s, FOX_HEADS, HEAD_DIM).transpose(0, 2, 1, 3)

    qh = rms_norm(heads(q), q_norm_w)
    kh = rms_norm(heads(k), k_norm_w)
    vh = heads(v)
    log_f = jax.nn.log_sigmoid(f_raw.astype(jnp.float32) + f_bias.astype(jnp.float32))
    cum = jnp.cumsum(log_f, axis=1).transpose(0, 2, 1)
    scale = HEAD_DIM ** -0.5
    outs = []
    for start in range(0, s, Q_BLOCK):
        end = start + Q_BLOCK
        logits = jnp.einsum('bhqd,bhkd->bhqk', qh[:, :, start:end], kh[:, :, :end],
                            preferred_element_type=jnp.float32) * scale
        logits = logits + cum[:, :, start:end, None] - cum[:, :, None, :end]
        causal = (start + jnp.arange(Q_BLOCK))[:, None] >= jnp.arange(end)[None, :]
        p = jax.nn.softmax(jnp.where(causal, logits, -jnp.inf), axis=-1)
        outs.append(jnp.einsum('bhqk,bhkd->bhqd', p.astype(vh.dtype), vh[:, :, :end]))
    o = jnp.concatenate(outs, axis=2)
    return o.transpose(0, 2, 1, 3).reshape(b, s, FOX_WIDTH)


def conv_gated_mlp(x, w_up, conv_w, conv_b, w_down):
    h = x @ w_up.astype(x.dtype)
    h = causal_depthwise_conv(h, conv_w, conv_b)
    gate, val = jnp.split(h, 2, axis=-1)
    return (jax.nn.silu(gate) * val) @ w_down.astype(x.dtype)


def setup_inputs(seed: int = 0) -> dict:
    key = jax.random.key(seed)
    ks = jax.random.split(key, 20)
    L = DEPTH

    def nrm(k, shape, scale):
        return jax.random.normal(k, shape, jnp.float32) * scale

    def gain(k, n):
        return 1.0 + 0.02 * jax.random.normal(k, (L, n), jnp.float32)

    dt = jnp.exp(jax.random.uniform(ks[5], (L, SSD_HEADS), jnp.float32, math.log(1e-3), math.log(1e-1)))
    dt_bias = dt + jnp.log(-jnp.expm1(-dt))
    return {
        'x': nrm(ks[0], (BATCH, SEQ, D_MODEL), 1.0),
        'norm_mix_w': gain(ks[1], D_MODEL),
        'w_in': nrm(ks[2], (L, D_MODEL, IN_COLS), D_MODEL ** -0.5),
        'ssd_conv_w': nrm(ks[3], (L, SSD_CONV, SSD_XBC_WIDTH), SSD_CONV ** -0.5),
        'ssd_conv_b': nrm(ks[4], (L, SSD_XBC_WIDTH), 0.02),
        'ssd_dt_bias': dt_bias,
        'ssd_a_log': jnp.log(jax.random.uniform(ks[6], (L, SSD_HEADS), jnp.float32, 1.0, 16.0)),
        'ssd_d': 1.0 + 0.1 * jax.random.normal(ks[7], (L, SSD_HEADS), jnp.float32),
        'ssd_norm_w': gain(ks[8], SSD_WIDTH),
        'fox_f_bias': jax.random.uniform(ks[9], (L, FOX_HEADS), jnp.float32, 2.0, 5.0),
        'fox_q_norm_w': gain(ks[10], HEAD_DIM),
        'fox_k_norm_w': gain(ks[11], HEAD_DIM),
        'w_out': nrm(ks[12], (L, MIX_WIDTH, D_MODEL), MIX_WIDTH ** -0.5),
        'norm_ffn_w': gain(ks[13], D_MODEL),
        'w_up': nrm(ks[14], (L, D_MODEL, 2 * D_FF), D_MODEL ** -0.5),
        'ffn_conv_w': nrm(ks[15], (L, FFN_CONV, 2 * D_FF), FFN_CONV ** -0.5),
        'ffn_conv_b': nrm(ks[16], (L, 2 * D_FF), 0.02),
        'w_down': nrm(ks[17], (L, D_FF, D_MODEL), D_FF ** -0.5),
    }


def reference(x, norm_mix_w, w_in, ssd_conv_w, ssd_conv_b, ssd_dt_bias, ssd_a_log, ssd_d,
              ssd_norm_w, fox_f_bias, fox_q_norm_w, fox_k_norm_w, w_out, norm_ffn_w,
              w_up, ffn_conv_w, ffn_conv_b, w_down):
    for layer in range(DEPTH):
        h = rms_norm(x, norm_mix_w[layer])
        proj = h @ w_in[layer].astype(h.dtype)
        z, xbc, dt_raw, q, k, v, f_raw = jnp.split(
            proj, [Z_END, XBC_END, DT_END, Q_END, K_END, V_END], axis=-1)
        y_ssd = ssd_mixer(xbc, z, dt_raw, ssd_conv_w[layer], ssd_conv_b[layer], ssd_dt_bias[layer],
                          ssd_a_log[layer], ssd_d[layer], ssd_norm_w[layer])
        y_fox = fox_mixer(q, k, v, f_raw, fox_f_bias[layer], fox_q_norm_w[layer], fox_k_norm_w[layer])
        mixed = jnp.concatenate([y_ssd, y_fox.astype(y_ssd.dtype)], axis=-1)
        x = x + (mixed @ w_out[layer].astype(mixed.dtype)).astype(x.dtype)
        hf = rms_norm(x, norm_ffn_w[layer])
        x = x + conv_gated_mlp(hf, w_up[layer], ffn_conv_w[layer], ffn_conv_b[layer], w_down[layer]).astype(x.dtype)
    return x
```

```python
import contextlib
import numpy as np
import concourse.bass as bass
import concourse.mybir as mybir
from concourse.bass_utils import run_bass_kernel_spmd

F32 = mybir.dt.float32
BF16 = mybir.dt.bfloat16
AF = mybir.ActivationFunctionType
ALU = mybir.AluOpType

PE, ACT, DVE, POOL, SP = "tensor", "scalar", "vector", "gpsimd", "sync"
EPOCH = 12000

S = 4096
D = 1024
NT = 8
IN_COLS = 5664
DFF = 2816
EPS = 1e-6
DEBUG = False
UPTO = 5
LIMIT = None


class StopBuild(Exception):
    pass


class Buf:
    def __init__(self, name, multi=False):
        self.name = name
        self.multi = multi
        self.psum = False
        self.writers = []
        self.readers = []


class Op:
    __slots__ = ("eng", "fn", "reads", "writes", "dma", "deps", "need", "tok", "dbuf", "idx")

    def __init__(self, eng, fn, reads, writes, dma, dbuf):
        self.eng = eng
        self.fn = fn
        self.reads = reads
        self.writes = writes
        self.dma = dma
        self.dbuf = dbuf
        self.deps = set()
        self.need = False
        self.tok = None


class Prog:
    def __init__(self):
        self.ops = []

    def op(self, eng, fn, reads=(), writes=(), dma=False, dbuf=None):
        o = Op(eng, fn, tuple(reads), tuple(writes), dma, dbuf)
        o.idx = len(self.ops)
        self.ops.append(o)
        return o

    def dma(self, fn, reads=(), writes=(), sbuf=None, eng=SP):
        return self.op(eng, fn, reads, writes, dma=True, dbuf=sbuf)

    def barrier(self):
        last = {}
        last_dma = {}
        for o in self.ops:
            if o.dma:
                last_dma[id(o.dbuf)] = o.idx
            elif o.fn is not None:
                last[o.eng] = o.idx
        deps = set(last.values()) | set(last_dma.values())
        for eng in (SP, PE, ACT, DVE, POOL):
            j = Op(eng, None, (), (), False, None)
            j.deps = set(deps)
            j.idx = len(self.ops)
            self.ops.append(j)

    def analyze(self):
        last_dma_on = {}
        for o in self.ops:
            deps = o.deps
            for b in o.reads:
                deps.update(b.writers)
                if b.psum:
                    for r in b.readers:
                        if self.ops[r].eng != o.eng:
                            deps.add(r)
            for b in o.writes:
                if not b.multi:
                    deps.update(b.writers)
                deps.update(b.readers)
            if o.dma:
                p = last_dma_on.get(id(o.dbuf))
                if p is not None:
                    deps.add(p)
                last_dma_on[id(o.dbuf)] = o.idx
            wset = set(id(b) for b in o.writes)
            for b in o.reads:
                if id(b) not in wset:
                    b.readers.append(o.idx)
            for b in o.writes:
                if b.multi:
                    b.writers.append(o.idx)
                else:
                    b.writers = [o.idx]
                b.readers = []
            deps.discard(o.idx)
            if o.eng == PE and not o.dma:
                for d in list(deps):
                    od = self.ops[d]
                    if od.eng == PE and not od.dma:
                        deps.discard(d)
            for d in deps:
                self.ops[d].need = True

    def emit(self, nc, final_wait_bufs=()):
        if LIMIT is not None:
            self.ops = self.ops[:LIMIT]
        self.analyze()
        join = Op(SP, None, (), (), False, None)
        join.idx = len(self.ops)
        for b in final_wait_bufs:
            for w in b.writers:
                join.deps.add(w)
                self.ops[w].need = True
        self.ops.append(join)

        eng_count = {PE: 0, ACT: 0, DVE: 0, POOL: 0, SP: 0}
        dma_slots = {}
        eng_sems = {}
        for o in self.ops:
            if o.dma:
                key = id(o.dbuf)
                k = dma_slots.get(key, 0) + 1
                dma_slots[key] = k
                o.tok = ("d", key, 16 * k)
            elif o.need and o.fn is not None:
                n = eng_count[o.eng]
                eng_count[o.eng] = n + 1
                o.tok = ("e", (o.eng, n // EPOCH), (n % EPOCH) + 1)
                eng_sems[(o.eng, n // EPOCH)] = True
        sem_keys = list(eng_sems.keys()) + [("dma", k) for k in dma_slots.keys()]
        es = contextlib.ExitStack()
        with es:
            sems = {}
            for i, k in enumerate(sem_keys):
                sems[k] = es.enter_context(nc.semaphore(f"s{i}"))
            block = es.enter_context(nc.Block())
            ops = self.ops

            def run_engine(engname):
                def body(e):
                    waited = {}
                    for o in ops:
                        if o.eng != engname:
                            continue
                        need = {}
                        for d in o.deps:
                            t = ops[d].tok
                            if t is None:
                                continue
                            k = (t[0], t[1])
                            if need.get(k, 0) < t[2]:
                                need[k] = t[2]
                        for k, v in need.items():
                            if waited.get(k, 0) >= v:
                                continue
                            waited[k] = v
                            s = sems[("dma", k[1])] if k[0] == "d" else sems[k[1]]
                            e.wait_ge(s, v)
                        if o.fn is None:
                            continue
                        ins = o.fn(e)
                        if o.tok is not None:
                            if o.tok[0] == "d":
                                ins.then_inc(sems[("dma", o.tok[1])], 16)
                            else:
                                ins.then_inc(sems[o.tok[1]], 1)
                return body

            block.sync(run_engine(SP))
            block.tensor(run_engine(PE))
            block.scalar(run_engine(ACT))
            block.vector(run_engine(DVE))
            block.gpsimd(run_engine(POOL))
        return len(sem_keys)


def I_mm(out, lhsT, rhs, start=True, stop=True):
    return lambda e: e.matmul(out, lhsT, rhs, start=start, stop=stop)


def I_tr(out, in_, ident):
    return lambda e: e.transpose(out, in_, ident)


def I_act(out, in_, func, bias=None, scale=None, accum=None):
    kw = {}
    if bias is not None:
        kw["bias"] = bias
    if scale is not None:
        kw["scale"] = scale
    if accum is not None:
        kw["accum_out"] = accum
    return lambda e: e.activation(out, in_, func, **kw)


def I_ts(out, in0, s1, s2, op0, op1=None):
    if op1 is None:
        return lambda e: e.tensor_scalar(out, in0, s1, None, op0)
    return lambda e: e.tensor_scalar(out, in0, s1, s2, op0, op1)


def I_tt(out, in0, in1, op):
    return lambda e: e.tensor_tensor(out, in0, in1, op)


def I_stt(out, in0, scalar, in1, op0, op1):
    return lambda e: e.scalar_tensor_tensor(out, in0, scalar, in1, op0, op1)


def I_copy(out, in_):
    return lambda e: e.tensor_copy(out, in_)


def I_recip(out, in_):
    return lambda e: e.reciprocal(out, in_)


def I_memset(ap, v):
    return lambda e: e.memset(ap, v)


def I_dma(out, in_):
    return lambda e: e.dma_start(out=out, in_=in_)


def I_scan(out, d0, d1, init, op0, op1):
    return lambda e: e.tensor_tensor_scan(out, d0, d1, init, op0, op1)


class T:
    def __init__(self, ap, name):
        self.ap = ap
        self.b = Buf(name)


class Arena:
    def __init__(self, ap, cap):
        self.ap = ap
        self.cap = cap
        self.off = 0

    def alloc(self, shape, dt, name):
        esz = 2 if dt == BF16 else 4
        n = int(np.prod(shape[1:]))
        nb = (n * esz + 63) // 64 * 64
        off = self.off
        self.off += nb
        assert self.off <= self.cap, (name, self.off, self.cap)
        v = self.ap[:, off // 4:(off + nb) // 4]
        if dt == BF16:
            v = v.bitcast(BF16)
        v = v[:, 0:n]
        if len(shape) == 3:
            v = v.rearrange("p (a b) -> p a b", a=shape[1])
        elif len(shape) == 4:
            v = v.rearrange("p (a b c) -> p a b c", a=shape[1], b=shape[2])
        if shape[0] != 128:
            v = v[0:shape[0]]
        return T(v, name)


PCOL = {}
_o = 0
for _n, _w in (("gmix", 8), ("gffn", 8), ("gssd", 8), ("cws", 48), ("cbs", 12), ("dcol", 8),
               ("wq", 1), ("wk", 1), ("fb", 1), ("dtb", 16), ("alog", 16), ("cwf", 132), ("cbf", 44)):
    PCOL[_n] = (_o, _o + _w)
    _o += _w
NPAR = _o


def build_program():
    nc = bass.Bass("TRN2", target_bir_lowering=False)
    x_d = nc.dram_tensor("x", [S, D], F32, kind="ExternalInput").ap()
    win_d = nc.dram_tensor("win", [128, 8 * IN_COLS], F32, kind="ExternalInput").ap()
    wout_d = nc.dram_tensor("wout", [128, 16 * D], F32, kind="ExternalInput").ap()
    wup_d = nc.dram_tensor("wup", [128, 8 * 2 * DFF], F32, kind="ExternalInput").ap()
    wdn_d = nc.dram_tensor("wdn", [128, 22 * D], F32, kind="ExternalInput").ap()
    par_d = nc.dram_tensor("par", [128, NPAR], F32, kind="ExternalInput").ap()
    cst_d = nc.dram_tensor("cst", [128, 384], F32, kind="ExternalInput").ap()
    out_d = nc.dram_tensor("out", [S, D], F32, kind="ExternalOutput").ap()
    def skind(nm):
        return "ExternalOutput" if (DEBUG and nm in DEBUG) else "Internal"
    hT_d = nc.dram_tensor("hT_s", [128, 8 * S], BF16, kind=skind("hT_s")).ap()
    mixT_d = nc.dram_tensor("mixT_s", [128, 16 * S], BF16, kind=skind("mixT_s")).ap()
    x1_d = nc.dram_tensor("x1_s", [S, D], F32, kind=skind("x1_s")).ap()
    h2T_d = nc.dram_tensor("h2T_s", [128, 8 * S], BF16, kind=skind("h2T_s")).ap()

    win_v = win_d.rearrange("p (k c) -> p k c", k=8)
    wout_v = wout_d.rearrange("p (k c) -> p k c", k=16)
    wup_v = wup_d.rearrange("p (k c) -> p k c", k=8)
    wdn_v = wdn_d.rearrange("p (k c) -> p k c", k=22)
    hT_v = hT_d.rearrange("p (k t) -> p k t", k=8)
    mixT_v = mixT_d.rearrange("p (k t) -> p k t", k=16)
    h2T_v = h2T_d.rearrange("p (k t) -> p k t", k=8)
    x_v = x_d.rearrange("(g b p) d -> g p b d", b=4, p=128)
    x1_v = x1_d.rearrange("(g b p) d -> g p b d", b=4, p=128)
    out_v = out_d.rearrange("(g b p) d -> g p b d", b=4, p=128)

    B_hT_d = Buf("hT_d", multi=True)
    B_mixT_d = Buf("mixT_d", multi=True)
    B_x1_d = Buf("x1_d", multi=True)
    B_h2T_d = Buf("h2T_d", multi=True)
    B_out_d = Buf("out_d", multi=True)

    CAP = 207 * 1024
    arena_t = nc.alloc_sbuf_tensor("arena", [128, CAP // 4], F32)
    A = Arena(arena_t[:, :], CAP)
    banks = []
    for i in range(8):
        pt = nc.alloc_psum_tensor(f"psb{i}", [128, 512], F32)
        banks.append(T(pt[:, :], f"psb{i}"))
        banks[-1].b.psum = True

    def bf_view(bank_ap, a):
        return bank_ap.bitcast(BF16).rearrange("p (a b) -> p a b", a=a)

    P = Prog()

    par = A.alloc([128, NPAR], F32, "par")
    cst = A.alloc([128, 384], F32, "cst")
    identb = A.alloc([128, 128], BF16, "identb")
    triub = A.alloc([128, 128], BF16, "triub")
    bonesb = A.alloc([128, 128], BF16, "bonesb")
    onesb = A.alloc([128, 128], BF16, "onesb")
    ones16 = A.alloc([128, 512], F32, "ones16")
    identf = cst.ap[:, 0:128]
    triuf = cst.ap[:, 128:256]

    def pc(name, a=None, b=None):
        lo, hi = PCOL[name]
        if a is None:
            return par.ap[:, lo:hi]
        return par.ap[:, lo + a:lo + (b if b is not None else a + 1)]

    P.dma(I_dma(par.ap, par_d), writes=[par.b], sbuf=par.b)
    P.dma(I_dma(cst.ap, cst_d), writes=[cst.b], sbuf=cst.b)
    P.op(DVE, I_copy(identb.ap, cst.ap[:, 0:128]), reads=[cst.b], writes=[identb.b])
    P.op(DVE, I_copy(triub.ap, cst.ap[:, 128:256]), reads=[cst.b], writes=[triub.b])
    P.op(DVE, I_copy(bonesb.ap, cst.ap[:, 256:384]), reads=[cst.b], writes=[bonesb.b])
    P.op(POOL, I_memset(onesb.ap, 1.0), writes=[onesb.b])
    P.op(POOL, I_memset(ones16.ap, 1.0), writes=[ones16.b])
    gbase = A.off

    def rms_stage_a(xg, ss, lnv, rstd, junk):
        for b in range(4):
            P.op(ACT, I_act(junk.ap, xg.ap[:, b, :], AF.Square, accum=ss.ap[:, b:b + 1]),
                 reads=[xg.b], writes=[junk.b, ss.b])
        P.op(ACT, I_act(lnv.ap, ss.ap, AF.Ln, bias=EPS, scale=1.0 / D), reads=[ss.b], writes=[lnv.b])
        P.op(ACT, I_act(rstd.ap, lnv.ap, AF.Exp, scale=-0.5), reads=[lnv.b], writes=[rstd.b])

    def norm_transpose(xg, rstd, xn, tbanks, hst, evac_toggle):
        for b in range(4):
            P.op(DVE, I_ts(xn.ap[:, b, :], xg.ap[:, b, :], rstd.ap[:, b:b + 1], None, ALU.mult),
                 reads=[xg.b, rstd.b], writes=[xn.b])
        for b in range(4):
            bk = tbanks[b % len(tbanks)]
            pv = bf_view(bk.ap, 8)
            for kc in range(8):
                P.op(PE, I_tr(pv[:, kc, :], xn.ap[:, b, kc * 128:(kc + 1) * 128], identb.ap),
                     reads=[xn.b, identb.b], writes=[bk.b])
            eng = ACT if (b + evac_toggle) % 2 == 0 else DVE
            if eng == ACT:
                P.op(ACT, I_act(hst.ap[:, :, b * 128:(b + 1) * 128], pv, AF.Copy), reads=[bk.b], writes=[hst.b])
            else:
                P.op(DVE, I_copy(hst.ap[:, :, b * 128:(b + 1) * 128], pv), reads=[bk.b], writes=[hst.b])

    def load_weight_rows(dst, src_v, nk, c0, c1, gcol_fn, stages, cnt, scale2=1.0, chunk=1024):
        for k in range(nk):
            for cc in range(c0, c1, chunk):
                ce = min(cc + chunk, c1)
                st = stages[cnt[0] % len(stages)]
                eng = DVE if cnt[0] % 2 == 0 else ACT
                cnt[0] += 1
                P.dma(I_dma(st.ap[:, 0:ce - cc], src_v[:, k, cc:ce]), writes=[st.b], sbuf=st.b)
                g = gcol_fn(k)
                d_ap = dst.ap[:, k, cc - c0:ce - c0]
                s_ap = st.ap[:, 0:ce - cc]
                if eng == ACT:
                    if g is None:
                        P.op(ACT, I_act(d_ap, s_ap, AF.Copy), reads=[st.b], writes=[dst.b])
                    else:
                        assert scale2 == 1.0
                        P.op(ACT, I_act(d_ap, s_ap, AF.Identity, scale=g), reads=[st.b, par.b], writes=[dst.b])
                elif g is None:
                    P.op(DVE, I_copy(d_ap, s_ap), reads=[st.b], writes=[dst.b])
                else:
                    P.op(DVE, I_ts(d_ap, s_ap, g, scale2, ALU.mult, ALU.mult), reads=[st.b, par.b], writes=[dst.b])

    try:
        A.off = gbase
        xgs = [A.alloc([128, 4, D], F32, f"xg{i}") for i in range(2)]
        xns = [A.alloc([128, 4, D], BF16, f"xn{i}") for i in range(2)]
        hsts = [A.alloc([128, 8, 512], BF16, f"hst{i}") for i in range(2)]
        junk = A.alloc([128, D], BF16, "junk")
        sss = [A.alloc([128, 4], F32, f"ss{i}") for i in range(2)]
        lnvs = [A.alloc([128, 4], F32, f"lnv{i}") for i in range(2)]
        rstds = [A.alloc([128, 4], F32, f"rstd{i}") for i in range(2)]

        def p1_a(g):
            s = g % 2
            P.dma(I_dma(xgs[s].ap, x_v[g]), writes=[xgs[s].b], sbuf=xgs[s].b)
            rms_stage_a(xgs[s], sss[s], lnvs[s], rstds[s], junk)

        def p1_b(g):
            s = g % 2
            norm_transpose(xgs[s], rstds[s], xns[s], banks[0:4], hsts[s], g)
            P.dma(I_dma(hT_v[:, :, g * 512:(g + 1) * 512], hsts[s].ap), reads=[hsts[s].b], writes=[B_hT_d], sbuf=hsts[s].b, eng=POOL)

        p1_a(0)
        for g in range(NT):
            if g + 1 < NT:
                p1_a(g + 1)
            p1_b(g)

        if UPTO < 2:
            raise StopBuild()
        P.barrier()
        A.off = gbase
        wssd = A.alloc([128, 8, 2560], BF16, "wssd")
        wdt = A.alloc([128, 8, 16], BF16, "wdt")
        wdts = A.alloc([128, 8, 16], F32, "wdts")
        wst = [A.alloc([128, 1024], F32, f"wst{i}") for i in range(2)]
        cwh = A.alloc([128, 48], F32, "cwh")
        cbh = A.alloc([128, 12], F32, "cbh")
        a_b = A.alloc([128, 16], F32, "a_b")
        hTt = [A.alloc([128, 8, 512], BF16, f"hTt{i}") for i in range(2)]
        szT = A.alloc([128, 8, 512], BF16, "szT")
        xpads = [A.alloc([128, 515], F32, f"xpad{i}") for i in range(12)]
        xcT = [A.alloc([128, 512], BF16, f"xcT{i}") for i in range(12)]
        cacc = [A.alloc([128, 512], F32, f"cacc{i}") for i in range(3)]
        thb = [A.alloc([128, 512], F32, f"thb{i}") for i in range(2)]
        dtx = A.alloc([128, 64], F32, "dtx")
        dt_t = A.alloc([128, 4, 16], F32, "dt_t")
        dtA_t = A.alloc([128, 4, 16], F32, "dtA_t")
        nacs = A.alloc([128, 16], F32, "nacs")
        dtmp = A.alloc([128, 16], F32, "dtmp")
        cdl = A.alloc([128, 16], F32, "cdl")
        dte = A.alloc([128, 16], F32, "dte")
        cds = [A.alloc([128, 16], F32, f"cd{i}") for i in range(2)]
        dhi = A.alloc([128, 4, 16], BF16, "dhi")
        dlo = A.alloc([128, 4, 16], BF16, "dlo")
        dres = A.alloc([128, 4, 16], F32, "dres")
        Dhi = A.alloc([128, 8, 128], BF16, "Dhi")
        Dlo = A.alloc([128, 8, 128], BF16, "Dlo")
        w2 = A.alloc([128, 16], F32, "w2")
        ET = A.alloc([128, 16, 128], BF16, "ET")
        ETp = A.alloc([128, 16, 128], F32, "ETp")
        EA = A.alloc([128, 16, 128], BF16, "EA")
        CBTm = A.alloc([128, 2, 128], F32, "CBTm")
        scT = A.alloc([128, 16, 128], BF16, "scT")
        xdt = A.alloc([128, 16, 64], BF16, "xdt")
        xdte = A.alloc([128, 16, 64], BF16, "xdte")
        Btok = A.alloc([128, 2, 128], BF16, "Btok")
        Csc = A.alloc([128, 16, 128], BF16, "Csc")
        Sst = A.alloc([128, 16, 64], F32, "Sst")
        Sbf = A.alloc([128, 16, 64], BF16, "Sbf")
        yt = A.alloc([128, 8, 128], F32, "yt")
        sqb = A.alloc([128, 8, 128], BF16, "sqb")
        lnr = A.alloc([128, 2, 128], F32, "lnr")
        rsr = A.alloc([128, 2, 128], F32, "rsr")
        ystage = A.alloc([128, 8, 512], BF16, "ystage")

        cnt = [0]
        load_weight_rows(wssd, win_v, 8, 0, 1024, lambda k: pc("gmix", k), wst, cnt, scale2=1.0)
        wx = T(wssd.ap[:, :, 1024:2560], "wssd_x")
        wx.b = wssd.b
        load_weight_rows(wx, win_v, 8, 1024, 2560, lambda k: pc("gmix", k), wst, cnt, scale2=1.0, chunk=768)
        P.dma(I_dma(wdts.ap, win_v[:, :, 2560:2576]), writes=[wdts.b], sbuf=wdts.b)
        P.op(DVE, I_tt(wdt.ap, wdts.ap, pc("gmix").unsqueeze(2).to_broadcast([128, 8, 16]), ALU.mult),
             reads=[wdts.b, par.b], writes=[wdt.b])
        P.op(DVE, I_ts(cwh.ap, pc("cws"), 0.5, None, ALU.mult), reads=[par.b], writes=[cwh.b])
        P.op(DVE, I_ts(cbh.ap, pc("cbs"), 0.5, None, ALU.mult), reads=[par.b], writes=[cbh.b])
        P.op(ACT, I_act(a_b.ap, pc("alog"), AF.Exp), reads=[par.b], writes=[a_b.b])
        P.op(DVE, I_ts(a_b.ap, a_b.ap, -1.0, None, ALU.mult), reads=[a_b.b], writes=[a_b.b])
        P.op(POOL, I_memset(Sst.ap, 0.0), writes=[Sst.b])
        P.op(POOL, I_memset(Sbf.ap, 0.0), writes=[Sbf.b])
        P.op(DVE, I_tt(yt.ap, identf.unsqueeze(1).to_broadcast([128, 8, 128]), pc("dcol").unsqueeze(2).to_broadcast([128, 8, 128]), ALU.mult),
             reads=[cst.b, par.b], writes=[yt.b])
        P.op(DVE, I_copy(Dhi.ap, yt.ap), reads=[yt.b], writes=[Dhi.b])
        P.op(DVE, I_tt(yt.ap, yt.ap, Dhi.ap, ALU.subtract), reads=[yt.b, Dhi.b], writes=[yt.b])
        P.op(DVE, I_copy(Dlo.ap, yt.ap), reads=[yt.b], writes=[Dlo.b])
        for i in range(12):
            P.op(POOL, I_memset(xpads[i].ap[:, 0:3], 0.0), writes=[xpads[i].b])

        pb_misc = banks[0]
        psD = T(pb_misc.ap[:, 0:64], "psD")
        psA = T(pb_misc.ap[:, 64:80], "psA")
        psC = T(pb_misc.ap[:, 128:384], "psC")
        psBt = T(pb_misc.ap[:, 384:512], "psBt")
        psSS = T(banks[3].ap[:, 256:512], "psSS")
        psR = [banks[1], banks[2]]
        psTb = T(banks[3].ap[:, 0:256], "psTx")
        psT = banks[7]
        psY = [banks[4], banks[5]]
        psSt = [banks[6], banks[3]]
        psSS.b = banks[3].b
        psD.b = psA.b = psC.b = psBt.b = pb_misc.b
        projbanks = [banks[1], banks[2], banks[4], banks[5], banks[6], banks[7]]
        pcount = [0]

        def ssd_tile_load(t):
            s = t % 2
            P.dma(I_dma(hTt[s].ap, hT_v[:, :, t * 512:(t + 1) * 512]), reads=[B_hT_d], writes=[hTt[s].b], sbuf=hTt[s].b)

        ssd_tile_load(0)
        for t in range(NT):
            hs = hTt[t % 2]
            if t + 1 < NT:
                ssd_tile_load(t + 1)
            for c in range(4):
                for kc in range(8):
                    P.op(PE, I_mm(psD.ap[:, c * 16:(c + 1) * 16], hs.ap[:, kc, c * 128:(c + 1) * 128], wdt.ap[:, kc, :],
                                  start=(kc == 0), stop=(kc == 7)), reads=[hs.b, wdt.b], writes=[psD.b])
            P.op(DVE, I_tt(dtx.ap.rearrange("p (c h) -> p c h", c=4), psD.ap.rearrange("p (c h) -> p c h", c=4),
                           pc("dtb").unsqueeze(1).to_broadcast([128, 4, 16]), ALU.add),
                 reads=[psD.b, par.b], writes=[dtx.b])
            P.op(ACT, I_act(dtx.ap, dtx.ap, AF.Exp), reads=[dtx.b], writes=[dtx.b])
            P.op(ACT, I_act(dt_t.ap.rearrange("p c h -> p (c h)"), dtx.ap, AF.Ln, bias=1.0, scale=1.0), reads=[dtx.b], writes=[dt_t.b])
            P.op(DVE, I_tt(dtA_t.ap, dt_t.ap, a_b.ap.unsqueeze(1).to_broadcast([128, 4, 16]), ALU.mult),
                 reads=[dt_t.b, a_b.b], writes=[dtA_t.b])
            P.op(DVE, I_copy(dhi.ap, dtA_t.ap), reads=[dtA_t.b], writes=[dhi.b])
            P.op(DVE, I_tt(dres.ap, dtA_t.ap, dhi.ap, ALU.subtract), reads=[dtA_t.b, dhi.b], writes=[dres.b])
            P.op(DVE, I_copy(dlo.ap, dres.ap), reads=[dres.b], writes=[dlo.b])
            for zb in range(8):
                bk = projbanks[pcount[0] % len(projbanks)]
                pcount[0] += 1
                for kc in range(8):
                    P.op(PE, I_mm(bk.ap, wssd.ap[:, kc, zb * 128:(zb + 1) * 128], hs.ap[:, kc, :], start=(kc == 0), stop=(kc == 7)),
                         reads=[wssd.b, hs.b], writes=[bk.b])
                P.op(ACT, I_act(szT.ap[:, zb, :], bk.ap, AF.Silu), reads=[bk.b], writes=[szT.b])
            def xbc_A(xb):
                bk = projbanks[pcount[0] % len(projbanks)]
                pcount[0] += 1
                for kc in range(8):
                    P.op(PE, I_mm(bk.ap, wssd.ap[:, kc, 1024 + xb * 128:1024 + (xb + 1) * 128], hs.ap[:, kc, :],
                                  start=(kc == 0), stop=(kc == 7)), reads=[wssd.b, hs.b], writes=[bk.b])
                xp = xpads[xb]
                ac = cacc[xb % 3]
                P.op(ACT, I_act(ac.ap, bk.ap, AF.Identity, bias=pc("cbs", xb), scale=pc("cws", xb * 4 + 3)),
                     reads=[bk.b, par.b], writes=[ac.b])
                P.op(ACT, I_act(xp.ap[:, 3:515], bk.ap, AF.Copy), reads=[bk.b], writes=[xp.b])

            def xbc_B(xb):
                xp = xpads[xb]
                ac = cacc[xb % 3]
                for k in range(3):
                    P.op(DVE, I_stt(ac.ap, xp.ap[:, k:k + 512], pc("cws", xb * 4 + k), ac.ap, ALU.mult, ALU.add),
                         reads=[xp.b, par.b, ac.b], writes=[ac.b])
                P.op(POOL, I_copy(xp.ap[:, 0:3], xp.ap[:, 512:515]), reads=[xp.b], writes=[xp.b])

            def xbc_C(xb):
                ac = cacc[xb % 3]
                P.op(ACT, I_act(xcT[xb].ap, ac.ap, AF.Silu), reads=[ac.b], writes=[xcT[xb].b])

            for step in range(12 + 2):
                if step < 12:
                    xbc_A(step)
                if 0 <= step - 1 < 12:
                    xbc_B(step - 1)
                if 0 <= step - 2 < 12:
                    xbc_C(step - 2)
            def ch_prep(c):
                P.op(PE, I_mm(psA.ap, triuf, dtA_t.ap[:, c, :]), reads=[cst.b, dtA_t.b], writes=[psA.b])
                P.op(DVE, I_ts(nacs.ap, psA.ap, -1.0, None, ALU.mult), reads=[psA.b], writes=[nacs.b])
                for rb in range(4):
                    bk = psR[rb % 2]
                    for hh in range(4):
                        h = rb * 4 + hh
                        o_ap = bk.ap[:, hh * 128:(hh + 1) * 128]
                        P.op(PE, I_mm(o_ap, dhi.ap[:, c, h:h + 1].to_broadcast([128, 128]), triub.ap, start=True, stop=False),
                             reads=[dhi.b, triub.b], writes=[bk.b])
                        P.op(PE, I_mm(o_ap, dlo.ap[:, c, h:h + 1].to_broadcast([128, 128]), triub.ap, start=False, stop=True),
                             reads=[dlo.b, triub.b], writes=[bk.b])
                    P.op(DVE, I_tt(ETp.ap[:, rb * 4:rb * 4 + 4, :], bk.ap.rearrange("p (a b) -> p a b", a=4),
                                   nacs.ap[:, rb * 4:rb * 4 + 4].unsqueeze(2).to_broadcast([128, 4, 128]), ALU.add),
                         reads=[bk.b, nacs.b], writes=[ETp.b])
                    P.op(ACT, I_act(EA.ap[:, rb * 4:rb * 4 + 4, :].rearrange("p a b -> p (a b)"), bk.ap, AF.Exp),
                         reads=[bk.b], writes=[EA.b])
                    last = bk.ap.rearrange("p (a b) -> p a b", a=4)[:, :, 127:128].rearrange("p a b -> p (a b)")
                    P.op(DVE, I_tt(dtmp.ap[:, rb * 4:rb * 4 + 4], last, nacs.ap[:, rb * 4:rb * 4 + 4], ALU.add),
                         reads=[bk.b, nacs.b], writes=[dtmp.b])
                    P.op(DVE, I_copy(cdl.ap[:, rb * 4:rb * 4 + 4], last), reads=[bk.b], writes=[cdl.b])
                P.op(ACT, I_act(ETp.ap.rearrange("p a b -> p (a b)"), ETp.ap.rearrange("p a b -> p (a b)"), AF.Abs),
                     reads=[ETp.b], writes=[ETp.b])
                P.op(ACT, I_act(ET.ap.rearrange("p a b -> p (a b)"), ETp.ap.rearrange("p a b -> p (a b)"), AF.Exp, scale=-1.0),
                     reads=[ETp.b], writes=[ET.b])
                P.op(ACT, I_act(dte.ap, dtmp.ap, AF.Exp), reads=[dtmp.b], writes=[dte.b])
                P.op(ACT, I_act(cds[c % 2].ap, cdl.ap, AF.Exp), reads=[cdl.b], writes=[cds[c % 2].b])

            def ch_mid(c):
                cs = slice(c * 128, (c + 1) * 128)
                for g in range(2):
                    P.op(PE, I_mm(psC.ap[:, g * 128:(g + 1) * 128], xcT[8 + g].ap[:, cs], xcT[10 + g].ap[:, cs]),
                         reads=[xcT[8 + g].b, xcT[10 + g].b], writes=[psC.b])
                P.op(DVE, I_tt(CBTm.ap, psC.ap.rearrange("p (g l) -> p g l", g=2), triuf.unsqueeze(1).to_broadcast([128, 2, 128]), ALU.mult),
                     reads=[psC.b, cst.b], writes=[CBTm.b])
                for g in range(2):
                    P.op(DVE, I_tt(scT.ap[:, 8 * g:8 * g + 8, :], ET.ap[:, 8 * g:8 * g + 8, :],
                                   CBTm.ap[:, g:g + 1, :].to_broadcast([128, 8, 128]), ALU.mult),
                         reads=[ET.b, CBTm.b], writes=[scT.b])
                for g in range(2):
                    P.op(POOL, I_tt(Csc.ap[:, 8 * g:8 * g + 8, :], EA.ap[:, 8 * g:8 * g + 8, :],
                                    xcT[10 + g].ap[:, cs].unsqueeze(1).to_broadcast([128, 8, 128]), ALU.mult),
                         reads=[EA.b, xcT[10 + g].b], writes=[Csc.b])
                pvT = bf_view(psT.ap, 8)
                for blk in range(8):
                    P.op(PE, I_tr(pvT[:, blk, :], xcT[blk].ap[:, cs], identb.ap), reads=[xcT[blk].b, identb.b], writes=[psT.b])
                P.op(DVE, I_tt(w2.ap, dt_t.ap[:, c, :], dte.ap, ALU.mult), reads=[dt_t.b, dte.b], writes=[w2.b])
                pvT16 = psT.ap.bitcast(BF16).rearrange("p (h d) -> p h d", h=16)
                P.op(DVE, I_tt(xdt.ap, pvT16, dt_t.ap[:, c, :].unsqueeze(2).to_broadcast([128, 16, 64]), ALU.mult),
                     reads=[psT.b, dt_t.b], writes=[xdt.b])
                P.op(DVE, I_tt(xdte.ap, pvT16, w2.ap.unsqueeze(2).to_broadcast([128, 16, 64]), ALU.mult),
                     reads=[psT.b, w2.b], writes=[xdte.b])
                pvB = psBt.ap.bitcast(BF16).rearrange("p (g n) -> p g n", g=2)
                for g in range(2):
                    P.op(PE, I_tr(pvB[:, g, :], xcT[8 + g].ap[:, cs], identb.ap), reads=[xcT[8 + g].b, identb.b], writes=[psBt.b])
                P.op(ACT, I_act(Btok.ap, pvB, AF.Copy), reads=[psBt.b], writes=[Btok.b])

            def ch_fin(c):
                cs = slice(c * 128, (c + 1) * 128)
                for hp in range(8):
                    bk = psY[hp // 4]
                    col = (hp % 4) * 128
                    full = bk.ap[:, col:col + 128]
                    P.op(PE, I_mm(full, Dhi.ap[:, hp, :], xcT[hp].ap[:, cs], start=True, stop=False),
                         reads=[Dhi.b, xcT[hp].b], writes=[bk.b])
                    P.op(PE, I_mm(full, Dlo.ap[:, hp, :], xcT[hp].ap[:, cs], start=False, stop=False),
                         reads=[Dlo.b, xcT[hp].b], writes=[bk.b])
                    for half in range(2):
                        h = 2 * hp + half
                        o_ap = bk.ap[64 * half:64 * half + 64, col:col + 128]
                        P.op(PE, I_mm(o_ap, xdt.ap[:, h, :], scT.ap[:, h, :], start=False, stop=False),
                             reads=[xdt.b, scT.b], writes=[bk.b])
                        P.op(PE, I_mm(o_ap, Sbf.ap[:, h, :], Csc.ap[:, h, :], start=False, stop=(half == 1)),
                             reads=[Sbf.b, Csc.b], writes=[bk.b])
                for k in range(2):
                    P.op(DVE, I_tt(yt.ap[:, 4 * k:4 * k + 4, :], psY[k].ap.rearrange("p (a b) -> p a b", a=4),
                                   szT.ap[:, 4 * k:4 * k + 4, cs], ALU.mult),
                         reads=[psY[k].b, szT.b], writes=[yt.b])
                P.op(ACT, I_act(sqb.ap, yt.ap, AF.Square), reads=[yt.b], writes=[sqb.b])
                for g in range(2):
                    for j in range(4):
                        P.op(PE, I_mm(psSS.ap[:, g * 128:(g + 1) * 128], onesb.ap, sqb.ap[:, 4 * g + j, :], start=(j == 0), stop=(j == 3)),
                             reads=[onesb.b, sqb.b], writes=[psSS.b])
                P.op(ACT, I_act(lnr.ap.rearrange("p g l -> p (g l)"), psSS.ap, AF.Ln, bias=EPS, scale=1.0 / 512),
                     reads=[psSS.b], writes=[lnr.b])
                P.op(ACT, I_act(rsr.ap, lnr.ap, AF.Exp, scale=-0.5), reads=[lnr.b], writes=[rsr.b])
                for g in range(2):
                    P.op(POOL, I_tt(ystage.ap[:, 4 * g:4 * g + 4, cs], yt.ap[:, 4 * g:4 * g + 4, :],
                                    rsr.ap[:, g:g + 1, :].to_broadcast([128, 4, 128]), ALU.mult),
                         reads=[yt.b, rsr.b], writes=[ystage.b])
                for g in range(2):
                    bk = psSt[g]
                    P.op(PE, I_mm(bk.ap, Btok.ap[:, g, :], xdte.ap[:, 8 * g:8 * g + 8, :].rearrange("p h d -> p (h d)")),
                         reads=[Btok.b, xdte.b], writes=[bk.b])
                cdc = cds[c % 2]
                for g in range(2):
                    bk = psSt[g]
                    sv = Sst.ap[:, 8 * g:8 * g + 8, :]
                    P.op(DVE, I_tt(sv, sv, cdc.ap[:, 8 * g:8 * g + 8].unsqueeze(2).to_broadcast([128, 8, 64]), ALU.mult),
                         reads=[Sst.b, cdc.b], writes=[Sst.b])
                    P.op(DVE, I_tt(sv, sv, bk.ap.rearrange("p (h d) -> p h d", h=8), ALU.add),
                         reads=[Sst.b, bk.b], writes=[Sst.b])
                P.op(ACT, I_act(Sbf.ap, Sst.ap, AF.Copy), reads=[Sst.b], writes=[Sbf.b])

            ch_prep(0)
            for c in range(4):
                ch_mid(c)
                if c + 1 < 4:
                    ch_prep(c + 1)
                ch_fin(c)
            P.dma(I_dma(mixT_v[:, 0:8, t * 512:(t + 1) * 512], ystage.ap), reads=[ystage.b], writes=[B_mixT_d], sbuf=ystage.b, eng=POOL)

        if UPTO < 3:
            raise StopBuild()
        P.barrier()
        A.off = gbase
        hT = A.alloc([128, 8, S], BF16, "hT")
        hT_k = []
        for kc in range(8):
            t_ = T(hT.ap[:, kc, :], f"hT{kc}")
            hT_k.append(t_)
        cumT = A.alloc([16, S], F32, "cumT")
        ftmp = A.alloc([16, S], F32, "ftmp")
        cumbf = A.alloc([16, S], BF16, "cumbf")
        ncum = A.alloc([128, 32, 16], F32, "ncum")
        nfb = A.alloc([128, 1], F32, "nfb")
        wq8 = A.alloc([128, 1], F32, "wq8")
        wfs = A.alloc([128, 8, 16], F32, "wfs")
        wfb = A.alloc([128, 8, 16], BF16, "wfb")
        wqs = [A.alloc([128, 8, 128], F32, f"wqs{i}") for i in range(3)]
        wqb = [A.alloc([128, 8, 128], BF16, f"wqb{i}") for i in range(3)]
        qA = A.alloc([128, S], BF16, "qA")
        qB = A.alloc([128, S], BF16, "qB")
        kA = A.alloc([128, S], BF16, "kA")
        kB = A.alloc([128, S], BF16, "kB")
        vA = A.alloc([128, 32, 128], BF16, "vA")
        vB = A.alloc([128, 32, 128], BF16, "vB")
        sq2 = [A.alloc([128, 512], BF16, f"sq2{i}") for i in range(3)]
        lnq = [A.alloc([128, 512], F32, f"lnq{i}") for i in range(3)]
        rsq = [A.alloc([128, 512], F32, f"rsq{i}") for i in range(3)]
        NPT = 4
        PT = [A.alloc([128, 512], BF16, f"PT{i}") for i in range(NPT)]
        rr = [A.alloc([128, 512], F32, f"rr{i}") for i in range(2)]
        ost = [A.alloc([128, 512], BF16, f"ost{i}") for i in range(2)]

        for kc in range(8):
            P.dma(I_dma(hT_k[kc].ap, hT_v[:, kc, :]), reads=[B_hT_d], writes=[hT_k[kc].b], sbuf=hT_k[kc].b)
        hT_bufs = [t_.b for t_ in hT_k]

        P.op(DVE, I_ts(nfb.ap, pc("fb"), -1.0, None, ALU.mult), reads=[par.b], writes=[nfb.b])
        P.op(DVE, I_ts(wq8.ap, pc("wq"), 0.125, None, ALU.mult), reads=[par.b], writes=[wq8.b])
        for tl in (qA, qB, kA, kB):
            P.op(POOL, I_memset(tl.ap, 0.0), writes=[tl.b])
        P.op(POOL, I_memset(kA.ap[64:65, :], 1.0), writes=[kA.b])
        P.op(POOL, I_memset(kB.ap[0:1, :], 1.0), writes=[kB.b])
        P.op(POOL, I_memset(vA.ap, 1.0), writes=[vA.b])
        P.op(POOL, I_memset(vB.ap, 1.0), writes=[vB.b])

        P.dma(I_dma(wfs.ap, win_v[:, :, 5648:5664]), writes=[wfs.b], sbuf=wfs.b)
        P.op(DVE, I_tt(wfb.ap, wfs.ap, pc("gmix").unsqueeze(2).to_broadcast([128, 8, 16]), ALU.mult),
             reads=[wfs.b, par.b], writes=[wfb.b])
        for t in range(NT):
            bk = banks[t % 2]
            tsl = slice(t * 512, (t + 1) * 512)
            for kc in range(8):
                P.op(PE, I_mm(bk.ap[0:16, :], wfb.ap[:, kc, :], hT_k[kc].ap[:, tsl], start=(kc == 0), stop=(kc == 7)),
                     reads=[wfb.b, hT_k[kc].b], writes=[bk.b])
            P.op(ACT, I_act(ftmp.ap[:, tsl], bk.ap[0:16, :], AF.Exp, bias=nfb.ap[0:16, :], scale=-1.0),
                 reads=[bk.b, nfb.b], writes=[ftmp.b])
            P.op(ACT, I_act(ftmp.ap[:, tsl], ftmp.ap[:, tsl], AF.Ln, bias=1.0, scale=1.0), reads=[ftmp.b], writes=[ftmp.b])
            init = 0.0 if t == 0 else cumT.ap[:, t * 512 - 1:t * 512]
            P.op(DVE, I_scan(cumT.ap[:, tsl], ones16.ap[0:16, :], ftmp.ap[:, tsl], init, ALU.mult, ALU.subtract),
                 reads=[ones16.b, ftmp.b, cumT.b], writes=[cumT.b])
        P.op(POOL, I_copy(cumbf.ap, cumT.ap), reads=[cumT.b], writes=[cumbf.b])
        pN = banks[2]
        for b in range(32):
            P.op(PE, I_tr(pN.ap[:, b * 16:(b + 1) * 16], cumT.ap[:, b * 128:(b + 1) * 128], identf[0:16, 0:16]),
                 reads=[cumT.b, cst.b], writes=[pN.b])
        P.op(DVE, I_ts(ncum.ap.rearrange("p b h -> p (b h)"), pN.ap, -1.0, None, ALU.mult), reads=[pN.b], writes=[ncum.b])

        psQ2 = [(banks[0], banks[1]), (banks[2], banks[3]), (banks[6], banks[7])]
        psVs = [banks[4], banks[5]]
        psS = [banks[2], banks[3], banks[4]]
        psO = [banks[5], banks[6], banks[7]]
        ocnt = [0]
        qcnt = [0]

        for hp in range(8):
            def w_cols(p_):
                return [2576 + p_ * 128, 3600 + p_ * 128, 4624 + p_ * 128]
            if hp == 0:
                for i in range(3):
                    P.dma(I_dma(wqs[i].ap, win_v[:, :, w_cols(0)[i]:w_cols(0)[i] + 128]), writes=[wqs[i].b], sbuf=wqs[i].b)
            for i in range(3):
                P.op(POOL if i == 1 else DVE, I_tt(wqb[i].ap, wqs[i].ap, pc("gmix").unsqueeze(2).to_broadcast([128, 8, 128]), ALU.mult),
                     reads=[wqs[i].b, par.b], writes=[wqb[i].b])
            if hp + 1 < 8:
                for i in range(3):
                    c_ = w_cols(hp + 1)[i]
                    P.dma(I_dma(wqs[i].ap, win_v[:, :, c_:c_ + 128]), writes=[wqs[i].b], sbuf=wqs[i].b)
            P.dma(I_dma(qA.ap[64:65, :], cumbf.ap[2 * hp:2 * hp + 1, :]), reads=[cumbf.b], writes=[qA.b], sbuf=qA.b)
            P.dma(I_dma(qB.ap[0:1, :], cumbf.ap[2 * hp + 1:2 * hp + 2, :]), reads=[cumbf.b], writes=[qB.b], sbuf=qB.b)
            qk_items = [(which, t) for which in range(2) for t in range(NT)]

            def qk_A(idx):
                which, t = qk_items[idx]
                bk, _ = psQ2[idx % 3]
                tsl = slice(t * 512, (t + 1) * 512)
                for kc in range(8):
                    P.op(PE, I_mm(bk.ap, wqb[which].ap[:, kc, :], hT_k[kc].ap[:, tsl], start=(kc == 0), stop=(kc == 7)),
                         reads=[wqb[which].b, hT_k[kc].b], writes=[bk.b])

            def qk_B(idx):
                which, t = qk_items[idx]
                XA, XB = (qA, qB) if which == 0 else (kA, kB)
                wcol = wq8.ap if which == 0 else pc("wk")
                wcol_b = wq8.b if which == 0 else par.b
                bk, bk2 = psQ2[idx % 3]
                tsl = slice(t * 512, (t + 1) * 512)
                sq = sq2[idx % 3]
                ln_ = lnq[idx % 3]
                rs_ = rsq[idx % 3]
                P.op(ACT, I_act(sq.ap, bk.ap, AF.Square), reads=[bk.b], writes=[sq.b])
                P.op(PE, I_mm(bk2.ap, bonesb.ap, sq.ap), reads=[bonesb.b, sq.b], writes=[bk2.b])
                P.op(ACT, I_act(ln_.ap, bk2.ap, AF.Ln, bias=EPS, scale=1.0 / 64), reads=[bk2.b], writes=[ln_.b])
                P.op(ACT, I_act(rs_.ap, ln_.ap, AF.Exp, scale=-0.5), reads=[ln_.b], writes=[rs_.b])
                P.op(DVE, I_stt(XA.ap[0:64, tsl], bk.ap[0:64, :], wcol[0:64, :], rs_.ap[0:64, :], ALU.mult, ALU.mult),
                     reads=[bk.b, wcol_b, rs_.b], writes=[XA.b])
                P.op(DVE, I_stt(XB.ap[64:128, tsl], bk.ap[64:128, :], wcol[64:128, :], rs_.ap[64:128, :], ALU.mult, ALU.mult),
                     reads=[bk.b, wcol_b, rs_.b], writes=[XB.b])

            qk_A(0)
            for idx in range(len(qk_items)):
                if idx + 1 < len(qk_items):
                    qk_A(idx + 1)
                qk_B(idx)
            for bq in range(8):
                psV = psVs[bq % 2]
                pvv = psV.ap.rearrange("p (j c) -> p j c", j=4)
                for j in range(4):
                    b = 4 * bq + j
                    for kc in range(8):
                        P.op(PE, I_mm(psV.ap[:, j * 128:(j + 1) * 128], hT_k[kc].ap[:, b * 128:(b + 1) * 128], wqb[2].ap[:, kc, :],
                                      start=(kc == 0), stop=(kc == 7)), reads=[hT_k[kc].b, wqb[2].b], writes=[psV.b])
                P.op(ACT, I_act(vA.ap[:, 4 * bq:4 * bq + 4, 0:64], pvv[:, :, 0:64], AF.Copy), reads=[psV.b], writes=[vA.b])
                P.op(DVE, I_copy(vB.ap[:, 4 * bq:4 * bq + 4, 64:128], pvv[:, :, 64:128]), reads=[psV.b], writes=[vB.b])
            seq = []
            for i in range(NT):
                for hd in range(2):
                    nj = 4 * i + 4
                    for j in range(nj):
                        seq.append((i, hd, j, nj))

            def emit_S(n):
                i, hd, j, nj = seq[n]
                r = j - 4 * i
                c0 = max(0, r) * 128
                N = 512 - c0
                Kf = kA if hd == 0 else kB
                Qf = qA if hd == 0 else qB
                bk = psS[n % 3]
                P.op(PE, I_mm(bk.ap[:, 0:N], Kf.ap[:, j * 128:(j + 1) * 128], Qf.ap[:, i * 512 + c0:(i + 1) * 512]),
                     reads=[Kf.b, Qf.b], writes=[bk.b])

            def emit_PV(n):
                i, hd, j, nj = seq[n]
                r = j - 4 * i
                c0 = max(0, r) * 128
                N = 512 - c0
                h = 2 * hp + hd
                bk = psS[n % 3]
                pt = PT[n % NPT]
                if j == 0:
                    ocnt[0] += 1
                ob = psO[ocnt[0] % 3]
                P.op(ACT, I_act(pt.ap[:, 0:N], bk.ap[:, 0:N], AF.Exp, bias=ncum.ap[:, j, h:h + 1], scale=1.0),
                     reads=[bk.b, ncum.b], writes=[pt.b])
                if r >= 0:
                    P.op(POOL, I_tt(pt.ap[:, 0:128], pt.ap[:, 0:128], triub.ap, ALU.mult), reads=[pt.b, triub.b], writes=[pt.b])
                V = vA if hd == 0 else vB
                P.op(PE, I_mm(ob.ap[:, c0:512], V.ap[:, j, :], pt.ap[:, 0:N], start=(j == 0), stop=(j == nj - 1)),
                     reads=[V.b, pt.b], writes=[ob.b])
                if j == nj - 1:
                    os_ = ost[i % 2]
                    rt = rr[hd]
                    if hd == 0:
                        P.op(DVE, I_recip(rt.ap[64:128, :], ob.ap[64:128, :]), reads=[ob.b], writes=[rt.b])
                        P.op(DVE, I_tt(os_.ap[0:64, :], ob.ap[0:64, :], rt.ap[64:128, :], ALU.mult), reads=[ob.b, rt.b], writes=[os_.b])
                    else:
                        P.op(DVE, I_recip(rt.ap[0:64, :], ob.ap[0:64, :]), reads=[ob.b], writes=[rt.b])
                        P.op(DVE, I_tt(os_.ap[64:128, :], ob.ap[64:128, :], rt.ap[0:64, :], ALU.mult), reads=[ob.b, rt.b], writes=[os_.b])
                        P.dma(I_dma(mixT_v[:, 8 + hp, i * 512:(i + 1) * 512], os_.ap), reads=[os_.b], writes=[B_mixT_d], sbuf=os_.b)

            LOOK = 3
            for n in range(min(LOOK, len(seq))):
                emit_S(n)
            for n in range(len(seq)):
                emit_PV(n)
                if n + LOOK < len(seq):
                    emit_S(n + LOOK)

        if UPTO < 4:
            raise StopBuild()
        P.barrier()
        A.off = gbase
        wout = A.alloc([128, 16, D], BF16, "wout")
        wst3 = [A.alloc([128, 1024], F32, f"wst3{i}") for i in range(2)]
        mixt = [A.alloc([128, 16, 512], BF16, f"mixt{i}") for i in range(2)]
        xg3 = [A.alloc([128, 4, D], F32, f"xg3{i}") for i in range(2)]
        xn3 = [A.alloc([128, 4, D], BF16, f"xn3{i}") for i in range(2)]
        hst3 = [A.alloc([128, 8, 512], BF16, f"hst3{i}") for i in range(2)]
        junk3 = A.alloc([128, D], BF16, "junk3")
        ss3 = [A.alloc([128, 4], F32, f"ss3{i}") for i in range(2)]
        lnv3 = [A.alloc([128, 4], F32, f"lnv3{i}") for i in range(2)]
        rstd3 = [A.alloc([128, 4], F32, f"rstd3{i}") for i in range(2)]
        cnt3 = [0]
        load_weight_rows(wout, wout_v, 16, 0, D, lambda k: (pc("gssd", k) if k < 8 else None), wst3, cnt3)

        def p3_load(t):
            s = t % 2
            P.dma(I_dma(mixt[s].ap, mixT_v[:, :, t * 512:(t + 1) * 512]), reads=[B_mixT_d], writes=[mixt[s].b], sbuf=mixt[s].b)
            P.dma(I_dma(xg3[s].ap, x_v[t]), writes=[xg3[s].b], sbuf=xg3[s].b)

        def p3_a(t):
            s = t % 2
            n = 0
            for b in range(4):
                for half in range(2):
                    bk = banks[4 + (n % 4)]
                    n += 1
                    for fc in range(16):
                        P.op(PE, I_mm(bk.ap, mixt[s].ap[:, fc, b * 128:(b + 1) * 128], wout.ap[:, fc, half * 512:(half + 1) * 512],
                                      start=(fc == 0), stop=(fc == 15)), reads=[mixt[s].b, wout.b], writes=[bk.b])
                    xs_ = xg3[s].ap[:, b, half * 512:(half + 1) * 512]
                    P.op(DVE, I_tt(xs_, xs_, bk.ap, ALU.add), reads=[xg3[s].b, bk.b], writes=[xg3[s].b])
            P.dma(I_dma(x1_v[t], xg3[s].ap), reads=[xg3[s].b], writes=[B_x1_d], sbuf=xg3[s].b, eng=POOL)
            rms_stage_a(xg3[s], ss3[s], lnv3[s], rstd3[s], junk3)

        def p3_b(t):
            s = t % 2
            norm_transpose(xg3[s], rstd3[s], xn3[s], banks[0:4], hst3[s], t)
            P.dma(I_dma(h2T_v[:, :, t * 512:(t + 1) * 512], hst3[s].ap), reads=[hst3[s].b], writes=[B_h2T_d], sbuf=hst3[s].b, eng=POOL)

        p3_load(0)
        p3_load(1)
        p3_a(0)
        for t in range(NT):
            if t + 1 < NT:
                p3_a(t + 1)
            p3_b(t)
            if t + 2 < NT:
                p3_load(t + 2)

        if UPTO < 5:
            raise StopBuild()
        P.barrier()
        A.off = gbase
        wup = A.alloc([128, 8, 2 * DFF], BF16, "wup")
        wdn = A.alloc([128, 22, D], BF16, "wdn")
        wst4 = [A.alloc([128, 512], F32, f"wst4{i}") for i in range(2)]
        cwf = A.alloc([128, 132], F32, "cwf")
        cbf = A.alloc([128, 44], F32, "cbf")
        halo = A.alloc([128, 44, 2], F32, "halo")
        h2t = [A.alloc([128, 8, 512], BF16, "h2t0")]
        x1t = A.alloc([128, 2, D], F32, "x1t")
        gpad = [A.alloc([128, 514], F32, f"gpad{i}") for i in range(2)]
        vpad = [A.alloc([128, 514], F32, f"vpad{i}") for i in range(2)]
        gpadh = [Buf(f"gpadh{i}") for i in range(2)]
        vpadh = [Buf(f"vpadh{i}") for i in range(2)]
        gacc = [A.alloc([128, 512], F32, f"gacc{i}") for i in range(3)]
        vacc = [A.alloc([128, 512], F32, f"vacc{i}") for i in range(3)]
        th4 = [A.alloc([128, 512], F32, f"th4{i}") for i in range(2)]
        gT = A.alloc([128, 22, 512], BF16, "gT")
        cnt4 = [0]
        wup_piece = {}
        for ci in (0, 5, 6, 1, 7, 2, 8, 3, 9, 4, 10):
            for k in range(8):
                cc = ci * 512
                st = wst4[cnt4[0] % len(wst4)]
                eng = DVE if cnt4[0] % 2 == 0 else ACT
                cnt4[0] += 1
                pb = Buf(f"wup_{k}_{ci}")
                wup_piece[(k, ci)] = pb
                P.dma(I_dma(st.ap[:, 0:512], wup_v[:, k, cc:cc + 512]), writes=[st.b], sbuf=st.b)
                if eng == ACT:
                    P.op(ACT, I_act(wup.ap[:, k, cc:cc + 512], st.ap[:, 0:512], AF.Identity, scale=pc("gffn", k)),
                         reads=[st.b, par.b], writes=[pb])
                else:
                    P.op(DVE, I_ts(wup.ap[:, k, cc:cc + 512], st.ap[:, 0:512], pc("gffn", k), 1.0, ALU.mult, ALU.mult),
                         reads=[st.b, par.b], writes=[pb])
        load_weight_rows(wdn, wdn_v, 22, 0, D, lambda k: None, wst4, cnt4, chunk=512)
        P.op(DVE, I_ts(cwf.ap[:, 0:66], pc("cwf", 0, 66), 0.5, None, ALU.mult), reads=[par.b], writes=[cwf.b])
        P.op(DVE, I_copy(cwf.ap[:, 66:132], pc("cwf", 66, 132)), reads=[par.b], writes=[cwf.b])
        P.op(DVE, I_ts(cbf.ap[:, 0:22], pc("cbf", 0, 22), 0.5, None, ALU.mult), reads=[par.b], writes=[cbf.b])
        P.op(DVE, I_copy(cbf.ap[:, 22:44], pc("cbf", 22, 44)), reads=[par.b], writes=[cbf.b])
        P.op(POOL, I_memset(halo.ap, 0.0), writes=[halo.b])

        def p4_load(t):
            s = 0
            P.dma(I_dma(h2t[s].ap, h2T_v[:, :, t * 512:(t + 1) * 512]), reads=[B_h2T_d], writes=[h2t[s].b], sbuf=h2t[s].b)

        def conv3(pad, acc, blk, first_eng):
            P.op(first_eng, I_ts(acc.ap, pad.ap[:, 0:512], cwf.ap[:, blk * 3:blk * 3 + 1], cbf.ap[:, blk:blk + 1], ALU.mult, ALU.add),
                 reads=[pad.b, cwf.b, cbf.b], writes=[acc.b])
            for k in range(1, 3):
                P.op(DVE, I_stt(acc.ap, pad.ap[:, k:k + 512], cwf.ap[:, blk * 3 + k:blk * 3 + k + 1], acc.ap, ALU.mult, ALU.add),
                     reads=[pad.b, cwf.b, acc.b], writes=[acc.b])

        p4_load(0)
        gbanks = [banks[0], banks[1]]
        vbanks = [banks[2], banks[3]]
        dbanks = [banks[4], banks[5], banks[6], banks[7]]
        for t in range(NT):
            hs = h2t[0]
            if t > 0:
                p4_load(t)
            def ffn_A(c):
                s = c % 2
                bg = gbanks[s]
                bv = vbanks[s]
                for kc in range(8):
                    P.op(PE, I_mm(bg.ap, wup.ap[:, kc, c * 128:(c + 1) * 128], hs.ap[:, kc, :], start=(kc == 0), stop=(kc == 7)),
                         reads=[wup_piece[(kc, (c * 128) // 512)], hs.b], writes=[bg.b])
                for kc in range(8):
                    P.op(PE, I_mm(bv.ap, wup.ap[:, kc, DFF + c * 128:DFF + (c + 1) * 128], hs.ap[:, kc, :], start=(kc == 0), stop=(kc == 7)),
                         reads=[wup_piece[(kc, (DFF + c * 128) // 512)], hs.b], writes=[bv.b])
                for (pad, padh, acc, bk, blk) in ((gpad[s], gpadh[s], gacc[c % 3], bg, c), (vpad[s], vpadh[s], vacc[c % 3], bv, 22 + c)):
                    P.op(POOL, I_copy(pad.ap[:, 0:2], halo.ap[:, blk, :]), reads=[halo.b], writes=[padh])
                    P.op(ACT, I_act(acc.ap, bk.ap, AF.Identity, bias=pc("cbf", blk), scale=pc("cwf", blk * 3 + 2)),
                         reads=[bk.b, par.b], writes=[acc.b])
                    P.op(ACT, I_act(pad.ap[:, 2:514], bk.ap, AF.Copy), reads=[bk.b], writes=[pad.b])
                    P.op(POOL, I_copy(halo.ap[:, blk, :], pad.ap[:, 512:514]), reads=[pad.b], writes=[halo.b])

            def ffn_B(c):
                s = c % 2
                for (pad, padh, acc, blk) in ((gpad[s], gpadh[s], gacc[c % 3], c), (vpad[s], vpadh[s], vacc[c % 3], 22 + c)):
                    for k in range(2):
                        P.op(DVE, I_stt(acc.ap, pad.ap[:, k:k + 512], pc("cwf", blk * 3 + k), acc.ap, ALU.mult, ALU.add),
                             reads=[pad.b, padh, par.b, acc.b], writes=[acc.b])

            def ffn_C(c):
                s = c % 2
                P.op(ACT, I_act(th4[s].ap, gacc[c % 3].ap, AF.Silu), reads=[gacc[c % 3].b], writes=[th4[s].b])
                P.op(POOL, I_tt(gT.ap[:, c, :], th4[s].ap, vacc[c % 3].ap, ALU.mult), reads=[th4[s].b, vacc[c % 3].b], writes=[gT.b])

            for step in range(22 + 2):
                if step < 22:
                    ffn_A(step)
                if 0 <= step - 1 < 22:
                    ffn_B(step - 1)
                if 0 <= step - 2 < 22:
                    ffn_C(step - 2)
            n = 0
            for hb in range(2):
                P.dma(I_dma(x1t.ap, x1_v[t][:, 2 * hb:2 * hb + 2, :]), reads=[B_x1_d], writes=[x1t.b], sbuf=x1t.b)
                for bb in range(2):
                    b = 2 * hb + bb
                    for half in range(2):
                        bk = dbanks[n % 4]
                        n += 1
                        for c in range(22):
                            P.op(PE, I_mm(bk.ap, gT.ap[:, c, b * 128:(b + 1) * 128], wdn.ap[:, c, half * 512:(half + 1) * 512],
                                          start=(c == 0), stop=(c == 21)), reads=[gT.b, wdn.b], writes=[bk.b])
                        xs_ = x1t.ap[:, bb, half * 512:(half + 1) * 512]
                        P.op(DVE, I_tt(xs_, xs_, bk.ap, ALU.add), reads=[x1t.b, bk.b], writes=[x1t.b])
                P.dma(I_dma(out_v[t][:, 2 * hb:2 * hb + 2, :], x1t.ap), reads=[x1t.b], writes=[B_out_d], sbuf=x1t.b, eng=POOL)


    except StopBuild:
        pass
    finals = [B_out_d]
    if DEBUG:
        finals += [B_hT_d, B_mixT_d, B_x1_d, B_h2T_d]
    nsem = P.emit(nc, final_wait_bufs=finals)
    return nc, nsem, len(P.ops)


def _pack_params(norm_mix_w, ssd_conv_w, ssd_conv_b, ssd_dt_bias, ssd_a_log, ssd_d, ssd_norm_w,
                 fox_f_bias, fox_q_norm_w, fox_k_norm_w, norm_ffn_w, ffn_conv_w, ffn_conv_b):
    par = np.zeros((128, NPAR), np.float32)

    def put(name, arr):
        lo, hi = PCOL[name]
        par[:, lo:hi] = arr

    put("gmix", norm_mix_w[0].reshape(8, 128).T)
    put("gffn", norm_ffn_w[0].reshape(8, 128).T)
    put("gssd", ssd_norm_w[0].reshape(8, 128).T)
    put("cws", ssd_conv_w[0].reshape(4, 12, 128).transpose(2, 1, 0).reshape(128, 48))
    put("cbs", ssd_conv_b[0].reshape(12, 128).T)
    put("dcol", np.repeat(ssd_d[0].reshape(8, 2), 64, axis=1).T)
    put("wq", np.tile(fox_q_norm_w[0], 2)[:, None])
    put("wk", np.tile(fox_k_norm_w[0], 2)[:, None])
    fb = np.zeros((128, 1), np.float32)
    fb[0:16, 0] = fox_f_bias[0]
    put("fb", fb)
    put("dtb", np.tile(ssd_dt_bias[0][None, :], (128, 1)))
    put("alog", np.tile(ssd_a_log[0][None, :], (128, 1)))
    put("cwf", ffn_conv_w[0].reshape(3, 44, 128).transpose(2, 1, 0).reshape(128, 132))
    put("cbf", ffn_conv_b[0].reshape(44, 128).T)
    return par


def _rows_to_pk(w, nk):
    c = w.shape[1]
    return np.ascontiguousarray(w.reshape(nk, 128, c).transpose(1, 0, 2).reshape(128, nk * c))


_CACHE = {}


def kernel(x, norm_mix_w, w_in, ssd_conv_w, ssd_conv_b, ssd_dt_bias, ssd_a_log, ssd_d,
           ssd_norm_w, fox_f_bias, fox_q_norm_w, fox_k_norm_w, w_out, norm_ffn_w,
           w_up, ffn_conv_w, ffn_conv_b, w_down):
    f = lambda a: np.asarray(a, dtype=np.float32)
    x = f(x)
    par = _pack_params(f(norm_mix_w), f(ssd_conv_w), f(ssd_conv_b), f(ssd_dt_bias), f(ssd_a_log), f(ssd_d),
                       f(ssd_norm_w), f(fox_f_bias), f(fox_q_norm_w), f(fox_k_norm_w), f(norm_ffn_w),
                       f(ffn_conv_w), f(ffn_conv_b))
    cst = np.concatenate([np.eye(128, dtype=np.float32), np.triu(np.ones((128, 128), np.float32)),
                          np.kron(np.eye(2, dtype=np.float32), np.ones((64, 64), np.float32))], axis=1)
    win = _rows_to_pk(f(w_in)[0], 8)
    wout = _rows_to_pk(f(w_out)[0], 16)
    wup = _rows_to_pk(f(w_up)[0], 8)
    wdn = _rows_to_pk(f(w_down)[0], 22)
    if "nc" not in _CACHE:
        _CACHE["nc"] = build_program()
    nc, nsem, nops = _CACHE["nc"]
    n = x.shape[0]
    in_maps = []
    for c in range(n):
        in_maps.append({"x": np.ascontiguousarray(x[c]), "win": win, "wout": wout, "wup": wup, "wdn": wdn,
                        "par": par, "cst": cst})
    res = run_bass_kernel_spmd(nc, in_maps, core_ids=list(range(n)))
    if DEBUG:
        _CACHE["dbg"] = res.results
    return np.stack([r["out"] for r in res.results], axis=0).astype(np.float32)
```

```python
import contextlib
import numpy as np
import concourse.bass as bass
import concourse.mybir as mybir
from concourse.bass_utils import run_bass_kernel_spmd

F32 = mybir.dt.float32
BF16 = mybir.dt.bfloat16
AF = mybir.ActivationFunctionType
ALU = mybir.AluOpType

PE, ACT, DVE, POOL, SP = "tensor", "scalar", "vector", "gpsimd", "sync"
EPOCH = 12000

S = 4096
D = 1024
NT = 8
IN_COLS = 5664
DFF = 2816
EPS = 1e-6
DEBUG = False
UPTO = 5
LIMIT = None


class StopBuild(Exception):
    pass


class Buf:
    def __init__(self, name, multi=False):
        self.name = name
        self.multi = multi
        self.psum = False
        self.writers = []
        self.readers = []


class Op:
    __slots__ = ("eng", "fn", "reads", "writes", "dma", "deps", "need", "tok", "dbuf", "idx")

    def __init__(self, eng, fn, reads, writes, dma, dbuf):
        self.eng = eng
        self.fn = fn
        self.reads = reads
        self.writes = writes
        self.dma = dma
        self.dbuf = dbuf
        self.deps = set()
        self.need = False
        self.tok = None


class Prog:
    def __init__(self):
        self.ops = []

    def op(self, eng, fn, reads=(), writes=(), dma=False, dbuf=None):
        o = Op(eng, fn, tuple(reads), tuple(writes), dma, dbuf)
        o.idx = len(self.ops)
        self.ops.append(o)
        return o

    def dma(self, fn, reads=(), writes=(), sbuf=None, eng=SP):
        return self.op(eng, fn, reads, writes, dma=True, dbuf=sbuf)

    def barrier(self):
        last = {}
        last_dma = {}
        for o in self.ops:
            if o.dma:
                last_dma[id(o.dbuf)] = o.idx
            elif o.fn is not None:
                last[o.eng] = o.idx
        deps = set(last.values()) | set(last_dma.values())
        for eng in (SP, PE, ACT, DVE, POOL):
            j = Op(eng, None, (), (), False, None)
            j.deps = set(deps)
            j.idx = len(self.ops)
            self.ops.append(j)

    def analyze(self):
        last_dma_on = {}
        for o in self.ops:
            deps = o.deps
            for b in o.reads:
                deps.update(b.writers)
                if b.psum:
                    for r in b.readers:
                        if self.ops[r].eng != o.eng:
                            deps.add(r)
            for b in o.writes:
                if not b.multi:
                    deps.update(b.writers)
                deps.update(b.readers)
            if o.dma:
                p = last_dma_on.get(id(o.dbuf))
                if p is not None:
                    deps.add(p)
                last_dma_on[id(o.dbuf)] = o.idx
            wset = set(id(b) for b in o.writes)
            for b in o.reads:
                if id(b) not in wset:
                    b.readers.append(o.idx)
            for b in o.writes:
                if b.multi:
                    b.writers.append(o.idx)
                else:
                    b.writers = [o.idx]
                b.readers = []
            deps.discard(o.idx)
            if o.eng == PE and not o.dma:
                for d in list(deps):
                    od = self.ops[d]
                    if od.eng == PE and not od.dma:
                        deps.discard(d)
            for d in deps:
                self.ops[d].need = True

    def emit(self, nc, final_wait_bufs=()):
        if LIMIT is not None:
            self.ops = self.ops[:LIMIT]
        self.analyze()
        join = Op(SP, None, (), (), False, None)
        join.idx = len(self.ops)
        for b in final_wait_bufs:
            for w in b.writers:
                join.deps.add(w)
                self.ops[w].need = True
        self.ops.append(join)

        eng_count = {PE: 0, ACT: 0, DVE: 0, POOL: 0, SP: 0}
        dma_slots = {}
        eng_sems = {}
        for o in self.ops:
            if o.dma:
                key = id(o.dbuf)
                k = dma_slots.get(key, 0) + 1
                dma_slots[key] = k
                o.tok = ("d", key, 16 * k)
            elif o.need and o.fn is not None:
                n = eng_count[o.eng]
                eng_count[o.eng] = n + 1
                o.tok = ("e", (o.eng, n // EPOCH), (n % EPOCH) + 1)
                eng_sems[(o.eng, n // EPOCH)] = True
        sem_keys = list(eng_sems.keys()) + [("dma", k) for k in dma_slots.keys()]
        es = contextlib.ExitStack()
        with es:
            sems = {}
            for i, k in enumerate(sem_keys):
                sems[k] = es.enter_context(nc.semaphore(f"s{i}"))
            block = es.enter_context(nc.Block())
            ops = self.ops

            def run_engine(engname):
                def body(e):
                    waited = {}
                    for o in ops:
                        if o.eng != engname:
                            continue
                        need = {}
                        for d in o.deps:
                            t = ops[d].tok
                            if t is None:
                                continue
                            k = (t[0], t[1])
                            if need.get(k, 0) < t[2]:
                                need[k] = t[2]
                        for k, v in need.items():
                            if waited.get(k, 0) >= v:
                                continue
                            waited[k] = v
                            s = sems[("dma", k[1])] if k[0] == "d" else sems[k[1]]
                            e.wait_ge(s, v)
                        if o.fn is None:
                            continue
                        ins = o.fn(e)
                        if o.tok is not None:
                            if o.tok[0] == "d":
                                ins.then_inc(sems[("dma", o.tok[1])], 16)
                            else:
                                ins.then_inc(sems[o.tok[1]], 1)
                return body

            block.sync(run_engine(SP))
            block.tensor(run_engine(PE))
            block.scalar(run_engine(ACT))
            block.vector(run_engine(DVE))
            block.gpsimd(run_engine(POOL))
        return len(sem_keys)


def I_mm(out, lhsT, rhs, start=True, stop=True):
    return lambda e: e.matmul(out, lhsT, rhs, start=start, stop=stop)


def I_tr(out, in_, ident):
    return lambda e: e.transpose(out, in_, ident)


def I_act(out, in_, func, bias=None, scale=None, accum=None):
    kw = {}
    if bias is not None:
        kw["bias"] = bias
    if scale is not None:
        kw["scale"] = scale
    if accum is not None:
        kw["accum_out"] = accum
    return lambda e: e.activation(out, in_, func, **kw)


def I_ts(out, in0, s1, s2, op0, op1=None):
    if op1 is None:
        return lambda e: e.tensor_scalar(out, in0, s1, None, op0)
    return lambda e: e.tensor_scalar(out, in0, s1, s2, op0, op1)


def I_tt(out, in0, in1, op):
    return lambda e: e.tensor_tensor(out, in0, in1, op)


def I_stt(out, in0, scalar, in1, op0, op1):
    return lambda e: e.scalar_tensor_tensor(out, in0, scalar, in1, op0, op1)


def I_copy(out, in_):
    return lambda e: e.tensor_copy(out, in_)


def I_recip(out, in_):
    return lambda e: e.reciprocal(out, in_)


def I_memset(ap, v):
    return lambda e: e.memset(ap, v)


def I_dma(out, in_):
    return lambda e: e.dma_start(out=out, in_=in_)


def I_scan(out, d0, d1, init, op0, op1):
    return lambda e: e.tensor_tensor_scan(out, d0, d1, init, op0, op1)


class T:
    def __init__(self, ap, name):
        self.ap = ap
        self.b = Buf(name)


class Arena:
    def __init__(self, ap, cap):
        self.ap = ap
        self.cap = cap
        self.off = 0

    def alloc(self, shape, dt, name):
        esz = 2 if dt == BF16 else 4
        n = int(np.prod(shape[1:]))
        nb = (n * esz + 63) // 64 * 64
        off = self.off
        self.off += nb
        assert self.off <= self.cap, (name, self.off, self.cap)
        v = self.ap[:, off // 4:(off + nb) // 4]
        if dt == BF16:
            v = v.bitcast(BF16)
        v = v[:, 0:n]
        if len(shape) == 3:
            v = v.rearrange("p (a b) -> p a b", a=shape[1])
        elif len(shape) == 4:
            v = v.rearrange("p (a b c) -> p a b c", a=shape[1], b=shape[2])
        if shape[0] != 128:
            v = v[0:shape[0]]
        return T(v, name)


PCOL = {}
_o = 0
for _n, _w in (("gmix", 8), ("gffn", 8), ("gssd", 8), ("cws", 48), ("cbs", 12), ("dcol", 8),
               ("wq", 1), ("wk", 1), ("fb", 1), ("dtb", 16), ("alog", 16), ("cwf", 132), ("cbf", 44)):
    PCOL[_n] = (_o, _o + _w)
    _o += _w
NPAR = _o


def build_program():
    nc = bass.Bass("TRN2", target_bir_lowering=False)
    x_d = nc.dram_tensor("x", [S, D], F32, kind="ExternalInput").ap()
    win_d = nc.dram_tensor("win", [128, 8 * IN_COLS], F32, kind="ExternalInput").ap()
    wout_d = nc.dram_tensor("wout", [128, 16 * D], F32, kind="ExternalInput").ap()
    wup_d = nc.dram_tensor("wup", [128, 8 * 2 * DFF], F32, kind="ExternalInput").ap()
    wdn_d = nc.dram_tensor("wdn", [128, 22 * D], F32, kind="ExternalInput").ap()
    par_d = nc.dram_tensor("par", [128, NPAR], F32, kind="ExternalInput").ap()
    cst_d = nc.dram_tensor("cst", [128, 384], F32, kind="ExternalInput").ap()
    out_d = nc.dram_tensor("out", [S, D], F32, kind="ExternalOutput").ap()
    def skind(nm):
        return "ExternalOutput" if (DEBUG and nm in DEBUG) else "Internal"
    hT_d = nc.dram_tensor("hT_s", [128, 8 * S], BF16, kind=skind("hT_s")).ap()
    mixT_d = nc.dram_tensor("mixT_s", [128, 16 * S], BF16, kind=skind("mixT_s")).ap()
    x1_d = nc.dram_tensor("x1_s", [S, D], F32, kind=skind("x1_s")).ap()
    h2T_d = nc.dram_tensor("h2T_s", [128, 8 * S], BF16, kind=skind("h2T_s")).ap()

    win_v = win_d.rearrange("p (k c) -> p k c", k=8)
    wout_v = wout_d.rearrange("p (k c) -> p k c", k=16)
    wup_v = wup_d.rearrange("p (k c) -> p k c", k=8)
    wdn_v = wdn_d.rearrange("p (k c) -> p k c", k=22)
    hT_v = hT_d.rearrange("p (k t) -> p k t", k=8)
    mixT_v = mixT_d.rearrange("p (k t) -> p k t", k=16)
    h2T_v = h2T_d.rearrange("p (k t) -> p k t", k=8)
    x_v = x_d.rearrange("(g b p) d -> g p b d", b=4, p=128)
    x1_v = x1_d.rearrange("(g b p) d -> g p b d", b=4, p=128)
    out_v = out_d.rearrange("(g b p) d -> g p b d", b=4, p=128)

    B_hT_d = Buf("hT_d", multi=True)
    B_mixT_d = Buf("mixT_d", multi=True)
    B_x1_d = Buf("x1_d", multi=True)
    B_h2T_d = Buf("h2T_d", multi=True)
    B_out_d = Buf("out_d", multi=True)

    CAP = 207 * 1024
    arena_t = nc.alloc_sbuf_tensor("arena", [128, CAP // 4], F32)
    A = Arena(arena_t[:, :], CAP)
    banks = []
    for i in range(8):
        pt = nc.alloc_psum_tensor(f"psb{i}", [128, 512], F32)
        banks.append(T(pt[:, :], f"psb{i}"))
        banks[-1].b.psum = True

    def bf_view(bank_ap, a):
        return bank_ap.bitcast(BF16).rearrange("p (a b) -> p a b", a=a)

    P = Prog()

    par = A.alloc([128, NPAR], F32, "par")
    cst = A.alloc([128, 384], F32, "cst")
    identb = A.alloc([128, 128], BF16, "identb")
    triub = A.alloc([128, 128], BF16, "triub")
    bonesb = A.alloc([128, 128], BF16, "bonesb")
    onesb = A.alloc([128, 128], BF16, "onesb")
    ones16 = A.alloc([128, 512], F32, "ones16")
    identf = cst.ap[:, 0:128]
    triuf = cst.ap[:, 128:256]

    def pc(name, a=None, b=None):
        lo, hi = PCOL[name]
        if a is None:
            return par.ap[:, lo:hi]
        return par.ap[:, lo + a:lo + (b if b is not None else a + 1)]

    P.dma(I_dma(par.ap, par_d), writes=[par.b], sbuf=par.b)
    P.dma(I_dma(cst.ap, cst_d), writes=[cst.b], sbuf=cst.b)
    P.op(DVE, I_copy(identb.ap, cst.ap[:, 0:128]), reads=[cst.b], writes=[identb.b])
    P.op(DVE, I_copy(triub.ap, cst.ap[:, 128:256]), reads=[cst.b], writes=[triub.b])
    P.op(DVE, I_copy(bonesb.ap, cst.ap[:, 256:384]), reads=[cst.b], writes=[bonesb.b])
    P.op(POOL, I_memset(onesb.ap, 1.0), writes=[onesb.b])
    P.op(POOL, I_memset(ones16.ap, 1.0), writes=[ones16.b])
    gbase = A.off

    def rms_stage_a(xg, ss, lnv, rstd, junk):
        for b in range(4):
            P.op(ACT, I_act(junk.ap, xg.ap[:, b, :], AF.Square, accum=ss.ap[:, b:b + 1]),
                 reads=[xg.b], writes=[junk.b, ss.b])
        P.op(ACT, I_act(lnv.ap, ss.ap, AF.Ln, bias=EPS, scale=1.0 / D), reads=[ss.b], writes=[lnv.b])
        P.op(ACT, I_act(rstd.ap, lnv.ap, AF.Exp, scale=-0.5), reads=[lnv.b], writes=[rstd.b])

    def norm_transpose(xg, rstd, xn, tbanks, hst, evac_toggle):
        for b in range(4):
            P.op(DVE, I_ts(xn.ap[:, b, :], xg.ap[:, b, :], rstd.ap[:, b:b + 1], None, ALU.mult),
                 reads=[xg.b, rstd.b], writes=[xn.b])
        for b in range(4):
            bk = tbanks[b % len(tbanks)]
            pv = bf_view(bk.ap, 8)
            for kc in range(8):
                P.op(PE, I_tr(pv[:, kc, :], xn.ap[:, b, kc * 128:(kc + 1) * 128], identb.ap),
                     reads=[xn.b, identb.b], writes=[bk.b])
            eng = ACT if (b + evac_toggle) % 2 == 0 else DVE
            if eng == ACT:
                P.op(ACT, I_act(hst.ap[:, :, b * 128:(b + 1) * 128], pv, AF.Copy), reads=[bk.b], writes=[hst.b])
            else:
                P.op(DVE, I_copy(hst.ap[:, :, b * 128:(b + 1) * 128], pv), reads=[bk.b], writes=[hst.b])

    def load_weight_rows(dst, src_v, nk, c0, c1, gcol_fn, stages, cnt, scale2=1.0, chunk=1024):
        for k in range(nk):
            for cc in range(c0, c1, chunk):
                ce = min(cc + chunk, c1)
                st = stages[cnt[0] % len(stages)]
                eng = DVE if cnt[0] % 2 == 0 else ACT
                cnt[0] += 1
                P.dma(I_dma(st.ap[:, 0:ce - cc], src_v[:, k, cc:ce]), writes=[st.b], sbuf=st.b)
                g = gcol_fn(k)
                d_ap = dst.ap[:, k, cc - c0:ce - c0]
                s_ap = st.ap[:, 0:ce - cc]
                if eng == ACT:
                    if g is None:
                        P.op(ACT, I_act(d_ap, s_ap, AF.Copy), reads=[st.b], writes=[dst.b])
                    else:
                        assert scale2 == 1.0
                        P.op(ACT, I_act(d_ap, s_ap, AF.Identity, scale=g), reads=[st.b, par.b], writes=[dst.b])
                elif g is None:
                    P.op(DVE, I_copy(d_ap, s_ap), reads=[st.b], writes=[dst.b])
                else:
                    P.op(DVE, I_ts(d_ap, s_ap, g, scale2, ALU.mult, ALU.mult), reads=[st.b, par.b], writes=[dst.b])

    try:
        A.off = gbase
        xgs = [A.alloc([128, 4, D], F32, f"xg{i}") for i in range(2)]
        xns = [A.alloc([128, 4, D], BF16, f"xn{i}") for i in range(2)]
        hsts = [A.alloc([128, 8, 512], BF16, f"hst{i}") for i in range(2)]
        junk = A.alloc([128, D], BF16, "junk")
        sss = [A.alloc([128, 4], F32, f"ss{i}") for i in range(2)]
        lnvs = [A.alloc([128, 4], F32, f"lnv{i}") for i in range(2)]
        rstds = [A.alloc([128, 4], F32, f"rstd{i}") for i in range(2)]

        def p1_a(g):
            s = g % 2
            P.dma(I_dma(xgs[s].ap, x_v[g]), writes=[xgs[s].b], sbuf=xgs[s].b)
            rms_stage_a(xgs[s], sss[s], lnvs[s], rstds[s], junk)

        def p1_b(g):
            s = g % 2
            norm_transpose(xgs[s], rstds[s], xns[s], banks[0:4], hsts[s], g)
            P.dma(I_dma(hT_v[:, :, g * 512:(g + 1) * 512], hsts[s].ap), reads=[hsts[s].b], writes=[B_hT_d], sbuf=hsts[s].b, eng=POOL)

        p1_a(0)
        for g in range(NT):
            if g + 1 < NT:
                p1_a(g + 1)
            p1_b(g)

        if UPTO < 2:
            raise StopBuild()
        P.barrier()
        A.off = gbase
        wssd = A.alloc([128, 8, 2560], BF16, "wssd")
        wdt = A.alloc([128, 8, 16], BF16, "wdt")
        wdts = A.alloc([128, 8, 16], F32, "wdts")
        wst = [A.alloc([128, 1024], F32, f"wst{i}") for i in range(2)]
        cwh = A.alloc([128, 48], F32, "cwh")
        cbh = A.alloc([128, 12], F32, "cbh")
        a_b = A.alloc([128, 16], F32, "a_b")
        hTt = [A.alloc([128, 8, 512], BF16, f"hTt{i}") for i in range(2)]
        szT = A.alloc([128, 8, 512], BF16, "szT")
        xpads = [A.alloc([128, 515], F32, f"xpad{i}") for i in range(12)]
        xcT = [A.alloc([128, 512], BF16, f"xcT{i}") for i in range(12)]
        cacc = [A.alloc([128, 512], F32, f"cacc{i}") for i in range(3)]
        thb = [A.alloc([128, 512], F32, f"thb{i}") for i in range(2)]
        dtx = A.alloc([128, 64], F32, "dtx")
        dt_t = A.alloc([128, 4, 16], F32, "dt_t")
        dtA_t = A.alloc([128, 4, 16], F32, "dtA_t")
        nacs = A.alloc([128, 16], F32, "nacs")
        dtmp = A.alloc([128, 16], F32, "dtmp")
        cdl = A.alloc([128, 16], F32, "cdl")
        dte = A.alloc([128, 16], F32, "dte")
        cds = [A.alloc([128, 16], F32, f"cd{i}") for i in range(2)]
        dhi = A.alloc([128, 4, 16], BF16, "dhi")
        dlo = A.alloc([128, 4, 16], BF16, "dlo")
        dres = A.alloc([128, 4, 16], F32, "dres")
        Dhi = A.alloc([128, 8, 128], BF16, "Dhi")
        Dlo = A.alloc([128, 8, 128], BF16, "Dlo")
        w2 = A.alloc([128, 16], F32, "w2")
        ET = A.alloc([128, 16, 128], BF16, "ET")
        ETp = A.alloc([128, 16, 128], F32, "ETp")
        EA = A.alloc([128, 16, 128], BF16, "EA")
        CBTm = A.alloc([128, 2, 128], F32, "CBTm")
        scT = A.alloc([128, 16, 128], BF16, "scT")
        xdt = A.alloc([128, 16, 64], BF16, "xdt")
        xdte = A.alloc([128, 16, 64], BF16, "xdte")
        Btok = A.alloc([128, 2, 128], BF16, "Btok")
        Csc = A.alloc([128, 16, 128], BF16, "Csc")
        Sst = A.alloc([128, 16, 64], F32, "Sst")
        Sbf = A.alloc([128, 16, 64], BF16, "Sbf")
        yt = A.alloc([128, 8, 128], F32, "yt")
        sqb = A.alloc([128, 8, 128], BF16, "sqb")
        lnr = A.alloc([128, 2, 128], F32, "lnr")
        rsr = A.alloc([128, 2, 128], F32, "rsr")
        ystage = A.alloc([128, 8, 512], BF16, "ystage")

        cnt = [0]
        load_weight_rows(wssd, win_v, 8, 0, 1024, lambda k: pc("gmix", k), wst, cnt, scale2=1.0)
        wx = T(wssd.ap[:, :, 1024:2560], "wssd_x")
        wx.b = wssd.b
        load_weight_rows(wx, win_v, 8, 1024, 2560, lambda k: pc("gmix", k), wst, cnt, scale2=1.0, chunk=768)
        P.dma(I_dma(wdts.ap, win_v[:, :, 2560:2576]), writes=[wdts.b], sbuf=wdts.b)
        P.op(DVE, I_tt(wdt.ap, wdts.ap, pc("gmix").unsqueeze(2).to_broadcast([128, 8, 16]), ALU.mult),
             reads=[wdts.b, par.b], writes=[wdt.b])
        P.op(DVE, I_ts(cwh.ap, pc("cws"), 0.5, None, ALU.mult), reads=[par.b], writes=[cwh.b])
        P.op(DVE, I_ts(cbh.ap, pc("cbs"), 0.5, None, ALU.mult), reads=[par.b], writes=[cbh.b])
        P.op(ACT, I_act(a_b.ap, pc("alog"), AF.Exp), reads=[par.b], writes=[a_b.b])
        P.op(DVE, I_ts(a_b.ap, a_b.ap, -1.0, None, ALU.mult), reads=[a_b.b], writes=[a_b.b])
        P.op(POOL, I_memset(Sst.ap, 0.0), writes=[Sst.b])
        P.op(POOL, I_memset(Sbf.ap, 0.0), writes=[Sbf.b])
        P.op(DVE, I_tt(yt.ap, identf.unsqueeze(1).to_broadcast([128, 8, 128]), pc("dcol").unsqueeze(2).to_broadcast([128, 8, 128]), ALU.mult),
             reads=[cst.b, par.b], writes=[yt.b])
        P.op(DVE, I_copy(Dhi.ap, yt.ap), reads=[yt.b], writes=[Dhi.b])
        P.op(DVE, I_tt(yt.ap, yt.ap, Dhi.ap, ALU.subtract), reads=[yt.b, Dhi.b], writes=[yt.b])
        P.op(DVE, I_copy(Dlo.ap, yt.ap), reads=[yt.b], writes=[Dlo.b])
        for i in range(12):
            P.op(POOL, I_memset(xpads[i].ap[:, 0:3], 0.0), writes=[xpads[i].b])

        pb_misc = banks[0]
        psD = T(pb_misc.ap[:, 0:64], "psD")
        psA = T(pb_misc.ap[:, 64:80], "psA")
        psC = T(pb_misc.ap[:, 128:384], "psC")
        psBt = T(pb_misc.ap[:, 384:512], "psBt")
        psSS = T(banks[3].ap[:, 256:512], "psSS")
        psR = [banks[1], banks[2]]
        psTb = T(banks[3].ap[:, 0:256], "psTx")
        psT = banks[7]
        psY = [banks[4], banks[5]]
        psSt = [banks[6], banks[3]]
        psSS.b = banks[3].b
        psD.b = psA.b = psC.b = psBt.b = pb_misc.b
        projbanks = [banks[1], banks[2], banks[4], banks[5], banks[6], banks[7]]
        pcount = [0]

        def ssd_tile_load(t):
            s = t % 2
            P.dma(I_dma(hTt[s].ap, hT_v[:, :, t * 512:(t + 1) * 512]), reads=[B_hT_d], writes=[hTt[s].b], sbuf=hTt[s].b)

        ssd_tile_load(0)
        for t in range(NT):
            hs = hTt[t % 2]
            if t + 1 < NT:
                ssd_tile_load(t + 1)
            for c in range(4):
                for kc in range(8):
                    P.op(PE, I_mm(psD.ap[:, c * 16:(c + 1) * 16], hs.ap[:, kc, c * 128:(c + 1) * 128], wdt.ap[:, kc, :],
                                  start=(kc == 0), stop=(kc == 7)), reads=[hs.b, wdt.b], writes=[psD.b])
            P.op(DVE, I_tt(dtx.ap.rearrange("p (c h) -> p c h", c=4), psD.ap.rearrange("p (c h) -> p c h", c=4),
                           pc("dtb").unsqueeze(1).to_broadcast([128, 4, 16]), ALU.add),
                 reads=[psD.b, par.b], writes=[dtx.b])
            P.op(ACT, I_act(dtx.ap, dtx.ap, AF.Exp), reads=[dtx.b], writes=[dtx.b])
            P.op(ACT, I_act(dt_t.ap.rearrange("p c h -> p (c h)"), dtx.ap, AF.Ln, bias=1.0, scale=1.0), reads=[dtx.b], writes=[dt_t.b])
            P.op(DVE, I_tt(dtA_t.ap, dt_t.ap, a_b.ap.unsqueeze(1).to_broadcast([128, 4, 16]), ALU.mult),
                 reads=[dt_t.b, a_b.b], writes=[dtA_t.b])
            P.op(DVE, I_copy(dhi.ap, dtA_t.ap), reads=[dtA_t.b], writes=[dhi.b])
            P.op(DVE, I_tt(dres.ap, dtA_t.ap, dhi.ap, ALU.subtract), reads=[dtA_t.b, dhi.b], writes=[dres.b])
            P.op(DVE, I_copy(dlo.ap, dres.ap), reads=[dres.b], writes=[dlo.b])
            for zb in range(8):
                bk = projbanks[pcount[0] % len(projbanks)]
                pcount[0] += 1
                for kc in range(8):
                    P.op(PE, I_mm(bk.ap, wssd.ap[:, kc, zb * 128:(zb + 1) * 128], hs.ap[:, kc, :], start=(kc == 0), stop=(kc == 7)),
                         reads=[wssd.b, hs.b], writes=[bk.b])
                P.op(ACT, I_act(szT.ap[:, zb, :], bk.ap, AF.Silu), reads=[bk.b], writes=[szT.b])
            def xbc_A(xb):
                bk = projbanks[pcount[0] % len(projbanks)]
                pcount[0] += 1
                for kc in range(8):
                    P.op(PE, I_mm(bk.ap, wssd.ap[:, kc, 1024 + xb * 128:1024 + (xb + 1) * 128], hs.ap[:, kc, :],
                                  start=(kc == 0), stop=(kc == 7)), reads=[wssd.b, hs.b], writes=[bk.b])
                xp = xpads[xb]
                ac = cacc[xb % 3]
                P.op(ACT, I_act(ac.ap, bk.ap, AF.Identity, bias=pc("cbs", xb), scale=pc("cws", xb * 4 + 3)),
                     reads=[bk.b, par.b], writes=[ac.b])
                P.op(ACT, I_act(xp.ap[:, 3:515], bk.ap, AF.Copy), reads=[bk.b], writes=[xp.b])

            def xbc_B(xb):
                xp = xpads[xb]
                ac = cacc[xb % 3]
                for k in range(3):
                    P.op(DVE, I_stt(ac.ap, xp.ap[:, k:k + 512], pc("cws", xb * 4 + k), ac.ap, ALU.mult, ALU.add),
                         reads=[xp.b, par.b, ac.b], writes=[ac.b])
                P.op(POOL, I_copy(xp.ap[:, 0:3], xp.ap[:, 512:515]), reads=[xp.b], writes=[xp.b])

            def xbc_C(xb):
                ac = cacc[xb % 3]
                P.op(ACT, I_act(xcT[xb].ap, ac.ap, AF.Silu), reads=[ac.b], writes=[xcT[xb].b])

            for step in range(12 + 2):
                if step < 12:
                    xbc_A(step)
                if 0 <= step - 1 < 12:
                    xbc_B(step - 1)
                if 0 <= step - 2 < 12:
                    xbc_C(step - 2)
            def ch_prep(c):
                P.op(PE, I_mm(psA.ap, triuf, dtA_t.ap[:, c, :]), reads=[cst.b, dtA_t.b], writes=[psA.b])
                P.op(DVE, I_ts(nacs.ap, psA.ap, -1.0, None, ALU.mult), reads=[psA.b], writes=[nacs.b])
                for rb in range(4):
                    bk = psR[rb % 2]
                    for hh in range(4):
                        h = rb * 4 + hh
                        o_ap = bk.ap[:, hh * 128:(hh + 1) * 128]
                        P.op(PE, I_mm(o_ap, dhi.ap[:, c, h:h + 1].to_broadcast([128, 128]), triub.ap, start=True, stop=False),
                             reads=[dhi.b, triub.b], writes=[bk.b])
                        P.op(PE, I_mm(o_ap, dlo.ap[:, c, h:h + 1].to_broadcast([128, 128]), triub.ap, start=False, stop=True),
                             reads=[dlo.b, triub.b], writes=[bk.b])
                    P.op(DVE, I_tt(ETp.ap[:, rb * 4:rb * 4 + 4, :], bk.ap.rearrange("p (a b) -> p a b", a=4),
                                   nacs.ap[:, rb * 4:rb * 4 + 4].unsqueeze(2).to_broadcast([128, 4, 128]), ALU.add),
                         reads=[bk.b, nacs.b], writes=[ETp.b])
                    P.op(ACT, I_act(EA.ap[:, rb * 4:rb * 4 + 4, :].rearrange("p a b -> p (a b)"), bk.ap, AF.Exp),
                         reads=[bk.b], writes=[EA.b])
                    last = bk.ap.rearrange("p (a b) -> p a b", a=4)[:, :, 127:128].rearrange("p a b -> p (a b)")
                    P.op(DVE, I_tt(dtmp.ap[:, rb * 4:rb * 4 + 4], last, nacs.ap[:, rb * 4:rb * 4 + 4], ALU.add),
                         reads=[bk.b, nacs.b], writes=[dtmp.b])
                    P.op(DVE, I_copy(cdl.ap[:, rb * 4:rb * 4 + 4], last), reads=[bk.b], writes=[cdl.b])
                P.op(ACT, I_act(ETp.ap.rearrange("p a b -> p (a b)"), ETp.ap.rearrange("p a b -> p (a b)"), AF.Abs),
                     reads=[ETp.b], writes=[ETp.b])
                P.op(ACT, I_act(ET.ap.rearrange("p a b -> p (a b)"), ETp.ap.rearrange("p a b -> p (a b)"), AF.Exp, scale=-1.0),
                     reads=[ETp.b], writes=[ET.b])
                P.op(ACT, I_act(dte.ap, dtmp.ap, AF.Exp), reads=[dtmp.b], writes=[dte.b])
                P.op(ACT, I_act(cds[c % 2].ap, cdl.ap, AF.Exp), reads=[cdl.b], writes=[cds[c % 2].b])

            def ch_mid(c):
                cs = slice(c * 128, (c + 1) * 128)
                for g in range(2):
                    P.op(PE, I_mm(psC.ap[:, g * 128:(g + 1) * 128], xcT[8 + g].ap[:, cs], xcT[10 + g].ap[:, cs]),
                         reads=[xcT[8 + g].b, xcT[10 + g].b], writes=[psC.b])
                P.op(DVE, I_tt(CBTm.ap, psC.ap.rearrange("p (g l) -> p g l", g=2), triuf.unsqueeze(1).to_broadcast([128, 2, 128]), ALU.mult),
                     reads=[psC.b, cst.b], writes=[CBTm.b])
                for g in range(2):
                    P.op(DVE, I_tt(scT.ap[:, 8 * g:8 * g + 8, :], ET.ap[:, 8 * g:8 * g + 8, :],
                                   CBTm.ap[:, g:g + 1, :].to_broadcast([128, 8, 128]), ALU.mult),
                         reads=[ET.b, CBTm.b], writes=[scT.b])
                for g in range(2):
                    P.op(POOL, I_tt(Csc.ap[:, 8 * g:8 * g + 8, :], EA.ap[:, 8 * g:8 * g + 8, :],
                                    xcT[10 + g].ap[:, cs].unsqueeze(1).to_broadcast([128, 8, 128]), ALU.mult),
                         reads=[EA.b, xcT[10 + g].b], writes=[Csc.b])
                pvT = bf_view(psT.ap, 8)
                for blk in range(8):
                    P.op(PE, I_tr(pvT[:, blk, :], xcT[blk].ap[:, cs], identb.ap), reads=[xcT[blk].b, identb.b], writes=[psT.b])
                P.op(DVE, I_tt(w2.ap, dt_t.ap[:, c, :], dte.ap, ALU.mult), reads=[dt_t.b, dte.b], writes=[w2.b])
                pvT16 = psT.ap.bitcast(BF16).rearrange("p (h d) -> p h d", h=16)
                P.op(DVE, I_tt(xdt.ap, pvT16, dt_t.ap[:, c, :].unsqueeze(2).to_broadcast([128, 16, 64]), ALU.mult),
                     reads=[psT.b, dt_t.b], writes=[xdt.b])
                P.op(DVE, I_tt(xdte.ap, pvT16, w2.ap.unsqueeze(2).to_broadcast([128, 16, 64]), ALU.mult),
                     reads=[psT.b, w2.b], writes=[xdte.b])
                pvB = psBt.ap.bitcast(BF16).rearrange("p (g n) -> p g n", g=2)
                for g in range(2):
                    P.op(PE, I_tr(pvB[:, g, :], xcT[8 + g].ap[:, cs], identb.ap), reads=[xcT[8 + g].b, identb.b], writes=[psBt.b])
                P.op(ACT, I_act(Btok.ap, pvB, AF.Copy), reads=[psBt.b], writes=[Btok.b])

            def ch_fin(c):
                cs = slice(c * 128, (c + 1) * 128)
                for hp in range(8):
                    bk = psY[hp // 4]
                    col = (hp % 4) * 128
                    full = bk.ap[:, col:col + 128]
                    P.op(PE, I_mm(full, Dhi.ap[:, hp, :], xcT[hp].ap[:, cs], start=True, stop=False),
                         reads=[Dhi.b, xcT[hp].b], writes=[bk.b])
                    P.op(PE, I_mm(full, Dlo.ap[:, hp, :], xcT[hp].ap[:, cs], start=False, stop=False),
                         reads=[Dlo.b, xcT[hp].b], writes=[bk.b])
                    for half in range(2):
                        h = 2 * hp + half
                        o_ap = bk.ap[64 * half:64 * half + 64, col:col + 128]
                        P.op(PE, I_mm(o_ap, xdt.ap[:, h, :], scT.ap[:, h, :], start=False, stop=False),
                             reads=[xdt.b, scT.b], writes=[bk.b])
                        P.op(PE, I_mm(o_ap, Sbf.ap[:, h, :], Csc.ap[:, h, :], start=False, stop=(half == 1)),
                             reads=[Sbf.b, Csc.b], writes=[bk.b])
                for k in range(2):
                    P.op(DVE, I_tt(yt.ap[:, 4 * k:4 * k + 4, :], psY[k].ap.rearrange("p (a b) -> p a b", a=4),
                                   szT.ap[:, 4 * k:4 * k + 4, cs], ALU.mult),
                         reads=[psY[k].b, szT.b], writes=[yt.b])
                P.op(ACT, I_act(sqb.ap, yt.ap, AF.Square), reads=[yt.b], writes=[sqb.b])
                for g in range(2):
                    for j in range(4):
                        P.op(PE, I_mm(psSS.ap[:, g * 128:(g + 1) * 128], onesb.ap, sqb.ap[:, 4 * g + j, :], start=(j == 0), stop=(j == 3)),
                             reads=[onesb.b, sqb.b], writes=[psSS.b])
                P.op(ACT, I_act(lnr.ap.rearrange("p g l -> p (g l)"), psSS.ap, AF.Ln, bias=EPS, scale=1.0 / 512),
                     reads=[psSS.b], writes=[lnr.b])
                P.op(ACT, I_act(rsr.ap, lnr.ap, AF.Exp, scale=-0.5), reads=[lnr.b], writes=[rsr.b])
                for g in range(2):
                    P.op(POOL, I_tt(ystage.ap[:, 4 * g:4 * g + 4, cs], yt.ap[:, 4 * g:4 * g + 4, :],
                                    rsr.ap[:, g:g + 1, :].to_broadcast([128, 4, 128]), ALU.mult),
                         reads=[yt.b, rsr.b], writes=[ystage.b])
                for g in range(2):
                    bk = psSt[g]
                    P.op(PE, I_mm(bk.ap, Btok.ap[:, g, :], xdte.ap[:, 8 * g:8 * g + 8, :].rearrange("p h d -> p (h d)")),
                         reads=[Btok.b, xdte.b], writes=[bk.b])
                cdc = cds[c % 2]
                for g in range(2):
                    bk = psSt[g]
                    sv = Sst.ap[:, 8 * g:8 * g + 8, :]
                    P.op(DVE, I_tt(sv, sv, cdc.ap[:, 8 * g:8 * g + 8].unsqueeze(2).to_broadcast([128, 8, 64]), ALU.mult),
                         reads=[Sst.b, cdc.b], writes=[Sst.b])
                    P.op(DVE, I_tt(sv, sv, bk.ap.rearrange("p (h d) -> p h d", h=8), ALU.add),
                         reads=[Sst.b, bk.b], writes=[Sst.b])
                P.op(ACT, I_act(Sbf.ap, Sst.ap, AF.Copy), reads=[Sst.b], writes=[Sbf.b])

            ch_prep(0)
            for c in range(4):
                ch_mid(c)
                if c + 1 < 4:
                    ch_prep(c + 1)
                ch_fin(c)
            P.dma(I_dma(mixT_v[:, 0:8, t * 512:(t + 1) * 512], ystage.ap), reads=[ystage.b], writes=[B_mixT_d], sbuf=ystage.b, eng=POOL)

        if UPTO < 3:
            raise StopBuild()
        P.barrier()
        A.off = gbase
        hT = A.alloc([128, 8, S], BF16, "hT")
        hT_k = []
        for kc in range(8):
            t_ = T(hT.ap[:, kc, :], f"hT{kc}")
            hT_k.append(t_)
        cumT = A.alloc([16, S], F32, "cumT")
        ftmp = A.alloc([16, S], F32, "ftmp")
        cumbf = A.alloc([16, S], BF16, "cumbf")
        ncum = A.alloc([128, 32, 16], F32, "ncum")
        nfb = A.alloc([128, 1], F32, "nfb")
        wq8 = A.alloc([128, 1], F32, "wq8")
        wfs = A.alloc([128, 8, 16], F32, "wfs")
        wfb = A.alloc([128, 8, 16], BF16, "wfb")
        wqs = [A.alloc([128, 8, 128], F32, f"wqs{i}") for i in range(3)]
        wqb = [A.alloc([128, 8, 128], BF16, f"wqb{i}") for i in range(3)]
        qA = A.alloc([128, S], BF16, "qA")
        qB = A.alloc([128, S], BF16, "qB")
        kA = A.alloc([128, S], BF16, "kA")
        kB = A.alloc([128, S], BF16, "kB")
        vA = A.alloc([128, 32, 128], BF16, "vA")
        vB = A.alloc([128, 32, 128], BF16, "vB")
        sq2 = [A.alloc([128, 512], BF16, f"sq2{i}") for i in range(3)]
        lnq = [A.alloc([128, 512], F32, f"lnq{i}") for i in range(3)]
        rsq = [A.alloc([128, 512], F32, f"rsq{i}") for i in range(3)]
        NPT = 4
        PT = [A.alloc([128, 512], BF16, f"PT{i}") for i in range(NPT)]
        rr = [A.alloc([128, 512], F32, f"rr{i}") for i in range(2)]
        ost = [A.alloc([128, 512], BF16, f"ost{i}") for i in range(2)]

        for kc in range(8):
            P.dma(I_dma(hT_k[kc].ap, hT_v[:, kc, :]), reads=[B_hT_d], writes=[hT_k[kc].b], sbuf=hT_k[kc].b)
        hT_bufs = [t_.b for t_ in hT_k]

        P.op(DVE, I_ts(nfb.ap, pc("fb"), -1.0, None, ALU.mult), reads=[par.b], writes=[nfb.b])
        P.op(DVE, I_ts(wq8.ap, pc("wq"), 0.125, None, ALU.mult), reads=[par.b], writes=[wq8.b])
        for tl in (qA, qB, kA, kB):
            P.op(POOL, I_memset(tl.ap, 0.0), writes=[tl.b])
        P.op(POOL, I_memset(kA.ap[64:65, :], 1.0), writes=[kA.b])
        P.op(POOL, I_memset(kB.ap[0:1, :], 1.0), writes=[kB.b])
        P.op(POOL, I_memset(vA.ap, 1.0), writes=[vA.b])
        P.op(POOL, I_memset(vB.ap, 1.0), writes=[vB.b])

        P.dma(I_dma(wfs.ap, win_v[:, :, 5648:5664]), writes=[wfs.b], sbuf=wfs.b)
        P.op(DVE, I_tt(wfb.ap, wfs.ap, pc("gmix").unsqueeze(2).to_broadcast([128, 8, 16]), ALU.mult),
             reads=[wfs.b, par.b], writes=[wfb.b])
        for t in range(NT):
            bk = banks[t % 2]
            tsl = slice(t * 512, (t + 1) * 512)
            for kc in range(8):
                P.op(PE, I_mm(bk.ap[0:16, :], wfb.ap[:, kc, :], hT_k[kc].ap[:, tsl], start=(kc == 0), stop=(kc == 7)),
                     reads=[wfb.b, hT_k[kc].b], writes=[bk.b])
            P.op(ACT, I_act(ftmp.ap[:, tsl], bk.ap[0:16, :], AF.Exp, bias=nfb.ap[0:16, :], scale=-1.0),
                 reads=[bk.b, nfb.b], writes=[ftmp.b])
            P.op(ACT, I_act(ftmp.ap[:, tsl], ftmp.ap[:, tsl], AF.Ln, bias=1.0, scale=1.0), reads=[ftmp.b], writes=[ftmp.b])
            init = 0.0 if t == 0 else cumT.ap[:, t * 512 - 1:t * 512]
            P.op(DVE, I_scan(cumT.ap[:, tsl], ones16.ap[0:16, :], ftmp.ap[:, tsl], init, ALU.mult, ALU.subtract),
                 reads=[ones16.b, ftmp.b, cumT.b], writes=[cumT.b])
        P.op(POOL, I_copy(cumbf.ap, cumT.ap), reads=[cumT.b], writes=[cumbf.b])
        pN = banks[2]
        for b in range(32):
            P.op(PE, I_tr(pN.ap[:, b * 16:(b + 1) * 16], cumT.ap[:, b * 128:(b + 1) * 128], identf[0:16, 0:16]),
                 reads=[cumT.b, cst.b], writes=[pN.b])
        P.op(DVE, I_ts(ncum.ap.rearrange("p b h -> p (b h)"), pN.ap, -1.0, None, ALU.mult), reads=[pN.b], writes=[ncum.b])

        psQ2 = [(banks[0], banks[1]), (banks[2], banks[3]), (banks[6], banks[7])]
        psVs = [banks[4], banks[5]]
        psS = [banks[2], banks[3], banks[4]]
        psO = [banks[5], banks[6], banks[7]]
        ocnt = [0]
        qcnt = [0]

        for hp in range(8):
            def w_cols(p_):
                return [2576 + p_ * 128, 3600 + p_ * 128, 4624 + p_ * 128]
            if hp == 0:
                for i in range(3):
                    P.dma(I_dma(wqs[i].ap, win_v[:, :, w_cols(0)[i]:w_cols(0)[i] + 128]), writes=[wqs[i].b], sbuf=wqs[i].b)
            for i in range(3):
                P.op(POOL if i == 1 else DVE, I_tt(wqb[i].ap, wqs[i].ap, pc("gmix").unsqueeze(2).to_broadcast([128, 8, 128]), ALU.mult),
                     reads=[wqs[i].b, par.b], writes=[wqb[i].b])
            if hp + 1 < 8:
                for i in range(3):
                    c_ = w_cols(hp + 1)[i]
                    P.dma(I_dma(wqs[i].ap, win_v[:, :, c_:c_ + 128]), writes=[wqs[i].b], sbuf=wqs[i].b)
            P.dma(I_dma(qA.ap[64:65, :], cumbf.ap[2 * hp:2 * hp + 1, :]), reads=[cumbf.b], writes=[qA.b], sbuf=qA.b)
            P.dma(I_dma(qB.ap[0:1, :], cumbf.ap[2 * hp + 1:2 * hp + 2, :]), reads=[cumbf.b], writes=[qB.b], sbuf=qB.b)
            qk_items = [(which, t) for which in range(2) for t in range(NT)]

            def qk_A(idx):
                which, t = qk_items[idx]
                bk, _ = psQ2[idx % 3]
                tsl = slice(t * 512, (t + 1) * 512)
                for kc in range(8):
                    P.op(PE, I_mm(bk.ap, wqb[which].ap[:, kc, :], hT_k[kc].ap[:, tsl], start=(kc == 0), stop=(kc == 7)),
                         reads=[wqb[which].b, hT_k[kc].b], writes=[bk.b])

            def qk_B(idx):
                which, t = qk_items[idx]
                XA, XB = (qA, qB) if which == 0 else (kA, kB)
                wcol = wq8.ap if which == 0 else pc("wk")
                wcol_b = wq8.b if which == 0 else par.b
                bk, bk2 = psQ2[idx % 3]
                tsl = slice(t * 512, (t + 1) * 512)
                sq = sq2[idx % 3]
                ln_ = lnq[idx % 3]
                rs_ = rsq[idx % 3]
                P.op(ACT, I_act(sq.ap, bk.ap, AF.Square), reads=[bk.b], writes=[sq.b])
                P.op(PE, I_mm(bk2.ap, bonesb.ap, sq.ap), reads=[bonesb.b, sq.b], writes=[bk2.b])
                P.op(ACT, I_act(ln_.ap, bk2.ap, AF.Ln, bias=EPS, scale=1.0 / 64), reads=[bk2.b], writes=[ln_.b])
                P.op(ACT, I_act(rs_.ap, ln_.ap, AF.Exp, scale=-0.5), reads=[ln_.b], writes=[rs_.b])
                P.op(DVE, I_stt(XA.ap[0:64, tsl], bk.ap[0:64, :], wcol[0:64, :], rs_.ap[0:64, :], ALU.mult, ALU.mult),
                     reads=[bk.b, wcol_b, rs_.b], writes=[XA.b])
                P.op(DVE, I_stt(XB.ap[64:128, tsl], bk.ap[64:128, :], wcol[64:128, :], rs_.ap[64:128, :], ALU.mult, ALU.mult),
                     reads=[bk.b, wcol_b, rs_.b], writes=[XB.b])

            qk_A(0)
            for idx in range(len(qk_items)):
                if idx + 1 < len(qk_items):
                    qk_A(idx + 1)
                qk_B(idx)
            for bq in range(8):
                psV = psVs[bq % 2]
                pvv = psV.ap.rearrange("p (j c) -> p j c", j=4)
                for j in range(4):
                    b = 4 * bq + j
                    for kc in range(8):
                        P.op(PE, I_mm(psV.ap[:, j * 128:(j + 1) * 128], hT_k[kc].ap[:, b * 128:(b + 1) * 128], wqb[2].ap[:, kc, :],
                                      start=(kc == 0), stop=(kc == 7)), reads=[hT_k[kc].b, wqb[2].b], writes=[psV.b])
                P.op(ACT, I_act(vA.ap[:, 4 * bq:4 * bq + 4, 0:64], pvv[:, :, 0:64], AF.Copy), reads=[psV.b], writes=[vA.b])
                P.op(DVE, I_copy(vB.ap[:, 4 * bq:4 * bq + 4, 64:128], pvv[:, :, 64:128]), reads=[psV.b], writes=[vB.b])
            seq = []
            for i in range(NT):
                for hd in range(2):
                    nj = 4 * i + 4
                    for j in range(nj):
                        seq.append((i, hd, j, nj))

            def emit_S(n):
                i, hd, j, nj = seq[n]
                r = j - 4 * i
                c0 = max(0, r) * 128
                N = 512 - c0
                Kf = kA if hd == 0 else kB
                Qf = qA if hd == 0 else qB
                bk = psS[n % 3]
                P.op(PE, I_mm(bk.ap[:, 0:N], Kf.ap[:, j * 128:(j + 1) * 128], Qf.ap[:, i * 512 + c0:(i + 1) * 512]),
                     reads=[Kf.b, Qf.b], writes=[bk.b])

            def emit_PV(n):
                i, hd, j, nj = seq[n]
                r = j - 4 * i
                c0 = max(0, r) * 128
                N = 512 - c0
                h = 2 * hp + hd
                bk = psS[n % 3]
                pt = PT[n % NPT]
                if j == 0:
                    ocnt[0] += 1
                ob = psO[ocnt[0] % 3]
                P.op(ACT, I_act(pt.ap[:, 0:N], bk.ap[:, 0:N], AF.Exp, bias=ncum.ap[:, j, h:h + 1], scale=1.0),
                     reads=[bk.b, ncum.b], writes=[pt.b])
                if r >= 0:
                    P.op(POOL, I_tt(pt.ap[:, 0:128], pt.ap[:, 0:128], triub.ap, ALU.mult), reads=[pt.b, triub.b], writes=[pt.b])
                V = vA if hd == 0 else vB
                P.op(PE, I_mm(ob.ap[:, c0:512], V.ap[:, j, :], pt.ap[:, 0:N], start=(j == 0), stop=(j == nj - 1)),
                     reads=[V.b, pt.b], writes=[ob.b])
                if j == nj - 1:
                    os_ = ost[i % 2]
                    rt = rr[hd]
                    if hd == 0:
                        P.op(DVE, I_recip(rt.ap[64:128, :], ob.ap[64:128, :]), reads=[ob.b], writes=[rt.b])
                        P.op(DVE, I_tt(os_.ap[0:64, :], ob.ap[0:64, :], rt.ap[64:128, :], ALU.mult), reads=[ob.b, rt.b], writes=[os_.b])
                    else:
                        P.op(DVE, I_recip(rt.ap[0:64, :], ob.ap[0:64, :]), reads=[ob.b], writes=[rt.b])
                        P.op(DVE, I_tt(os_.ap[64:128, :], ob.ap[64:128, :], rt.ap[0:64, :], ALU.mult), reads=[ob.b, rt.b], writes=[os_.b])
                        P.dma(I_dma(mixT_v[:, 8 + hp, i * 512:(i + 1) * 512], os_.ap), reads=[os_.b], writes=[B_mixT_d], sbuf=os_.b)

            LOOK = 3
            for n in range(min(LOOK, len(seq))):
                emit_S(n)
            for n in range(len(seq)):
                emit_PV(n)
                if n + LOOK < len(seq):
                    emit_S(n + LOOK)

        if UPTO < 4:
            raise StopBuild()
        P.barrier()
        A.off = gbase
        wout = A.alloc([128, 16, D], BF16, "wout")
        wst3 = [A.alloc([128, 1024], F32, f"wst3{i}") for i in range(2)]
        mixt = [A.alloc([128, 16, 512], BF16, f"mixt{i}") for i in range(2)]
        xg3 = [A.alloc([128, 4, D], F32, f"xg3{i}") for i in range(2)]
        xn3 = [A.alloc([128, 4, D], BF16, f"xn3{i}") for i in range(2)]
        hst3 = [A.alloc([128, 8, 512], BF16, f"hst3{i}") for i in range(2)]
        junk3 = A.alloc([128, D], BF16, "junk3")
        ss3 = [A.alloc([128, 4], F32, f"ss3{i}") for i in range(2)]
        lnv3 = [A.alloc([128, 4], F32, f"lnv3{i}") for i in range(2)]
        rstd3 = [A.alloc([128, 4], F32, f"rstd3{i}") for i in range(2)]
        cnt3 = [0]
        load_weight_rows(wout, wout_v, 16, 0, D, lambda k: (pc("gssd", k) if k < 8 else None), wst3, cnt3)

        def p3_load(t):
            s = t % 2
            P.dma(I_dma(mixt[s].ap, mixT_v[:, :, t * 512:(t + 1) * 512]), reads=[B_mixT_d], writes=[mixt[s].b], sbuf=mixt[s].b)
            P.dma(I_dma(xg3[s].ap, x_v[t]), writes=[xg3[s].b], sbuf=xg3[s].b)

        def p3_a(t):
            s = t % 2
            n = 0
            for b in range(4):
                for half in range(2):
                    bk = banks[4 + (n % 4)]
                    n += 1
                    for fc in range(16):
                        P.op(PE, I_mm(bk.ap, mixt[s].ap[:, fc, b * 128:(b + 1) * 128], wout.ap[:, fc, half * 512:(half + 1) * 512],
                                      start=(fc == 0), stop=(fc == 15)), reads=[mixt[s].b, wout.b], writes=[bk.b])
                    xs_ = xg3[s].ap[:, b, half * 512:(half + 1) * 512]
                    P.op(DVE, I_tt(xs_, xs_, bk.ap, ALU.add), reads=[xg3[s].b, bk.b], writes=[xg3[s].b])
            P.dma(I_dma(x1_v[t], xg3[s].ap), reads=[xg3[s].b], writes=[B_x1_d], sbuf=xg3[s].b, eng=POOL)
            rms_stage_a(xg3[s], ss3[s], lnv3[s], rstd3[s], junk3)

        def p3_b(t):
            s = t % 2
            norm_transpose(xg3[s], rstd3[s], xn3[s], banks[0:4], hst3[s], t)
            P.dma(I_dma(h2T_v[:, :, t * 512:(t + 1) * 512], hst3[s].ap), reads=[hst3[s].b], writes=[B_h2T_d], sbuf=hst3[s].b, eng=POOL)

        p3_load(0)
        p3_load(1)
        p3_a(0)
        for t in range(NT):
            if t + 1 < NT:
                p3_a(t + 1)
            p3_b(t)
            if t + 2 < NT:
                p3_load(t + 2)

        if UPTO < 5:
            raise StopBuild()
        P.barrier()
        A.off = gbase
        wup = A.alloc([128, 8, 2 * DFF], BF16, "wup")
        wdn = A.alloc([128, 22, D], BF16, "wdn")
        wst4 = [A.alloc([128, 512], F32, f"wst4{i}") for i in range(2)]
        cwf = A.alloc([128, 132], F32, "cwf")
        cbf = A.alloc([128, 44], F32, "cbf")
        halo = A.alloc([128, 44, 2], F32, "halo")
        h2t = [A.alloc([128, 8, 512], BF16, "h2t0")]
        x1t = A.alloc([128, 2, D], F32, "x1t")
        gpad = [A.alloc([128, 514], F32, f"gpad{i}") for i in range(2)]
        vpad = [A.alloc([128, 514], F32, f"vpad{i}") for i in range(2)]
        gpadh = [Buf(f"gpadh{i}") for i in range(2)]
        vpadh = [Buf(f"vpadh{i}") for i in range(2)]
        gacc = [A.alloc([128, 512], F32, f"gacc{i}") for i in range(3)]
        vacc = [A.alloc([128, 512], F32, f"vacc{i}") for i in range(3)]
        th4 = [A.alloc([128, 512], F32, f"th4{i}") for i in range(2)]
        gT = A.alloc([128, 22, 512], BF16, "gT")
        cnt4 = [0]
        wup_piece = {}
        wunits = [("up", k, ci) for ci in (0, 5, 6, 1, 7, 2, 8, 3, 9, 4, 10) for k in range(8)]
        wunits += [("dn", k, cc) for k in range(22) for cc in (0, 512)]
        upos = [0]

        def emit_units(n):
            for _ in range(n):
                if upos[0] >= len(wunits):
                    return
                kind, k, x_ = wunits[upos[0]]
                upos[0] += 1
                st = wst4[cnt4[0] % len(wst4)]
                eng = DVE if cnt4[0] % 2 == 0 else ACT
                cnt4[0] += 1
                if kind == "up":
                    cc = x_ * 512
                    pb = Buf(f"wup_{k}_{x_}")
                    wup_piece[(k, x_)] = pb
                    P.dma(I_dma(st.ap[:, 0:512], wup_v[:, k, cc:cc + 512]), writes=[st.b], sbuf=st.b)
                    if eng == ACT:
                        P.op(ACT, I_act(wup.ap[:, k, cc:cc + 512], st.ap[:, 0:512], AF.Identity, scale=pc("gffn", k)),
                             reads=[st.b, par.b], writes=[pb])
                    else:
                        P.op(DVE, I_ts(wup.ap[:, k, cc:cc + 512], st.ap[:, 0:512], pc("gffn", k), 1.0, ALU.mult, ALU.mult),
                             reads=[st.b, par.b], writes=[pb])
                else:
                    cc = x_
                    P.dma(I_dma(st.ap[:, 0:512], wdn_v[:, k, cc:cc + 512]), writes=[st.b], sbuf=st.b)
                    if eng == ACT:
                        P.op(ACT, I_act(wdn.ap[:, k, cc:cc + 512], st.ap[:, 0:512], AF.Copy), reads=[st.b], writes=[wdn.b])
                    else:
                        P.op(DVE, I_copy(wdn.ap[:, k, cc:cc + 512], st.ap[:, 0:512]), reads=[st.b], writes=[wdn.b])

        emit_units(24)
        P.op(DVE, I_ts(cwf.ap[:, 0:66], pc("cwf", 0, 66), 0.5, None, ALU.mult), reads=[par.b], writes=[cwf.b])
        P.op(DVE, I_copy(cwf.ap[:, 66:132], pc("cwf", 66, 132)), reads=[par.b], writes=[cwf.b])
        P.op(DVE, I_ts(cbf.ap[:, 0:22], pc("cbf", 0, 22), 0.5, None, ALU.mult), reads=[par.b], writes=[cbf.b])
        P.op(DVE, I_copy(cbf.ap[:, 22:44], pc("cbf", 22, 44)), reads=[par.b], writes=[cbf.b])
        P.op(POOL, I_memset(halo.ap, 0.0), writes=[halo.b])

        def p4_load(t):
            s = 0
            P.dma(I_dma(h2t[s].ap, h2T_v[:, :, t * 512:(t + 1) * 512]), reads=[B_h2T_d], writes=[h2t[s].b], sbuf=h2t[s].b)

        def conv3(pad, acc, blk, first_eng):
            P.op(first_eng, I_ts(acc.ap, pad.ap[:, 0:512], cwf.ap[:, blk * 3:blk * 3 + 1], cbf.ap[:, blk:blk + 1], ALU.mult, ALU.add),
                 reads=[pad.b, cwf.b, cbf.b], writes=[acc.b])
            for k in range(1, 3):
                P.op(DVE, I_stt(acc.ap, pad.ap[:, k:k + 512], cwf.ap[:, blk * 3 + k:blk * 3 + k + 1], acc.ap, ALU.mult, ALU.add),
                     reads=[pad.b, cwf.b, acc.b], writes=[acc.b])

        p4_load(0)
        gbanks = [banks[0], banks[1]]
        vbanks = [banks[2], banks[3]]
        dbanks = [banks[4], banks[5], banks[6], banks[7]]
        for t in range(NT):
            hs = h2t[0]
            if t > 0:
                p4_load(t)
            def ffn_A(c):
                s = c % 2
                bg = gbanks[s]
                bv = vbanks[s]
                for kc in range(8):
                    P.op(PE, I_mm(bg.ap, wup.ap[:, kc, c * 128:(c + 1) * 128], hs.ap[:, kc, :], start=(kc == 0), stop=(kc == 7)),
                         reads=[wup_piece[(kc, (c * 128) // 512)], hs.b], writes=[bg.b])
                for kc in range(8):
                    P.op(PE, I_mm(bv.ap, wup.ap[:, kc, DFF + c * 128:DFF + (c + 1) * 128], hs.ap[:, kc, :], start=(kc == 0), stop=(kc == 7)),
                         reads=[wup_piece[(kc, (DFF + c * 128) // 512)], hs.b], writes=[bv.b])
                for (pad, padh, acc, bk, blk) in ((gpad[s], gpadh[s], gacc[c % 3], bg, c), (vpad[s], vpadh[s], vacc[c % 3], bv, 22 + c)):
                    P.op(POOL, I_copy(pad.ap[:, 0:2], halo.ap[:, blk, :]), reads=[halo.b], writes=[padh])
                    P.op(ACT, I_act(acc.ap, bk.ap, AF.Identity, bias=pc("cbf", blk), scale=pc("cwf", blk * 3 + 2)),
                         reads=[bk.b, par.b], writes=[acc.b])
                    P.op(ACT, I_act(pad.ap[:, 2:514], bk.ap, AF.Copy), reads=[bk.b], writes=[pad.b])
                    P.op(POOL, I_copy(halo.ap[:, blk, :], pad.ap[:, 512:514]), reads=[pad.b], writes=[halo.b])

            def ffn_B(c):
                s = c % 2
                for (pad, padh, acc, blk) in ((gpad[s], gpadh[s], gacc[c % 3], c), (vpad[s], vpadh[s], vacc[c % 3], 22 + c)):
                    for k in range(2):
                        P.op(DVE, I_stt(acc.ap, pad.ap[:, k:k + 512], pc("cwf", blk * 3 + k), acc.ap, ALU.mult, ALU.add),
                             reads=[pad.b, padh, par.b, acc.b], writes=[acc.b])

            def ffn_C(c):
                s = c % 2
                P.op(ACT, I_act(th4[s].ap, gacc[c % 3].ap, AF.Silu), reads=[gacc[c % 3].b], writes=[th4[s].b])
                P.op(POOL, I_tt(gT.ap[:, c, :], th4[s].ap, vacc[c % 3].ap, ALU.mult), reads=[th4[s].b, vacc[c % 3].b], writes=[gT.b])

            for step in range(22 + 2):
                if t == 0:
                    emit_units(5)
                if step < 22:
                    ffn_A(step)
                if 0 <= step - 1 < 22:
                    ffn_B(step - 1)
                if 0 <= step - 2 < 22:
                    ffn_C(step - 2)
            if t == 0:
                emit_units(len(wunits))
            n = 0
            for hb in range(2):
                P.dma(I_dma(x1t.ap, x1_v[t][:, 2 * hb:2 * hb + 2, :]), reads=[B_x1_d], writes=[x1t.b], sbuf=x1t.b)
                for bb in range(2):
                    b = 2 * hb + bb
                    for half in range(2):
                        bk = dbanks[n % 4]
                        n += 1
                        for c in range(22):
                            P.op(PE, I_mm(bk.ap, gT.ap[:, c, b * 128:(b + 1) * 128], wdn.ap[:, c, half * 512:(half + 1) * 512],
                                          start=(c == 0), stop=(c == 21)), reads=[gT.b, wdn.b], writes=[bk.b])
                        xs_ = x1t.ap[:, bb, half * 512:(half + 1) * 512]
                        P.op(DVE, I_tt(xs_, xs_, bk.ap, ALU.add), reads=[x1t.b, bk.b], writes=[x1t.b])
                P.dma(I_dma(out_v[t][:, 2 * hb:2 * hb + 2, :], x1t.ap), reads=[x1t.b], writes=[B_out_d], sbuf=x1t.b, eng=POOL)


    except StopBuild:
        pass
    finals = [B_out_d]
    if DEBUG:
        finals += [B_hT_d, B_mixT_d, B_x1_d, B_h2T_d]
    nsem = P.emit(nc, final_wait_bufs=finals)
    return nc, nsem, len(P.ops)


def _pack_params(norm_mix_w, ssd_conv_w, ssd_conv_b, ssd_dt_bias, ssd_a_log, ssd_d, ssd_norm_w,
                 fox_f_bias, fox_q_norm_w, fox_k_norm_w, norm_ffn_w, ffn_conv_w, ffn_conv_b):
    par = np.zeros((128, NPAR), np.float32)

    def put(name, arr):
        lo, hi = PCOL[name]
        par[:, lo:hi] = arr

    put("gmix", norm_mix_w[0].reshape(8, 128).T)
    put("gffn", norm_ffn_w[0].reshape(8, 128).T)
    put("gssd", ssd_norm_w[0].reshape(8, 128).T)
    put("cws", ssd_conv_w[0].reshape(4, 12, 128).transpose(2, 1, 0).reshape(128, 48))
    put("cbs", ssd_conv_b[0].reshape(12, 128).T)
    put("dcol", np.repeat(ssd_d[0].reshape(8, 2), 64, axis=1).T)
    put("wq", np.tile(fox_q_norm_w[0], 2)[:, None])
    put("wk", np.tile(fox_k_norm_w[0], 2)[:, None])
    fb = np.zeros((128, 1), np.float32)
    fb[0:16, 0] = fox_f_bias[0]
    put("fb", fb)
    put("dtb", np.tile(ssd_dt_bias[0][None, :], (128, 1)))
    put("alog", np.tile(ssd_a_log[0][None, :], (128, 1)))
    put("cwf", ffn_conv_w[0].reshape(3, 44, 128).transpose(2, 1, 0).reshape(128, 132))
    put("cbf", ffn_conv_b[0].reshape(44, 128).T)
    return par


def _rows_to_pk(w, nk):
    c = w.shape[1]
    return np.ascontiguousarray(w.reshape(nk, 128, c).transpose(1, 0, 2).reshape(128, nk * c))


_CACHE = {}


def kernel(x, norm_mix_w, w_in, ssd_conv_w, ssd_conv_b, ssd_dt_bias, ssd_a_log, ssd_d,
           ssd_norm_w, fox_f_bias, fox_q_norm_w, fox_k_norm_w, w_out, norm_ffn_w,
           w_up, ffn_conv_w, ffn_conv_b, w_down):
    f = lambda a: np.asarray(a, dtype=np.float32)
    x = f(x)
    par = _pack_params(f(norm_mix_w), f(ssd_conv_w), f(ssd_conv_b), f(ssd_dt_bias), f(ssd_a_log), f(ssd_d),
                       f(ssd_norm_w), f(fox_f_bias), f(fox_q_norm_w), f(fox_k_norm_w), f(norm_ffn_w),
                       f(ffn_conv_w), f(ffn_conv_b))
    cst = np.concatenate([np.eye(128, dtype=np.float32), np.triu(np.ones((128, 128), np.float32)),
                          np.kron(np.eye(2, dtype=np.float32), np.ones((64, 64), np.float32))], axis=1)
    win = _rows_to_pk(f(w_in)[0], 8)
    wout = _rows_to_pk(f(w_out)[0], 16)
    wup = _rows_to_pk(f(w_up)[0], 8)
    wdn = _rows_to_pk(f(w_down)[0], 22)
    if "nc" not in _CACHE:
        _CACHE["nc"] = build_program()
    nc, nsem, nops = _CACHE["nc"]
    n = x.shape[0]
    in_maps = []
    for c in range(n):
        in_maps.append({"x": np.ascontiguousarray(x[c]), "win": win, "wout": wout, "wup": wup, "wdn": wdn,
                        "par": par, "cst": cst})
    res = run_bass_kernel_spmd(nc, in_maps, core_ids=list(range(n)))
    if DEBUG:
        _CACHE["dbg"] = res.results
    return np.stack([r["out"] for r in res.results], axis=0).astype(np.float32)
```

```python
import contextlib
import numpy as np
import concourse.bass as bass
import concourse.mybir as mybir
from concourse.bass_utils import run_bass_kernel_spmd

F32 = mybir.dt.float32
BF16 = mybir.dt.bfloat16
AF = mybir.ActivationFunctionType
ALU = mybir.AluOpType

PE, ACT, DVE, POOL, SP = "tensor", "scalar", "vector", "gpsimd", "sync"
EPOCH = 12000

S = 4096
D = 1024
NT = 8
IN_COLS = 5664
DFF = 2816
EPS = 1e-6
DEBUG = False
UPTO = 5
LIMIT = None


class StopBuild(Exception):
    pass


class Buf:
    def __init__(self, name, multi=False):
        self.name = name
        self.multi = multi
        self.psum = False
        self.writers = []
        self.readers = []


class Op:
    __slots__ = ("eng", "fn", "reads", "writes", "dma", "deps", "need", "tok", "dbuf", "idx")

    def __init__(self, eng, fn, reads, writes, dma, dbuf):
        self.eng = eng
        self.fn = fn
        self.reads = reads
        self.writes = writes
        self.dma = dma
        self.dbuf = dbuf
        self.deps = set()
        self.need = False
        self.tok = None


class Prog:
    def __init__(self):
        self.ops = []

    def op(self, eng, fn, reads=(), writes=(), dma=False, dbuf=None):
        o = Op(eng, fn, tuple(reads), tuple(writes), dma, dbuf)
        o.idx = len(self.ops)
        self.ops.append(o)
        return o

    def dma(self, fn, reads=(), writes=(), sbuf=None, eng=SP):
        return self.op(eng, fn, reads, writes, dma=True, dbuf=sbuf)

    def barrier(self):
        last = {}
        last_dma = {}
        for o in self.ops:
            if o.dma:
                last_dma[id(o.dbuf)] = o.idx
            elif o.fn is not None:
                last[o.eng] = o.idx
        deps = set(last.values()) | set(last_dma.values())
        for eng in (SP, PE, ACT, DVE, POOL):
            j = Op(eng, None, (), (), False, None)
            j.deps = set(deps)
            j.idx = len(self.ops)
            self.ops.append(j)

    def analyze(self):
        last_dma_on = {}
        for o in self.ops:
            deps = o.deps
            for b in o.reads:
                deps.update(b.writers)
                if b.psum:
                    for r in b.readers:
                        if self.ops[r].eng != o.eng:
                            deps.add(r)
            for b in o.writes:
                if not b.multi:
                    deps.update(b.writers)
                deps.update(b.readers)
            if o.dma:
                p = last_dma_on.get(id(o.dbuf))
                if p is not None:
                    deps.add(p)
                last_dma_on[id(o.dbuf)] = o.idx
            wset = set(id(b) for b in o.writes)
            for b in o.reads:
                if id(b) not in wset:
                    b.readers.append(o.idx)
            for b in o.writes:
                if b.multi:
                    b.writers.append(o.idx)
                else:
                    b.writers = [o.idx]
                b.readers = []
            deps.discard(o.idx)
            if o.eng == PE and not o.dma:
                for d in list(deps):
                    od = self.ops[d]
                    if od.eng == PE and not od.dma:
                        deps.discard(d)
            for d in deps:
                self.ops[d].need = True

    def emit(self, nc, final_wait_bufs=()):
        if LIMIT is not None:
            self.ops = self.ops[:LIMIT]
        self.analyze()
        join = Op(SP, None, (), (), False, None)
        join.idx = len(self.ops)
        for b in final_wait_bufs:
            for w in b.writers:
                join.deps.add(w)
                self.ops[w].need = True
        self.ops.append(join)

        eng_count = {PE: 0, ACT: 0, DVE: 0, POOL: 0, SP: 0}
        dma_slots = {}
        eng_sems = {}
        for o in self.ops:
            if o.dma:
                key = id(o.dbuf)
                k = dma_slots.get(key, 0) + 1
                dma_slots[key] = k
                o.tok = ("d", key, 16 * k)
            elif o.need and o.fn is not None:
                n = eng_count[o.eng]
                eng_count[o.eng] = n + 1
                o.tok = ("e", (o.eng, n // EPOCH), (n % EPOCH) + 1)
                eng_sems[(o.eng, n // EPOCH)] = True
        sem_keys = list(eng_sems.keys()) + [("dma", k) for k in dma_slots.keys()]
        es = contextlib.ExitStack()
        with es:
            sems = {}
            for i, k in enumerate(sem_keys):
                sems[k] = es.enter_context(nc.semaphore(f"s{i}"))
            block = es.enter_context(nc.Block())
            ops = self.ops

            def run_engine(engname):
                def body(e):
                    waited = {}
                    for o in ops:
                        if o.eng != engname:
                            continue
                        need = {}
                        for d in o.deps:
                            t = ops[d].tok
                            if t is None:
                                continue
                            k = (t[0], t[1])
                            if need.get(k, 0) < t[2]:
                                need[k] = t[2]
                        for k, v in need.items():
                            if waited.get(k, 0) >= v:
                                continue
                            waited[k] = v
                            s = sems[("dma", k[1])] if k[0] == "d" else sems[k[1]]
                            e.wait_ge(s, v)
                        if o.fn is None:
                            continue
                        ins = o.fn(e)
                        if o.tok is not None:
                            if o.tok[0] == "d":
                                ins.then_inc(sems[("dma", o.tok[1])], 16)
                            else:
                                ins.then_inc(sems[o.tok[1]], 1)
                return body

            block.sync(run_engine(SP))
            block.tensor(run_engine(PE))
            block.scalar(run_engine(ACT))
            block.vector(run_engine(DVE))
            block.gpsimd(run_engine(POOL))
        return len(sem_keys)


def I_mm(out, lhsT, rhs, start=True, stop=True):
    return lambda e: e.matmul(out, lhsT, rhs, start=start, stop=stop)


def I_tr(out, in_, ident):
    return lambda e: e.transpose(out, in_, ident)


def I_act(out, in_, func, bias=None, scale=None, accum=None):
    kw = {}
    if bias is not None:
        kw["bias"] = bias
    if scale is not None:
        kw["scale"] = scale
    if accum is not None:
        kw["accum_out"] = accum
    return lambda e: e.activation(out, in_, func, **kw)


def I_ts(out, in0, s1, s2, op0, op1=None):
    if op1 is None:
        return lambda e: e.tensor_scalar(out, in0, s1, None, op0)
    return lambda e: e.tensor_scalar(out, in0, s1, s2, op0, op1)


def I_tt(out, in0, in1, op):
    return lambda e: e.tensor_tensor(out, in0, in1, op)


def I_stt(out, in0, scalar, in1, op0, op1):
    return lambda e: e.scalar_tensor_tensor(out, in0, scalar, in1, op0, op1)


def I_copy(out, in_):
    return lambda e: e.tensor_copy(out, in_)


def I_recip(out, in_):
    return lambda e: e.reciprocal(out, in_)


def I_memset(ap, v):
    return lambda e: e.memset(ap, v)


def I_dma(out, in_):
    return lambda e: e.dma_start(out=out, in_=in_)


def I_scan(out, d0, d1, init, op0, op1):
    return lambda e: e.tensor_tensor_scan(out, d0, d1, init, op0, op1)


class T:
    def __init__(self, ap, name):
        self.ap = ap
        self.b = Buf(name)


class Arena:
    def __init__(self, ap, cap):
        self.ap = ap
        self.cap = cap
        self.off = 0

    def alloc(self, shape, dt, name):
        esz = 2 if dt == BF16 else 4
        n = int(np.prod(shape[1:]))
        nb = (n * esz + 63) // 64 * 64
        off = self.off
        self.off += nb
        assert self.off <= self.cap, (name, self.off, self.cap)
        self.peak = max(getattr(self, 'peak', 0), self.off)
        v = self.ap[:, off // 4:(off + nb) // 4]
        if dt == BF16:
            v = v.bitcast(BF16)
        v = v[:, 0:n]
        if len(shape) == 3:
            v = v.rearrange("p (a b) -> p a b", a=shape[1])
        elif len(shape) == 4:
            v = v.rearrange("p (a b c) -> p a b c", a=shape[1], b=shape[2])
        if shape[0] != 128:
            v = v[0:shape[0]]
        return T(v, name)


PCOL = {}
_o = 0
for _n, _w in (("gmix", 8), ("gffn", 8), ("gssd", 8), ("cws", 48), ("cbs", 12), ("dcol", 8),
               ("wq", 1), ("wk", 1), ("fb", 1), ("dtb", 16), ("alog", 16), ("cwf", 132), ("cbf", 44)):
    PCOL[_n] = (_o, _o + _w)
    _o += _w
NPAR = _o


def build_program():
    nc = bass.Bass("TRN2", target_bir_lowering=False)
    x_d = nc.dram_tensor("x", [S, D], F32, kind="ExternalInput").ap()
    win_d = nc.dram_tensor("win", [128, 8 * IN_COLS], F32, kind="ExternalInput").ap()
    wout_d = nc.dram_tensor("wout", [128, 16 * D], F32, kind="ExternalInput").ap()
    wup_d = nc.dram_tensor("wup", [128, 8 * 2 * DFF], F32, kind="ExternalInput").ap()
    wdn_d = nc.dram_tensor("wdn", [128, 22 * D], F32, kind="ExternalInput").ap()
    par_d = nc.dram_tensor("par", [128, NPAR], F32, kind="ExternalInput").ap()
    cst_d = nc.dram_tensor("cst", [128, 384], F32, kind="ExternalInput").ap()
    out_d = nc.dram_tensor("out", [S, D], F32, kind="ExternalOutput").ap()
    def skind(nm):
        return "ExternalOutput" if (DEBUG and nm in DEBUG) else "Internal"
    hT_d = nc.dram_tensor("hT_s", [128, 8 * S], BF16, kind=skind("hT_s")).ap()
    mixT_d = nc.dram_tensor("mixT_s", [128, 16 * S], BF16, kind=skind("mixT_s")).ap()
    x1_d = nc.dram_tensor("x1_s", [S, D], F32, kind=skind("x1_s")).ap()
    h2T_d = nc.dram_tensor("h2T_s", [128, 8 * S], BF16, kind=skind("h2T_s")).ap()

    win_v = win_d.rearrange("p (k c) -> p k c", k=8)
    wout_v = wout_d.rearrange("p (k c) -> p k c", k=16)
    wup_v = wup_d.rearrange("p (k c) -> p k c", k=8)
    wdn_v = wdn_d.rearrange("p (k c) -> p k c", k=22)
    hT_v = hT_d.rearrange("p (k t) -> p k t", k=8)
    mixT_v = mixT_d.rearrange("p (k t) -> p k t", k=16)
    h2T_v = h2T_d.rearrange("p (k t) -> p k t", k=8)
    x_v = x_d.rearrange("(g b p) d -> g p b d", b=4, p=128)
    x1_v = x1_d.rearrange("(g b p) d -> g p b d", b=4, p=128)
    out_v = out_d.rearrange("(g b p) d -> g p b d", b=4, p=128)

    B_hT_d = Buf("hT_d", multi=True)
    B_mixT_d = Buf("mixT_d", multi=True)
    B_x1_d = Buf("x1_d", multi=True)
    B_h2T_d = Buf("h2T_d", multi=True)
    B_out_d = Buf("out_d", multi=True)

    CAP = 207 * 1024
    arena_t = nc.alloc_sbuf_tensor("arena", [128, CAP // 4], F32)
    A = Arena(arena_t[:, :], CAP)
    banks = []
    for i in range(8):
        pt = nc.alloc_psum_tensor(f"psb{i}", [128, 512], F32)
        banks.append(T(pt[:, :], f"psb{i}"))
        banks[-1].b.psum = True

    def bf_view(bank_ap, a):
        return bank_ap.bitcast(BF16).rearrange("p (a b) -> p a b", a=a)

    P = Prog()

    par = A.alloc([128, NPAR], F32, "par")
    cst = A.alloc([128, 384], F32, "cst")
    identb = A.alloc([128, 128], BF16, "identb")
    triub = A.alloc([128, 128], BF16, "triub")
    bonesb = A.alloc([128, 128], BF16, "bonesb")
    onesb = A.alloc([128, 128], BF16, "onesb")
    ones16 = A.alloc([128, 512], F32, "ones16")
    identf = cst.ap[:, 0:128]
    triuf = cst.ap[:, 128:256]

    def pc(name, a=None, b=None):
        lo, hi = PCOL[name]
        if a is None:
            return par.ap[:, lo:hi]
        return par.ap[:, lo + a:lo + (b if b is not None else a + 1)]

    P.dma(I_dma(par.ap, par_d), writes=[par.b], sbuf=par.b)
    P.dma(I_dma(cst.ap, cst_d), writes=[cst.b], sbuf=cst.b)
    P.op(DVE, I_copy(identb.ap, cst.ap[:, 0:128]), reads=[cst.b], writes=[identb.b])
    P.op(DVE, I_copy(triub.ap, cst.ap[:, 128:256]), reads=[cst.b], writes=[triub.b])
    P.op(DVE, I_copy(bonesb.ap, cst.ap[:, 256:384]), reads=[cst.b], writes=[bonesb.b])
    P.op(POOL, I_memset(onesb.ap, 1.0), writes=[onesb.b])
    P.op(POOL, I_memset(ones16.ap, 1.0), writes=[ones16.b])
    gbase = A.off

    def rms_stage_a(xg, ss, lnv, rstd, junk):
        for b in range(4):
            P.op(ACT, I_act(junk.ap, xg.ap[:, b, :], AF.Square, accum=ss.ap[:, b:b + 1]),
                 reads=[xg.b], writes=[junk.b, ss.b])
        P.op(ACT, I_act(lnv.ap, ss.ap, AF.Ln, bias=EPS, scale=1.0 / D), reads=[ss.b], writes=[lnv.b])
        P.op(ACT, I_act(rstd.ap, lnv.ap, AF.Exp, scale=-0.5), reads=[lnv.b], writes=[rstd.b])

    def norm_transpose(xg, rstd, xn, tbanks, hst, evac_toggle):
        for b in range(4):
            P.op(DVE, I_ts(xn.ap[:, b, :], xg.ap[:, b, :], rstd.ap[:, b:b + 1], None, ALU.mult),
                 reads=[xg.b, rstd.b], writes=[xn.b])
        for b in range(4):
            bk = tbanks[b % len(tbanks)]
            pv = bf_view(bk.ap, 8)
            for kc in range(8):
                P.op(PE, I_tr(pv[:, kc, :], xn.ap[:, b, kc * 128:(kc + 1) * 128], identb.ap),
                     reads=[xn.b, identb.b], writes=[bk.b])
            eng = ACT if (b + evac_toggle) % 2 == 0 else DVE
            if eng == ACT:
                P.op(ACT, I_act(hst.ap[:, :, b * 128:(b + 1) * 128], pv, AF.Copy), reads=[bk.b], writes=[hst.b])
            else:
                P.op(DVE, I_copy(hst.ap[:, :, b * 128:(b + 1) * 128], pv), reads=[bk.b], writes=[hst.b])

    def load_weight_rows(dst, src_v, nk, c0, c1, gcol_fn, stages, cnt, scale2=1.0, chunk=1024):
        for k in range(nk):
            for cc in range(c0, c1, chunk):
                ce = min(cc + chunk, c1)
                st = stages[cnt[0] % len(stages)]
                eng = DVE if cnt[0] % 2 == 0 else ACT
                cnt[0] += 1
                P.dma(I_dma(st.ap[:, 0:ce - cc], src_v[:, k, cc:ce]), writes=[st.b], sbuf=st.b)
                g = gcol_fn(k)
                d_ap = dst.ap[:, k, cc - c0:ce - c0]
                s_ap = st.ap[:, 0:ce - cc]
                if eng == ACT:
                    if g is None:
                        P.op(ACT, I_act(d_ap, s_ap, AF.Copy), reads=[st.b], writes=[dst.b])
                    else:
                        assert scale2 == 1.0
                        P.op(ACT, I_act(d_ap, s_ap, AF.Identity, scale=g), reads=[st.b, par.b], writes=[dst.b])
                elif g is None:
                    P.op(DVE, I_copy(d_ap, s_ap), reads=[st.b], writes=[dst.b])
                else:
                    P.op(DVE, I_ts(d_ap, s_ap, g, scale2, ALU.mult, ALU.mult), reads=[st.b, par.b], writes=[dst.b])

    try:
        A.off = gbase
        xgs = [A.alloc([128, 4, D], F32, f"xg{i}") for i in range(2)]
        xns = [A.alloc([128, 4, D], BF16, f"xn{i}") for i in range(2)]
        hsts = [A.alloc([128, 8, 512], BF16, f"hst{i}") for i in range(2)]
        junk = A.alloc([128, D], BF16, "junk")
        sss = [A.alloc([128, 4], F32, f"ss{i}") for i in range(2)]
        lnvs = [A.alloc([128, 4], F32, f"lnv{i}") for i in range(2)]
        rstds = [A.alloc([128, 4], F32, f"rstd{i}") for i in range(2)]

        def p1_a(g):
            s = g % 2
            P.dma(I_dma(xgs[s].ap, x_v[g]), writes=[xgs[s].b], sbuf=xgs[s].b)
            rms_stage_a(xgs[s], sss[s], lnvs[s], rstds[s], junk)

        def p1_b(g):
            s = g % 2
            norm_transpose(xgs[s], rstds[s], xns[s], banks[0:4], hsts[s], g)
            P.dma(I_dma(hT_v[:, :, g * 512:(g + 1) * 512], hsts[s].ap), reads=[hsts[s].b], writes=[B_hT_d], sbuf=hsts[s].b, eng=POOL)

        p1_a(0)
        for g in range(NT):
            if g + 1 < NT:
                p1_a(g + 1)
            p1_b(g)

        if UPTO < 2:
            raise StopBuild()
        P.barrier()
        A.off = gbase
        wssd = A.alloc([128, 8, 2560], BF16, "wssd")
        wdt = A.alloc([128, 8, 16], BF16, "wdt")
        wdts = A.alloc([128, 8, 16], F32, "wdts")
        wst = [A.alloc([128, 1024], F32, f"wst{i}") for i in range(4)]
        cwh = A.alloc([128, 48], F32, "cwh")
        cbh = A.alloc([128, 12], F32, "cbh")
        a_b = A.alloc([128, 16], F32, "a_b")
        hTt = [A.alloc([128, 8, 512], BF16, f"hTt{i}") for i in range(2)]
        szT = A.alloc([128, 8, 512], BF16, "szT")
        xpads = [A.alloc([128, 515], F32, f"xpad{i}") for i in range(12)]
        xcT = [A.alloc([128, 512], BF16, f"xcT{i}") for i in range(12)]
        cacc = [A.alloc([128, 512], F32, f"cacc{i}") for i in range(3)]
        thb = [A.alloc([128, 512], F32, f"thb{i}") for i in range(2)]
        dtx = A.alloc([128, 64], F32, "dtx")
        dt_t = A.alloc([128, 4, 16], F32, "dt_t")
        dtA_t = A.alloc([128, 4, 16], F32, "dtA_t")
        nacs = A.alloc([128, 16], F32, "nacs")
        dtmp = A.alloc([128, 16], F32, "dtmp")
        cdl = A.alloc([128, 16], F32, "cdl")
        dte = A.alloc([128, 16], F32, "dte")
        cds = [A.alloc([128, 16], F32, f"cd{i}") for i in range(2)]
        dhi = A.alloc([128, 4, 16], BF16, "dhi")
        dlo = A.alloc([128, 4, 16], BF16, "dlo")
        dres = A.alloc([128, 4, 16], F32, "dres")
        Dhi = A.alloc([128, 8, 128], BF16, "Dhi")
        Dlo = A.alloc([128, 8, 128], BF16, "Dlo")
        w2 = A.alloc([128, 16], F32, "w2")
        ET = A.alloc([128, 16, 128], BF16, "ET")
        ETp = A.alloc([128, 16, 128], F32, "ETp")
        EA = A.alloc([128, 16, 128], BF16, "EA")
        CBTm = A.alloc([128, 2, 128], F32, "CBTm")
        scT = A.alloc([128, 16, 128], BF16, "scT")
        xdt = A.alloc([128, 16, 64], BF16, "xdt")
        xdte = A.alloc([128, 16, 64], BF16, "xdte")
        Btok = A.alloc([128, 2, 128], BF16, "Btok")
        Csc = A.alloc([128, 16, 128], BF16, "Csc")
        Sst = A.alloc([128, 16, 64], F32, "Sst")
        Sbf = A.alloc([128, 16, 64], BF16, "Sbf")
        yt = A.alloc([128, 8, 128], F32, "yt")
        sqb = A.alloc([128, 8, 128], BF16, "sqb")
        lnr = A.alloc([128, 2, 128], F32, "lnr")
        rsr = A.alloc([128, 2, 128], F32, "rsr")
        ystage = A.alloc([128, 8, 512], BF16, "ystage")

        cnt = [0]
        load_weight_rows(wssd, win_v, 8, 0, 1024, lambda k: pc("gmix", k), wst, cnt, scale2=1.0)
        wx = T(wssd.ap[:, :, 1024:2560], "wssd_x")
        wx.b = wssd.b
        load_weight_rows(wx, win_v, 8, 1024, 2560, lambda k: pc("gmix", k), wst, cnt, scale2=1.0, chunk=768)
        P.dma(I_dma(wdts.ap, win_v[:, :, 2560:2576]), writes=[wdts.b], sbuf=wdts.b)
        P.op(DVE, I_tt(wdt.ap, wdts.ap, pc("gmix").unsqueeze(2).to_broadcast([128, 8, 16]), ALU.mult),
             reads=[wdts.b, par.b], writes=[wdt.b])
        P.op(DVE, I_ts(cwh.ap, pc("cws"), 0.5, None, ALU.mult), reads=[par.b], writes=[cwh.b])
        P.op(DVE, I_ts(cbh.ap, pc("cbs"), 0.5, None, ALU.mult), reads=[par.b], writes=[cbh.b])
        P.op(ACT, I_act(a_b.ap, pc("alog"), AF.Exp), reads=[par.b], writes=[a_b.b])
        P.op(DVE, I_ts(a_b.ap, a_b.ap, -1.0, None, ALU.mult), reads=[a_b.b], writes=[a_b.b])
        P.op(POOL, I_memset(Sst.ap, 0.0), writes=[Sst.b])
        P.op(POOL, I_memset(Sbf.ap, 0.0), writes=[Sbf.b])
        P.op(DVE, I_tt(yt.ap, identf.unsqueeze(1).to_broadcast([128, 8, 128]), pc("dcol").unsqueeze(2).to_broadcast([128, 8, 128]), ALU.mult),
             reads=[cst.b, par.b], writes=[yt.b])
        P.op(DVE, I_copy(Dhi.ap, yt.ap), reads=[yt.b], writes=[Dhi.b])
        P.op(DVE, I_tt(yt.ap, yt.ap, Dhi.ap, ALU.subtract), reads=[yt.b, Dhi.b], writes=[yt.b])
        P.op(DVE, I_copy(Dlo.ap, yt.ap), reads=[yt.b], writes=[Dlo.b])
        for i in range(12):
            P.op(POOL, I_memset(xpads[i].ap[:, 0:3], 0.0), writes=[xpads[i].b])

        pb_misc = banks[0]
        psD = T(pb_misc.ap[:, 0:64], "psD")
        psA = T(pb_misc.ap[:, 64:80], "psA")
        psC = T(pb_misc.ap[:, 128:384], "psC")
        psBt = T(pb_misc.ap[:, 384:512], "psBt")
        psSS = T(banks[3].ap[:, 256:512], "psSS")
        psR = [banks[1], banks[2]]
        psTb = T(banks[3].ap[:, 0:256], "psTx")
        psT = banks[7]
        psY = [banks[4], banks[5]]
        psSt = [banks[6], banks[3]]
        psSS.b = banks[3].b
        psD.b = psA.b = psC.b = psBt.b = pb_misc.b
        projbanks = [banks[1], banks[2], banks[4], banks[5], banks[6], banks[7]]
        pcount = [0]

        def ssd_tile_load(t):
            s = t % 2
            P.dma(I_dma(hTt[s].ap, hT_v[:, :, t * 512:(t + 1) * 512]), reads=[B_hT_d], writes=[hTt[s].b], sbuf=hTt[s].b)

        ssd_tile_load(0)
        for t in range(NT):
            hs = hTt[t % 2]
            if t + 1 < NT:
                ssd_tile_load(t + 1)
            for c in range(4):
                for kc in range(8):
                    P.op(PE, I_mm(psD.ap[:, c * 16:(c + 1) * 16], hs.ap[:, kc, c * 128:(c + 1) * 128], wdt.ap[:, kc, :],
                                  start=(kc == 0), stop=(kc == 7)), reads=[hs.b, wdt.b], writes=[psD.b])
            P.op(DVE, I_tt(dtx.ap.rearrange("p (c h) -> p c h", c=4), psD.ap.rearrange("p (c h) -> p c h", c=4),
                           pc("dtb").unsqueeze(1).to_broadcast([128, 4, 16]), ALU.add),
                 reads=[psD.b, par.b], writes=[dtx.b])
            P.op(ACT, I_act(dtx.ap, dtx.ap, AF.Exp), reads=[dtx.b], writes=[dtx.b])
            P.op(ACT, I_act(dt_t.ap.rearrange("p c h -> p (c h)"), dtx.ap, AF.Ln, bias=1.0, scale=1.0), reads=[dtx.b], writes=[dt_t.b])
            P.op(DVE, I_tt(dtA_t.ap, dt_t.ap, a_b.ap.unsqueeze(1).to_broadcast([128, 4, 16]), ALU.mult),
                 reads=[dt_t.b, a_b.b], writes=[dtA_t.b])
            P.op(DVE, I_copy(dhi.ap, dtA_t.ap), reads=[dtA_t.b], writes=[dhi.b])
            P.op(DVE, I_tt(dres.ap, dtA_t.ap, dhi.ap, ALU.subtract), reads=[dtA_t.b, dhi.b], writes=[dres.b])
            P.op(DVE, I_copy(dlo.ap, dres.ap), reads=[dres.b], writes=[dlo.b])
            for zb in range(8):
                bk = projbanks[pcount[0] % len(projbanks)]
                pcount[0] += 1
                for kc in range(8):
                    P.op(PE, I_mm(bk.ap, wssd.ap[:, kc, zb * 128:(zb + 1) * 128], hs.ap[:, kc, :], start=(kc == 0), stop=(kc == 7)),
                         reads=[wssd.b, hs.b], writes=[bk.b])
                P.op(ACT, I_act(szT.ap[:, zb, :], bk.ap, AF.Silu), reads=[bk.b], writes=[szT.b])
            def xbc_A(xb):
                bk = projbanks[pcount[0] % len(projbanks)]
                pcount[0] += 1
                for kc in range(8):
                    P.op(PE, I_mm(bk.ap, wssd.ap[:, kc, 1024 + xb * 128:1024 + (xb + 1) * 128], hs.ap[:, kc, :],
                                  start=(kc == 0), stop=(kc == 7)), reads=[wssd.b, hs.b], writes=[bk.b])
                xp = xpads[xb]
                ac = cacc[xb % 3]
                P.op(ACT, I_act(ac.ap, bk.ap, AF.Identity, bias=pc("cbs", xb), scale=pc("cws", xb * 4 + 3)),
                     reads=[bk.b, par.b], writes=[ac.b])
                P.op(ACT, I_act(xp.ap[:, 3:515], bk.ap, AF.Copy), reads=[bk.b], writes=[xp.b])

            def xbc_B(xb):
                xp = xpads[xb]
                ac = cacc[xb % 3]
                for k in range(3):
                    P.op(DVE, I_stt(ac.ap, xp.ap[:, k:k + 512], pc("cws", xb * 4 + k), ac.ap, ALU.mult, ALU.add),
                         reads=[xp.b, par.b, ac.b], writes=[ac.b])
                P.op(POOL, I_copy(xp.ap[:, 0:3], xp.ap[:, 512:515]), reads=[xp.b], writes=[xp.b])

            def xbc_C(xb):
                ac = cacc[xb % 3]
                P.op(ACT, I_act(xcT[xb].ap, ac.ap, AF.Silu), reads=[ac.b], writes=[xcT[xb].b])

            for step in range(12 + 2):
                if step < 12:
                    xbc_A(step)
                if 0 <= step - 1 < 12:
                    xbc_B(step - 1)
                if 0 <= step - 2 < 12:
                    xbc_C(step - 2)
            def ch_prep(c):
                P.op(PE, I_mm(psA.ap, triuf, dtA_t.ap[:, c, :]), reads=[cst.b, dtA_t.b], writes=[psA.b])
                P.op(DVE, I_ts(nacs.ap, psA.ap, -1.0, None, ALU.mult), reads=[psA.b], writes=[nacs.b])
                for rb in range(4):
                    bk = psR[rb % 2]
                    for hh in range(4):
                        h = rb * 4 + hh
                        o_ap = bk.ap[:, hh * 128:(hh + 1) * 128]
                        P.op(PE, I_mm(o_ap, dhi.ap[:, c, h:h + 1].to_broadcast([128, 128]), triub.ap, start=True, stop=False),
                             reads=[dhi.b, triub.b], writes=[bk.b])
                        P.op(PE, I_mm(o_ap, dlo.ap[:, c, h:h + 1].to_broadcast([128, 128]), triub.ap, start=False, stop=True),
                             reads=[dlo.b, triub.b], writes=[bk.b])
                    P.op(DVE, I_tt(ETp.ap[:, rb * 4:rb * 4 + 4, :], bk.ap.rearrange("p (a b) -> p a b", a=4),
                                   nacs.ap[:, rb * 4:rb * 4 + 4].unsqueeze(2).to_broadcast([128, 4, 128]), ALU.add),
                         reads=[bk.b, nacs.b], writes=[ETp.b])
                    P.op(ACT, I_act(EA.ap[:, rb * 4:rb * 4 + 4, :].rearrange("p a b -> p (a b)"), bk.ap, AF.Exp),
                         reads=[bk.b], writes=[EA.b])
                    last = bk.ap.rearrange("p (a b) -> p a b", a=4)[:, :, 127:128].rearrange("p a b -> p (a b)")
                    P.op(DVE, I_tt(dtmp.ap[:, rb * 4:rb * 4 + 4], last, nacs.ap[:, rb * 4:rb * 4 + 4], ALU.add),
                         reads=[bk.b, nacs.b], writes=[dtmp.b])
                    P.op(DVE, I_copy(cdl.ap[:, rb * 4:rb * 4 + 4], last), reads=[bk.b], writes=[cdl.b])
                P.op(ACT, I_act(ETp.ap.rearrange("p a b -> p (a b)"), ETp.ap.rearrange("p a b -> p (a b)"), AF.Abs),
                     reads=[ETp.b], writes=[ETp.b])
                P.op(ACT, I_act(ET.ap.rearrange("p a b -> p (a b)"), ETp.ap.rearrange("p a b -> p (a b)"), AF.Exp, scale=-1.0),
                     reads=[ETp.b], writes=[ET.b])
                P.op(ACT, I_act(dte.ap, dtmp.ap, AF.Exp), reads=[dtmp.b], writes=[dte.b])
                P.op(ACT, I_act(cds[c % 2].ap, cdl.ap, AF.Exp), reads=[cdl.b], writes=[cds[c % 2].b])

            def ch_mid(c):
                cs = slice(c * 128, (c + 1) * 128)
                for g in range(2):
                    P.op(PE, I_mm(psC.ap[:, g * 128:(g + 1) * 128], xcT[8 + g].ap[:, cs], xcT[10 + g].ap[:, cs]),
                         reads=[xcT[8 + g].b, xcT[10 + g].b], writes=[psC.b])
                P.op(DVE, I_tt(CBTm.ap, psC.ap.rearrange("p (g l) -> p g l", g=2), triuf.unsqueeze(1).to_broadcast([128, 2, 128]), ALU.mult),
                     reads=[psC.b, cst.b], writes=[CBTm.b])
                for g in range(2):
                    P.op(DVE, I_tt(scT.ap[:, 8 * g:8 * g + 8, :], ET.ap[:, 8 * g:8 * g + 8, :],
                                   CBTm.ap[:, g:g + 1, :].to_broadcast([128, 8, 128]), ALU.mult),
                         reads=[ET.b, CBTm.b], writes=[scT.b])
                for g in range(2):
                    P.op(POOL, I_tt(Csc.ap[:, 8 * g:8 * g + 8, :], EA.ap[:, 8 * g:8 * g + 8, :],
                                    xcT[10 + g].ap[:, cs].unsqueeze(1).to_broadcast([128, 8, 128]), ALU.mult),
                         reads=[EA.b, xcT[10 + g].b], writes=[Csc.b])
                pvT = bf_view(psT.ap, 8)
                for blk in range(8):
                    P.op(PE, I_tr(pvT[:, blk, :], xcT[blk].ap[:, cs], identb.ap), reads=[xcT[blk].b, identb.b], writes=[psT.b])
                P.op(DVE, I_tt(w2.ap, dt_t.ap[:, c, :], dte.ap, ALU.mult), reads=[dt_t.b, dte.b], writes=[w2.b])
                pvT16 = psT.ap.bitcast(BF16).rearrange("p (h d) -> p h d", h=16)
                P.op(DVE, I_tt(xdt.ap, pvT16, dt_t.ap[:, c, :].unsqueeze(2).to_broadcast([128, 16, 64]), ALU.mult),
                     reads=[psT.b, dt_t.b], writes=[xdt.b])
                P.op(DVE, I_tt(xdte.ap, pvT16, w2.ap.unsqueeze(2).to_broadcast([128, 16, 64]), ALU.mult),
                     reads=[psT.b, w2.b], writes=[xdte.b])
                pvB = psBt.ap.bitcast(BF16).rearrange("p (g n) -> p g n", g=2)
                for g in range(2):
                    P.op(PE, I_tr(pvB[:, g, :], xcT[8 + g].ap[:, cs], identb.ap), reads=[xcT[8 + g].b, identb.b], writes=[psBt.b])
                P.op(ACT, I_act(Btok.ap, pvB, AF.Copy), reads=[psBt.b], writes=[Btok.b])

            def ch_fin(c):
                cs = slice(c * 128, (c + 1) * 128)
                for hp in range(8):
                    bk = psY[hp // 4]
                    col = (hp % 4) * 128
                    full = bk.ap[:, col:col + 128]
                    P.op(PE, I_mm(full, Dhi.ap[:, hp, :], xcT[hp].ap[:, cs], start=True, stop=False),
                         reads=[Dhi.b, xcT[hp].b], writes=[bk.b])
                    P.op(PE, I_mm(full, Dlo.ap[:, hp, :], xcT[hp].ap[:, cs], start=False, stop=False),
                         reads=[Dlo.b, xcT[hp].b], writes=[bk.b])
                    for half in range(2):
                        h = 2 * hp + half
                        o_ap = bk.ap[64 * half:64 * half + 64, col:col + 128]
                        P.op(PE, I_mm(o_ap, xdt.ap[:, h, :], scT.ap[:, h, :], start=False, stop=False),
                             reads=[xdt.b, scT.b], writes=[bk.b])
                        P.op(PE, I_mm(o_ap, Sbf.ap[:, h, :], Csc.ap[:, h, :], start=False, stop=(half == 1)),
                             reads=[Sbf.b, Csc.b], writes=[bk.b])
                for k in range(2):
                    P.op(DVE, I_tt(yt.ap[:, 4 * k:4 * k + 4, :], psY[k].ap.rearrange("p (a b) -> p a b", a=4),
                                   szT.ap[:, 4 * k:4 * k + 4, cs], ALU.mult),
                         reads=[psY[k].b, szT.b], writes=[yt.b])
                P.op(ACT, I_act(sqb.ap, yt.ap, AF.Square), reads=[yt.b], writes=[sqb.b])
                for g in range(2):
                    for j in range(4):
                        P.op(PE, I_mm(psSS.ap[:, g * 128:(g + 1) * 128], onesb.ap, sqb.ap[:, 4 * g + j, :], start=(j == 0), stop=(j == 3)),
                             reads=[onesb.b, sqb.b], writes=[psSS.b])
                P.op(ACT, I_act(lnr.ap.rearrange("p g l -> p (g l)"), psSS.ap, AF.Ln, bias=EPS, scale=1.0 / 512),
                     reads=[psSS.b], writes=[lnr.b])
                P.op(ACT, I_act(rsr.ap, lnr.ap, AF.Exp, scale=-0.5), reads=[lnr.b], writes=[rsr.b])
                for g in range(2):
                    P.op(POOL, I_tt(ystage.ap[:, 4 * g:4 * g + 4, cs], yt.ap[:, 4 * g:4 * g + 4, :],
                                    rsr.ap[:, g:g + 1, :].to_broadcast([128, 4, 128]), ALU.mult),
                         reads=[yt.b, rsr.b], writes=[ystage.b])
                for g in range(2):
                    bk = psSt[g]
                    P.op(PE, I_mm(bk.ap, Btok.ap[:, g, :], xdte.ap[:, 8 * g:8 * g + 8, :].rearrange("p h d -> p (h d)")),
                         reads=[Btok.b, xdte.b], writes=[bk.b])
                cdc = cds[c % 2]
                for g in range(2):
                    bk = psSt[g]
                    sv = Sst.ap[:, 8 * g:8 * g + 8, :]
                    P.op(DVE, I_tt(sv, sv, cdc.ap[:, 8 * g:8 * g + 8].unsqueeze(2).to_broadcast([128, 8, 64]), ALU.mult),
                         reads=[Sst.b, cdc.b], writes=[Sst.b])
                    P.op(DVE, I_tt(sv, sv, bk.ap.rearrange("p (h d) -> p h d", h=8), ALU.add),
                         reads=[Sst.b, bk.b], writes=[Sst.b])
                P.op(ACT, I_act(Sbf.ap, Sst.ap, AF.Copy), reads=[Sst.b], writes=[Sbf.b])

            ch_prep(0)
            for c in range(4):
                ch_mid(c)
                if c + 1 < 4:
                    ch_prep(c + 1)
                ch_fin(c)
            P.dma(I_dma(mixT_v[:, 0:8, t * 512:(t + 1) * 512], ystage.ap), reads=[ystage.b], writes=[B_mixT_d], sbuf=ystage.b, eng=POOL)

        if UPTO < 3:
            raise StopBuild()
        P.barrier()
        A.off = gbase
        hT = A.alloc([128, 8, S], BF16, "hT")
        hT_k = []
        for kc in range(8):
            t_ = T(hT.ap[:, kc, :], f"hT{kc}")
            hT_k.append(t_)
        cumT = A.alloc([16, S], F32, "cumT")
        ftmp = A.alloc([16, S], F32, "ftmp")
        cumbf = A.alloc([16, S], BF16, "cumbf")
        ncum = A.alloc([128, 32, 16], F32, "ncum")
        nfb = A.alloc([128, 1], F32, "nfb")
        wq8 = A.alloc([128, 1], F32, "wq8")
        wfs = A.alloc([128, 8, 16], F32, "wfs")
        wfb = A.alloc([128, 8, 16], BF16, "wfb")
        wqs = [A.alloc([128, 8, 128], F32, f"wqs{i}") for i in range(3)]
        wqb = [A.alloc([128, 8, 128], BF16, f"wqb{i}") for i in range(3)]
        qA = A.alloc([128, S], BF16, "qA")
        qB = A.alloc([128, S], BF16, "qB")
        kA = A.alloc([128, S], BF16, "kA")
        kB = A.alloc([128, S], BF16, "kB")
        vA = A.alloc([128, 32, 128], BF16, "vA")
        vB = A.alloc([128, 32, 128], BF16, "vB")
        sq2 = [A.alloc([128, 512], BF16, f"sq2{i}") for i in range(3)]
        lnq = [A.alloc([128, 512], F32, f"lnq{i}") for i in range(3)]
        rsq = [A.alloc([128, 512], F32, f"rsq{i}") for i in range(3)]
        NPT = 4
        PT = [A.alloc([128, 512], BF16, f"PT{i}") for i in range(NPT)]
        rr = [A.alloc([128, 512], F32, f"rr{i}") for i in range(2)]
        ost = [A.alloc([128, 512], BF16, f"ost{i}") for i in range(2)]

        for kc in range(8):
            P.dma(I_dma(hT_k[kc].ap, hT_v[:, kc, :]), reads=[B_hT_d], writes=[hT_k[kc].b], sbuf=hT_k[kc].b)
        hT_bufs = [t_.b for t_ in hT_k]

        P.op(DVE, I_ts(nfb.ap, pc("fb"), -1.0, None, ALU.mult), reads=[par.b], writes=[nfb.b])
        P.op(DVE, I_ts(wq8.ap, pc("wq"), 0.125, None, ALU.mult), reads=[par.b], writes=[wq8.b])
        for tl in (qA, qB, kA, kB):
            P.op(POOL, I_memset(tl.ap, 0.0), writes=[tl.b])
        P.op(POOL, I_memset(kA.ap[64:65, :], 1.0), writes=[kA.b])
        P.op(POOL, I_memset(kB.ap[0:1, :], 1.0), writes=[kB.b])
        P.op(POOL, I_memset(vA.ap, 1.0), writes=[vA.b])
        P.op(POOL, I_memset(vB.ap, 1.0), writes=[vB.b])

        P.dma(I_dma(wfs.ap, win_v[:, :, 5648:5664]), writes=[wfs.b], sbuf=wfs.b)
        P.op(DVE, I_tt(wfb.ap, wfs.ap, pc("gmix").unsqueeze(2).to_broadcast([128, 8, 16]), ALU.mult),
             reads=[wfs.b, par.b], writes=[wfb.b])
        for t in range(NT):
            bk = banks[t % 2]
            tsl = slice(t * 512, (t + 1) * 512)
            for kc in range(8):
                P.op(PE, I_mm(bk.ap[0:16, :], wfb.ap[:, kc, :], hT_k[kc].ap[:, tsl], start=(kc == 0), stop=(kc == 7)),
                     reads=[wfb.b, hT_k[kc].b], writes=[bk.b])
            P.op(ACT, I_act(ftmp.ap[:, tsl], bk.ap[0:16, :], AF.Exp, bias=nfb.ap[0:16, :], scale=-1.0),
                 reads=[bk.b, nfb.b], writes=[ftmp.b])
            P.op(ACT, I_act(ftmp.ap[:, tsl], ftmp.ap[:, tsl], AF.Ln, bias=1.0, scale=1.0), reads=[ftmp.b], writes=[ftmp.b])
            init = 0.0 if t == 0 else cumT.ap[:, t * 512 - 1:t * 512]
            P.op(DVE, I_scan(cumT.ap[:, tsl], ones16.ap[0:16, :], ftmp.ap[:, tsl], init, ALU.mult, ALU.subtract),
                 reads=[ones16.b, ftmp.b, cumT.b], writes=[cumT.b])
        P.op(POOL, I_copy(cumbf.ap, cumT.ap), reads=[cumT.b], writes=[cumbf.b])
        pN = banks[2]
        for b in range(32):
            P.op(PE, I_tr(pN.ap[:, b * 16:(b + 1) * 16], cumT.ap[:, b * 128:(b + 1) * 128], identf[0:16, 0:16]),
                 reads=[cumT.b, cst.b], writes=[pN.b])
        P.op(DVE, I_ts(ncum.ap.rearrange("p b h -> p (b h)"), pN.ap, -1.0, None, ALU.mult), reads=[pN.b], writes=[ncum.b])

        psQ2 = [(banks[0], banks[1]), (banks[2], banks[3]), (banks[6], banks[7])]
        psVs = [banks[4], banks[5]]
        psS = [banks[2], banks[3], banks[4]]
        psO = [banks[5], banks[6], banks[7]]
        ocnt = [0]
        qcnt = [0]

        for hp in range(8):
            def w_cols(p_):
                return [2576 + p_ * 128, 3600 + p_ * 128, 4624 + p_ * 128]
            if hp == 0:
                for i in range(3):
                    P.dma(I_dma(wqs[i].ap, win_v[:, :, w_cols(0)[i]:w_cols(0)[i] + 128]), writes=[wqs[i].b], sbuf=wqs[i].b)
            for i in range(3):
                P.op(POOL if i == 1 else DVE, I_tt(wqb[i].ap, wqs[i].ap, pc("gmix").unsqueeze(2).to_broadcast([128, 8, 128]), ALU.mult),
                     reads=[wqs[i].b, par.b], writes=[wqb[i].b])
            if hp + 1 < 8:
                for i in range(3):
                    c_ = w_cols(hp + 1)[i]
                    P.dma(I_dma(wqs[i].ap, win_v[:, :, c_:c_ + 128]), writes=[wqs[i].b], sbuf=wqs[i].b)
            P.dma(I_dma(qA.ap[64:65, :], cumbf.ap[2 * hp:2 * hp + 1, :]), reads=[cumbf.b], writes=[qA.b], sbuf=qA.b)
            P.dma(I_dma(qB.ap[0:1, :], cumbf.ap[2 * hp + 1:2 * hp + 2, :]), reads=[cumbf.b], writes=[qB.b], sbuf=qB.b)
            qk_items = [(which, t) for which in range(2) for t in range(NT)]

            def qk_A(idx):
                which, t = qk_items[idx]
                bk, _ = psQ2[idx % 3]
                tsl = slice(t * 512, (t + 1) * 512)
                for kc in range(8):
                    P.op(PE, I_mm(bk.ap, wqb[which].ap[:, kc, :], hT_k[kc].ap[:, tsl], start=(kc == 0), stop=(kc == 7)),
                         reads=[wqb[which].b, hT_k[kc].b], writes=[bk.b])

            def qk_B(idx):
                which, t = qk_items[idx]
                XA, XB = (qA, qB) if which == 0 else (kA, kB)
                wcol = wq8.ap if which == 0 else pc("wk")
                wcol_b = wq8.b if which == 0 else par.b
                bk, bk2 = psQ2[idx % 3]
                tsl = slice(t * 512, (t + 1) * 512)
                sq = sq2[idx % 3]
                ln_ = lnq[idx % 3]
                rs_ = rsq[idx % 3]
                P.op(ACT, I_act(sq.ap, bk.ap, AF.Square), reads=[bk.b], writes=[sq.b])
                P.op(PE, I_mm(bk2.ap, bonesb.ap, sq.ap), reads=[bonesb.b, sq.b], writes=[bk2.b])
                P.op(ACT, I_act(ln_.ap, bk2.ap, AF.Ln, bias=EPS, scale=1.0 / 64), reads=[bk2.b], writes=[ln_.b])
                P.op(ACT, I_act(rs_.ap, ln_.ap, AF.Exp, scale=-0.5), reads=[ln_.b], writes=[rs_.b])
                P.op(DVE, I_stt(XA.ap[0:64, tsl], bk.ap[0:64, :], wcol[0:64, :], rs_.ap[0:64, :], ALU.mult, ALU.mult),
                     reads=[bk.b, wcol_b, rs_.b], writes=[XA.b])
                P.op(DVE, I_stt(XB.ap[64:128, tsl], bk.ap[64:128, :], wcol[64:128, :], rs_.ap[64:128, :], ALU.mult, ALU.mult),
                     reads=[bk.b, wcol_b, rs_.b], writes=[XB.b])

            qk_A(0)
            for idx in range(len(qk_items)):
                if idx + 1 < len(qk_items):
                    qk_A(idx + 1)
                qk_B(idx)
            for bq in range(8):
                psV = psVs[bq % 2]
                pvv = psV.ap.rearrange("p (j c) -> p j c", j=4)
                for j in range(4):
                    b = 4 * bq + j
                    for kc in range(8):
                        P.op(PE, I_mm(psV.ap[:, j * 128:(j + 1) * 128], hT_k[kc].ap[:, b * 128:(b + 1) * 128], wqb[2].ap[:, kc, :],
                                      start=(kc == 0), stop=(kc == 7)), reads=[hT_k[kc].b, wqb[2].b], writes=[psV.b])
                P.op(ACT, I_act(vA.ap[:, 4 * bq:4 * bq + 4, 0:64], pvv[:, :, 0:64], AF.Copy), reads=[psV.b], writes=[vA.b])
                P.op(DVE, I_copy(vB.ap[:, 4 * bq:4 * bq + 4, 64:128], pvv[:, :, 64:128]), reads=[psV.b], writes=[vB.b])
            seq = []
            for i in range(NT):
                for hd in range(2):
                    nj = 4 * i + 4
                    for j in range(nj):
                        seq.append((i, hd, j, nj))

            def emit_S(n):
                i, hd, j, nj = seq[n]
                r = j - 4 * i
                c0 = max(0, r) * 128
                N = 512 - c0
                Kf = kA if hd == 0 else kB
                Qf = qA if hd == 0 else qB
                bk = psS[n % 3]
                P.op(PE, I_mm(bk.ap[:, 0:N], Kf.ap[:, j * 128:(j + 1) * 128], Qf.ap[:, i * 512 + c0:(i + 1) * 512]),
                     reads=[Kf.b, Qf.b], writes=[bk.b])

            def emit_PV(n):
                i, hd, j, nj = seq[n]
                r = j - 4 * i
                c0 = max(0, r) * 128
                N = 512 - c0
                h = 2 * hp + hd
                bk = psS[n % 3]
                pt = PT[n % NPT]
                if j == 0:
                    ocnt[0] += 1
                ob = psO[ocnt[0] % 3]
                P.op(ACT, I_act(pt.ap[:, 0:N], bk.ap[:, 0:N], AF.Exp, bias=ncum.ap[:, j, h:h + 1], scale=1.0),
                     reads=[bk.b, ncum.b], writes=[pt.b])
                if r >= 0:
                    P.op(POOL, I_tt(pt.ap[:, 0:128], pt.ap[:, 0:128], triub.ap, ALU.mult), reads=[pt.b, triub.b], writes=[pt.b])
                V = vA if hd == 0 else vB
                P.op(PE, I_mm(ob.ap[:, c0:512], V.ap[:, j, :], pt.ap[:, 0:N], start=(j == 0), stop=(j == nj - 1)),
                     reads=[V.b, pt.b], writes=[ob.b])
                if j == nj - 1:
                    os_ = ost[i % 2]
                    rt = rr[hd]
                    if hd == 0:
                        P.op(DVE, I_recip(rt.ap[64:128, :], ob.ap[64:128, :]), reads=[ob.b], writes=[rt.b])
                        P.op(DVE, I_tt(os_.ap[0:64, :], ob.ap[0:64, :], rt.ap[64:128, :], ALU.mult), reads=[ob.b, rt.b], writes=[os_.b])
                    else:
                        P.op(DVE, I_recip(rt.ap[0:64, :], ob.ap[0:64, :]), reads=[ob.b], writes=[rt.b])
                        P.op(DVE, I_tt(os_.ap[64:128, :], ob.ap[64:128, :], rt.ap[0:64, :], ALU.mult), reads=[ob.b, rt.b], writes=[os_.b])
                        P.dma(I_dma(mixT_v[:, 8 + hp, i * 512:(i + 1) * 512], os_.ap), reads=[os_.b], writes=[B_mixT_d], sbuf=os_.b)

            LOOK = 3
            for n in range(min(LOOK, len(seq))):
                emit_S(n)
            for n in range(len(seq)):
                emit_PV(n)
                if n + LOOK < len(seq):
                    emit_S(n + LOOK)

        if UPTO < 4:
            raise StopBuild()
        P.barrier()
        A.off = gbase
        wout = A.alloc([128, 16, D], BF16, "wout")
        wst3 = [A.alloc([128, 1024], F32, f"wst3{i}") for i in range(4)]
        mixt = [A.alloc([128, 16, 512], BF16, f"mixt{i}") for i in range(2)]
        xg3 = [A.alloc([128, 4, D], F32, f"xg3{i}") for i in range(2)]
        xn3 = [A.alloc([128, 4, D], BF16, f"xn3{i}") for i in range(2)]
        hst3 = [A.alloc([128, 8, 512], BF16, f"hst3{i}") for i in range(2)]
        junk3 = A.alloc([128, D], BF16, "junk3")
        ss3 = [A.alloc([128, 4], F32, f"ss3{i}") for i in range(2)]
        lnv3 = [A.alloc([128, 4], F32, f"lnv3{i}") for i in range(2)]
        rstd3 = [A.alloc([128, 4], F32, f"rstd3{i}") for i in range(2)]
        cnt3 = [0]
        load_weight_rows(wout, wout_v, 16, 0, D, lambda k: (pc("gssd", k) if k < 8 else None), wst3, cnt3)

        def p3_load(t):
            s = t % 2
            P.dma(I_dma(mixt[s].ap, mixT_v[:, :, t * 512:(t + 1) * 512]), reads=[B_mixT_d], writes=[mixt[s].b], sbuf=mixt[s].b)
            P.dma(I_dma(xg3[s].ap, x_v[t]), writes=[xg3[s].b], sbuf=xg3[s].b)

        def p3_a(t):
            s = t % 2
            n = 0
            for b in range(4):
                for half in range(2):
                    bk = banks[4 + (n % 4)]
                    n += 1
                    for fc in range(16):
                        P.op(PE, I_mm(bk.ap, mixt[s].ap[:, fc, b * 128:(b + 1) * 128], wout.ap[:, fc, half * 512:(half + 1) * 512],
                                      start=(fc == 0), stop=(fc == 15)), reads=[mixt[s].b, wout.b], writes=[bk.b])
                    xs_ = xg3[s].ap[:, b, half * 512:(half + 1) * 512]
                    P.op(DVE, I_tt(xs_, xs_, bk.ap, ALU.add), reads=[xg3[s].b, bk.b], writes=[xg3[s].b])
            P.dma(I_dma(x1_v[t], xg3[s].ap), reads=[xg3[s].b], writes=[B_x1_d], sbuf=xg3[s].b, eng=POOL)
            rms_stage_a(xg3[s], ss3[s], lnv3[s], rstd3[s], junk3)

        def p3_b(t):
            s = t % 2
            norm_transpose(xg3[s], rstd3[s], xn3[s], banks[0:4], hst3[s], t)
            P.dma(I_dma(h2T_v[:, :, t * 512:(t + 1) * 512], hst3[s].ap), reads=[hst3[s].b], writes=[B_h2T_d], sbuf=hst3[s].b, eng=POOL)

        p3_load(0)
        p3_load(1)
        p3_a(0)
        for t in range(NT):
            if t + 1 < NT:
                p3_a(t + 1)
            p3_b(t)
            if t + 2 < NT:
                p3_load(t + 2)

        if UPTO < 5:
            raise StopBuild()
        P.barrier()
        A.off = gbase
        wup = A.alloc([128, 8, 2 * DFF], BF16, "wup")
        wdn = A.alloc([128, 22, D], BF16, "wdn")
        wst4 = [A.alloc([128, 512], F32, f"wst4{i}") for i in range(2)]
        cwf = A.alloc([128, 132], F32, "cwf")
        cbf = A.alloc([128, 44], F32, "cbf")
        halo = A.alloc([128, 44, 2], F32, "halo")
        h2t = [A.alloc([128, 8, 512], BF16, "h2t0")]
        x1t = A.alloc([128, 2, D], F32, "x1t")
        gpad = [A.alloc([128, 514], F32, f"gpad{i}") for i in range(2)]
        vpad = [A.alloc([128, 514], F32, f"vpad{i}") for i in range(2)]
        gpadh = [Buf(f"gpadh{i}") for i in range(2)]
        vpadh = [Buf(f"vpadh{i}") for i in range(2)]
        gacc = [A.alloc([128, 512], F32, f"gacc{i}") for i in range(3)]
        vacc = [A.alloc([128, 512], F32, f"vacc{i}") for i in range(3)]
        th4 = [A.alloc([128, 512], F32, f"th4{i}") for i in range(2)]
        gT = A.alloc([128, 22, 512], BF16, "gT")
        cnt4 = [0]
        wup_piece = {}
        wunits = [("up", k, ci) for ci in (0, 5, 6, 1, 7, 2, 8, 3, 9, 4, 10) for k in range(8)]
        wunits += [("dn", k, cc) for k in range(22) for cc in (0, 512)]
        upos = [0]

        def emit_units(n):
            for _ in range(n):
                if upos[0] >= len(wunits):
                    return
                kind, k, x_ = wunits[upos[0]]
                upos[0] += 1
                st = wst4[cnt4[0] % len(wst4)]
                eng = DVE if cnt4[0] % 2 == 0 else ACT
                cnt4[0] += 1
                if kind == "up":
                    cc = x_ * 512
                    pb = Buf(f"wup_{k}_{x_}")
                    wup_piece[(k, x_)] = pb
                    P.dma(I_dma(st.ap[:, 0:512], wup_v[:, k, cc:cc + 512]), writes=[st.b], sbuf=st.b)
                    if eng == ACT:
                        P.op(ACT, I_act(wup.ap[:, k, cc:cc + 512], st.ap[:, 0:512], AF.Identity, scale=pc("gffn", k)),
                             reads=[st.b, par.b], writes=[pb])
                    else:
                        P.op(DVE, I_ts(wup.ap[:, k, cc:cc + 512], st.ap[:, 0:512], pc("gffn", k), 1.0, ALU.mult, ALU.mult),
                             reads=[st.b, par.b], writes=[pb])
                else:
                    cc = x_
                    P.dma(I_dma(st.ap[:, 0:512], wdn_v[:, k, cc:cc + 512]), writes=[st.b], sbuf=st.b)
                    if eng == ACT:
                        P.op(ACT, I_act(wdn.ap[:, k, cc:cc + 512], st.ap[:, 0:512], AF.Copy), reads=[st.b], writes=[wdn.b])
                    else:
                        P.op(DVE, I_copy(wdn.ap[:, k, cc:cc + 512], st.ap[:, 0:512]), reads=[st.b], writes=[wdn.b])

        emit_units(24)
        P.op(DVE, I_ts(cwf.ap[:, 0:66], pc("cwf", 0, 66), 0.5, None, ALU.mult), reads=[par.b], writes=[cwf.b])
        P.op(DVE, I_copy(cwf.ap[:, 66:132], pc("cwf", 66, 132)), reads=[par.b], writes=[cwf.b])
        P.op(DVE, I_ts(cbf.ap[:, 0:22], pc("cbf", 0, 22), 0.5, None, ALU.mult), reads=[par.b], writes=[cbf.b])
        P.op(DVE, I_copy(cbf.ap[:, 22:44], pc("cbf", 22, 44)), reads=[par.b], writes=[cbf.b])
        P.op(POOL, I_memset(halo.ap, 0.0), writes=[halo.b])

        def p4_load(t):
            s = 0
            P.dma(I_dma(h2t[s].ap, h2T_v[:, :, t * 512:(t + 1) * 512]), reads=[B_h2T_d], writes=[h2t[s].b], sbuf=h2t[s].b)

        def conv3(pad, acc, blk, first_eng):
            P.op(first_eng, I_ts(acc.ap, pad.ap[:, 0:512], cwf.ap[:, blk * 3:blk * 3 + 1], cbf.ap[:, blk:blk + 1], ALU.mult, ALU.add),
                 reads=[pad.b, cwf.b, cbf.b], writes=[acc.b])
            for k in range(1, 3):
                P.op(DVE, I_stt(acc.ap, pad.ap[:, k:k + 512], cwf.ap[:, blk * 3 + k:blk * 3 + k + 1], acc.ap, ALU.mult, ALU.add),
                     reads=[pad.b, cwf.b, acc.b], writes=[acc.b])

        p4_load(0)
        gbanks = [banks[0], banks[1]]
        vbanks = [banks[2], banks[3]]
        dbanks = [banks[4], banks[5], banks[6], banks[7]]
        for t in range(NT):
            hs = h2t[0]
            if t > 0:
                p4_load(t)
            def ffn_A(c):
                s = c % 2
                bg = gbanks[s]
                bv = vbanks[s]
                for kc in range(8):
                    P.op(PE, I_mm(bg.ap, wup.ap[:, kc, c * 128:(c + 1) * 128], hs.ap[:, kc, :], start=(kc == 0), stop=(kc == 7)),
                         reads=[wup_piece[(kc, (c * 128) // 512)], hs.b], writes=[bg.b])
                for kc in range(8):
                    P.op(PE, I_mm(bv.ap, wup.ap[:, kc, DFF + c * 128:DFF + (c + 1) * 128], hs.ap[:, kc, :], start=(kc == 0), stop=(kc == 7)),
                         reads=[wup_piece[(kc, (DFF + c * 128) // 512)], hs.b], writes=[bv.b])
                for (pad, padh, acc, bk, blk) in ((gpad[s], gpadh[s], gacc[c % 3], bg, c), (vpad[s], vpadh[s], vacc[c % 3], bv, 22 + c)):
                    P.op(POOL, I_copy(pad.ap[:, 0:2], halo.ap[:, blk, :]), reads=[halo.b], writes=[padh])
                    P.op(ACT, I_act(acc.ap, bk.ap, AF.Identity, bias=pc("cbf", blk), scale=pc("cwf", blk * 3 + 2)),
                         reads=[bk.b, par.b], writes=[acc.b])
                    P.op(ACT, I_act(pad.ap[:, 2:514], bk.ap, AF.Copy), reads=[bk.b], writes=[pad.b])
                    P.op(POOL, I_copy(halo.ap[:, blk, :], pad.ap[:, 512:514]), reads=[pad.b], writes=[halo.b])

            def ffn_B(c):
                s = c % 2
                for (pad, padh, acc, blk) in ((gpad[s], gpadh[s], gacc[c % 3], c), (vpad[s], vpadh[s], vacc[c % 3], 22 + c)):
                    for k in range(2):
                        P.op(DVE, I_stt(acc.ap, pad.ap[:, k:k + 512], pc("cwf", blk * 3 + k), acc.ap, ALU.mult, ALU.add),
                             reads=[pad.b, padh, par.b, acc.b], writes=[acc.b])

            def ffn_C(c):
                s = c % 2
                P.op(ACT, I_act(th4[s].ap, gacc[c % 3].ap, AF.Silu), reads=[gacc[c % 3].b], writes=[th4[s].b])
                P.op(POOL, I_tt(gT.ap[:, c, :], th4[s].ap, vacc[c % 3].ap, ALU.mult), reads=[th4[s].b, vacc[c % 3].b], writes=[gT.b])

            for step in range(22 + 2):
                if t == 0:
                    emit_units(5)
                if step < 22:
                    ffn_A(step)
                if 0 <= step - 1 < 22:
                    ffn_B(step - 1)
                if 0 <= step - 2 < 22:
                    ffn_C(step - 2)
            if t == 0:
                emit_units(len(wunits))
            n = 0
            for hb in range(2):
                P.dma(I_dma(x1t.ap, x1_v[t][:, 2 * hb:2 * hb + 2, :]), reads=[B_x1_d], writes=[x1t.b], sbuf=x1t.b)
                for bb in range(2):
                    b = 2 * hb + bb
                    for half in range(2):
                        bk = dbanks[n % 4]
                        n += 1
                        for c in range(22):
                            P.op(PE, I_mm(bk.ap, gT.ap[:, c, b * 128:(b + 1) * 128], wdn.ap[:, c, half * 512:(half + 1) * 512],
                                          start=(c == 0), stop=(c == 21)), reads=[gT.b, wdn.b], writes=[bk.b])
                        xs_ = x1t.ap[:, bb, half * 512:(half + 1) * 512]
                        P.op(DVE, I_tt(xs_, xs_, bk.ap, ALU.add), reads=[x1t.b, bk.b], writes=[x1t.b])
                P.dma(I_dma(out_v[t][:, 2 * hb:2 * hb + 2, :], x1t.ap), reads=[x1t.b], writes=[B_out_d], sbuf=x1t.b, eng=POOL)


    except StopBuild:
        pass
    finals = [B_out_d]
    if DEBUG:
        finals += [B_hT_d, B_mixT_d, B_x1_d, B_h2T_d]
    nsem = P.emit(nc, final_wait_bufs=finals)
    return nc, nsem, len(P.ops)


def _pack_params(norm_mix_w, ssd_conv_w, ssd_conv_b, ssd_dt_bias, ssd_a_log, ssd_d, ssd_norm_w,
                 fox_f_bias, fox_q_norm_w, fox_k_norm_w, norm_ffn_w, ffn_conv_w, ffn_conv_b):
    par = np.zeros((128, NPAR), np.float32)

    def put(name, arr):
        lo, hi = PCOL[name]
        par[:, lo:hi] = arr

    put("gmix", norm_mix_w[0].reshape(8, 128).T)
    put("gffn", norm_ffn_w[0].reshape(8, 128).T)
    put("gssd", ssd_norm_w[0].reshape(8, 128).T)
    put("cws", ssd_conv_w[0].reshape(4, 12, 128).transpose(2, 1, 0).reshape(128, 48))
    put("cbs", ssd_conv_b[0].reshape(12, 128).T)
    put("dcol", np.repeat(ssd_d[0].reshape(8, 2), 64, axis=1).T)
    put("wq", np.tile(fox_q_norm_w[0], 2)[:, None])
    put("wk", np.tile(fox_k_norm_w[0], 2)[:, None])
    fb = np.zeros((128, 1), np.float32)
    fb[0:16, 0] = fox_f_bias[0]
    put("fb", fb)
    put("dtb", np.tile(ssd_dt_bias[0][None, :], (128, 1)))
    put("alog", np.tile(ssd_a_log[0][None, :], (128, 1)))
    put("cwf", ffn_conv_w[0].reshape(3, 44, 128).transpose(2, 1, 0).reshape(128, 132))
    put("cbf", ffn_conv_b[0].reshape(44, 128).T)
    return par


def _rows_to_pk(w, nk):
    c = w.shape[1]
    return np.ascontiguousarray(w.reshape(nk, 128, c).transpose(1, 0, 2).reshape(128, nk * c))


_CACHE = {}


def kernel(x, norm_mix_w, w_in, ssd_conv_w, ssd_conv_b, ssd_dt_bias, ssd_a_log, ssd_d,
           ssd_norm_w, fox_f_bias, fox_q_norm_w, fox_k_norm_w, w_out, norm_ffn_w,
           w_up, ffn_conv_w, ffn_conv_b, w_down):
    f = lambda a: np.asarray(a, dtype=np.float32)
    x = f(x)
    par = _pack_params(f(norm_mix_w), f(ssd_conv_w), f(ssd_conv_b), f(ssd_dt_bias), f(ssd_a_log), f(ssd_d),
                       f(ssd_norm_w), f(fox_f_bias), f(fox_q_norm_w), f(fox_k_norm_w), f(norm_ffn_w),
                       f(ffn_conv_w), f(ffn_conv_b))
    cst = np.concatenate([np.eye(128, dtype=np.float32), np.triu(np.ones((128, 128), np.float32)),
                          np.kron(np.eye(2, dtype=np.float32), np.ones((64, 64), np.float32))], axis=1)
    win = _rows_to_pk(f(w_in)[0], 8)
    wout = _rows_to_pk(f(w_out)[0], 16)
    wup = _rows_to_pk(f(w_up)[0], 8)
    wdn = _rows_to_pk(f(w_down)[0], 22)
    if "nc" not in _CACHE:
        _CACHE["nc"] = build_program()
    nc, nsem, nops = _CACHE["nc"]
    n = x.shape[0]
    in_maps = []
    for c in range(n):
        in_maps.append({"x": np.ascontiguousarray(x[c]), "win": win, "wout": wout, "wup": wup, "wdn": wdn,
                        "par": par, "cst": cst})
    res = run_bass_kernel_spmd(nc, in_maps, core_ids=list(range(n)))
    if DEBUG:
        _CACHE["dbg"] = res.results
    return np.stack([r["out"] for r in res.results], axis=0).astype(np.float32)
```

```python
import contextlib
import numpy as np
import concourse.bass as bass
import concourse.mybir as mybir
from concourse.bass_utils import run_bass_kernel_spmd

F32 = mybir.dt.float32
BF16 = mybir.dt.bfloat16
AF = mybir.ActivationFunctionType
ALU = mybir.AluOpType

PE, ACT, DVE, POOL, SP = "tensor", "scalar", "vector", "gpsimd", "sync"
EPOCH = 12000

S = 4096
D = 1024
NT = 8
IN_COLS = 5664
DFF = 2816
EPS = 1e-6
DEBUG = False
UPTO = 5
LIMIT = None


class StopBuild(Exception):
    pass


class Buf:
    def __init__(self, name, multi=False):
        self.name = name
        self.multi = multi
        self.psum = False
        self.writers = []
        self.readers = []


class Op:
    __slots__ = ("eng", "fn", "reads", "writes", "dma", "deps", "need", "tok", "dbuf", "idx")

    def __init__(self, eng, fn, reads, writes, dma, dbuf):
        self.eng = eng
        self.fn = fn
        self.reads = reads
        self.writes = writes
        self.dma = dma
        self.dbuf = dbuf
        self.deps = set()
        self.need = False
        self.tok = None


class Prog:
    def __init__(self):
        self.ops = []

    def op(self, eng, fn, reads=(), writes=(), dma=False, dbuf=None):
        o = Op(eng, fn, tuple(reads), tuple(writes), dma, dbuf)
        o.idx = len(self.ops)
        self.ops.append(o)
        return o

    def dma(self, fn, reads=(), writes=(), sbuf=None, eng=SP):
        return self.op(eng, fn, reads, writes, dma=True, dbuf=sbuf)

    def barrier(self):
        last = {}
        last_dma = {}
        for o in self.ops:
            if o.dma:
                last_dma[id(o.dbuf)] = o.idx
            elif o.fn is not None:
                last[o.eng] = o.idx
        deps = set(last.values()) | set(last_dma.values())
        for eng in (SP, PE, ACT, DVE, POOL):
            j = Op(eng, None, (), (), False, None)
            j.deps = set(deps)
            j.idx = len(self.ops)
            self.ops.append(j)

    def analyze(self):
        last_dma_on = {}
        for o in self.ops:
            deps = o.deps
            for b in o.reads:
                deps.update(b.writers)
                if b.psum:
                    for r in b.readers:
                        if self.ops[r].eng != o.eng:
                            deps.add(r)
            for b in o.writes:
                if not b.multi:
                    deps.update(b.writers)
                deps.update(b.readers)
            if o.dma:
                p = last_dma_on.get(id(o.dbuf))
                if p is not None:
                    deps.add(p)
                last_dma_on[id(o.dbuf)] = o.idx
            wset = set(id(b) for b in o.writes)
            for b in o.reads:
                if id(b) not in wset:
                    b.readers.append(o.idx)
            for b in o.writes:
                if b.multi:
                    b.writers.append(o.idx)
                else:
                    b.writers = [o.idx]
                b.readers = []
            deps.discard(o.idx)
            if o.eng == PE and not o.dma:
                for d in list(deps):
                    od = self.ops[d]
                    if od.eng == PE and not od.dma:
                        deps.discard(d)
            for d in deps:
                self.ops[d].need = True

    def emit(self, nc, final_wait_bufs=()):
        if LIMIT is not None:
            self.ops = self.ops[:LIMIT]
        self.analyze()
        join = Op(SP, None, (), (), False, None)
        join.idx = len(self.ops)
        for b in final_wait_bufs:
            for w in b.writers:
                join.deps.add(w)
                self.ops[w].need = True
        self.ops.append(join)

        eng_count = {PE: 0, ACT: 0, DVE: 0, POOL: 0, SP: 0}
        dma_slots = {}
        eng_sems = {}
        for o in self.ops:
            if o.dma:
                key = id(o.dbuf)
                k = dma_slots.get(key, 0) + 1
                dma_slots[key] = k
                o.tok = ("d", key, 16 * k)
            elif o.need and o.fn is not None:
                n = eng_count[o.eng]
                eng_count[o.eng] = n + 1
                o.tok = ("e", (o.eng, n // EPOCH), (n % EPOCH) + 1)
                eng_sems[(o.eng, n // EPOCH)] = True
        sem_keys = list(eng_sems.keys()) + [("dma", k) for k in dma_slots.keys()]
        es = contextlib.ExitStack()
        with es:
            sems = {}
            for i, k in enumerate(sem_keys):
                sems[k] = es.enter_context(nc.semaphore(f"s{i}"))
            block = es.enter_context(nc.Block())
            ops = self.ops

            def run_engine(engname):
                def body(e):
                    waited = {}
                    for o in ops:
                        if o.eng != engname:
                            continue
                        need = {}
                        for d in o.deps:
                            t = ops[d].tok
                            if t is None:
                                continue
                            k = (t[0], t[1])
                            if need.get(k, 0) < t[2]:
                                need[k] = t[2]
                        for k, v in need.items():
                            if waited.get(k, 0) >= v:
                                continue
                            waited[k] = v
                            s = sems[("dma", k[1])] if k[0] == "d" else sems[k[1]]
                            e.wait_ge(s, v)
                        if o.fn is None:
                            continue
                        ins = o.fn(e)
                        if o.tok is not None:
                            if o.tok[0] == "d":
                                ins.then_inc(sems[("dma", o.tok[1])], 16)
                            else:
                                ins.then_inc(sems[o.tok[1]], 1)
                return body

            block.sync(run_engine(SP))
            block.tensor(run_engine(PE))
            block.scalar(run_engine(ACT))
            block.vector(run_engine(DVE))
            block.gpsimd(run_engine(POOL))
        return len(sem_keys)


def I_mm(out, lhsT, rhs, start=True, stop=True):
    return lambda e: e.matmul(out, lhsT, rhs, start=start, stop=stop)


def I_tr(out, in_, ident):
    return lambda e: e.transpose(out, in_, ident)


def I_act(out, in_, func, bias=None, scale=None, accum=None):
    kw = {}
    if bias is not None:
        kw["bias"] = bias
    if scale is not None:
        kw["scale"] = scale
    if accum is not None:
        kw["accum_out"] = accum
    return lambda e: e.activation(out, in_, func, **kw)


def I_ts(out, in0, s1, s2, op0, op1=None):
    if op1 is None:
        return lambda e: e.tensor_scalar(out, in0, s1, None, op0)
    return lambda e: e.tensor_scalar(out, in0, s1, s2, op0, op1)


def I_tt(out, in0, in1, op):
    return lambda e: e.tensor_tensor(out, in0, in1, op)


def I_stt(out, in0, scalar, in1, op0, op1):
    return lambda e: e.scalar_tensor_tensor(out, in0, scalar, in1, op0, op1)


def I_copy(out, in_):
    return lambda e: e.tensor_copy(out, in_)


def I_recip(out, in_):
    return lambda e: e.reciprocal(out, in_)


def I_memset(ap, v):
    return lambda e: e.memset(ap, v)


def I_dma(out, in_):
    return lambda e: e.dma_start(out=out, in_=in_)


def I_scan(out, d0, d1, init, op0, op1):
    return lambda e: e.tensor_tensor_scan(out, d0, d1, init, op0, op1)


class T:
    def __init__(self, ap, name):
        self.ap = ap
        self.b = Buf(name)


class Arena:
    def __init__(self, ap, cap):
        self.ap = ap
        self.cap = cap
        self.off = 0

    def alloc(self, shape, dt, name):
        esz = 2 if dt == BF16 else 4
        n = int(np.prod(shape[1:]))
        nb = (n * esz + 63) // 64 * 64
        off = self.off
        self.off += nb
        assert self.off <= self.cap, (name, self.off, self.cap)
        self.peak = max(getattr(self, 'peak', 0), self.off)
        v = self.ap[:, off // 4:(off + nb) // 4]
        if dt == BF16:
            v = v.bitcast(BF16)
        v = v[:, 0:n]
        if len(shape) == 3:
            v = v.rearrange("p (a b) -> p a b", a=shape[1])
        elif len(shape) == 4:
            v = v.rearrange("p (a b c) -> p a b c", a=shape[1], b=shape[2])
        if shape[0] != 128:
            v = v[0:shape[0]]
        return T(v, name)


PCOL = {}
_o = 0
for _n, _w in (("gmix", 8), ("gffn", 8), ("gssd", 8), ("cws", 48), ("cbs", 12), ("dcol", 8),
               ("wq", 1), ("wk", 1), ("fb", 1), ("dtb", 16), ("alog", 16), ("cwf", 132), ("cbf", 44)):
    PCOL[_n] = (_o, _o + _w)
    _o += _w
NPAR = _o


def build_program():
    nc = bass.Bass("TRN2", target_bir_lowering=False)
    x_d = nc.dram_tensor("x", [S, D], F32, kind="ExternalInput").ap()
    win_d = nc.dram_tensor("win", [128, 8 * IN_COLS], F32, kind="ExternalInput").ap()
    wout_d = nc.dram_tensor("wout", [128, 16 * D], F32, kind="ExternalInput").ap()
    wup_d = nc.dram_tensor("wup", [128, 8 * 2 * DFF], F32, kind="ExternalInput").ap()
    wdn_d = nc.dram_tensor("wdn", [128, 22 * D], F32, kind="ExternalInput").ap()
    par_d = nc.dram_tensor("par", [128, NPAR], F32, kind="ExternalInput").ap()
    cst_d = nc.dram_tensor("cst", [128, 384], F32, kind="ExternalInput").ap()
    out_d = nc.dram_tensor("out", [S, D], F32, kind="ExternalOutput").ap()
    def skind(nm):
        return "ExternalOutput" if (DEBUG and nm in DEBUG) else "Internal"
    hT_d = nc.dram_tensor("hT_s", [128, 8 * S], BF16, kind=skind("hT_s")).ap()
    mixT_d = nc.dram_tensor("mixT_s", [128, 16 * S], BF16, kind=skind("mixT_s")).ap()
    x1_d = nc.dram_tensor("x1_s", [S, D], F32, kind=skind("x1_s")).ap()
    h2T_d = nc.dram_tensor("h2T_s", [128, 8 * S], BF16, kind=skind("h2T_s")).ap()

    win_v = win_d.rearrange("p (k c) -> p k c", k=8)
    wout_v = wout_d.rearrange("p (k c) -> p k c", k=16)
    wup_v = wup_d.rearrange("p (k c) -> p k c", k=8)
    wdn_v = wdn_d.rearrange("p (k c) -> p k c", k=22)
    hT_v = hT_d.rearrange("p (k t) -> p k t", k=8)
    mixT_v = mixT_d.rearrange("p (k t) -> p k t", k=16)
    h2T_v = h2T_d.rearrange("p (k t) -> p k t", k=8)
    x_v = x_d.rearrange("(g b p) d -> g p b d", b=4, p=128)
    x1_v = x1_d.rearrange("(g b p) d -> g p b d", b=4, p=128)
    out_v = out_d.rearrange("(g b p) d -> g p b d", b=4, p=128)

    B_hT_d = Buf("hT_d", multi=True)
    B_mixT_d = Buf("mixT_d", multi=True)
    B_x1_d = Buf("x1_d", multi=True)
    B_h2T_d = Buf("h2T_d", multi=True)
    B_out_d = Buf("out_d", multi=True)

    CAP = 212800
    arena_t = nc.alloc_sbuf_tensor("arena", [128, CAP // 4], F32)
    A = Arena(arena_t[:, :], CAP)
    banks = []
    for i in range(8):
        pt = nc.alloc_psum_tensor(f"psb{i}", [128, 512], F32)
        banks.append(T(pt[:, :], f"psb{i}"))
        banks[-1].b.psum = True

    def bf_view(bank_ap, a):
        return bank_ap.bitcast(BF16).rearrange("p (a b) -> p a b", a=a)

    P = Prog()

    par = A.alloc([128, NPAR], F32, "par")
    cst = A.alloc([128, 384], F32, "cst")
    identb = A.alloc([128, 128], BF16, "identb")
    triub = A.alloc([128, 128], BF16, "triub")
    bonesb = A.alloc([128, 128], BF16, "bonesb")
    onesb = A.alloc([128, 128], BF16, "onesb")
    ones16 = A.alloc([128, 512], F32, "ones16")
    identf = cst.ap[:, 0:128]
    triuf = cst.ap[:, 128:256]

    def pc(name, a=None, b=None):
        lo, hi = PCOL[name]
        if a is None:
            return par.ap[:, lo:hi]
        return par.ap[:, lo + a:lo + (b if b is not None else a + 1)]

    P.dma(I_dma(par.ap, par_d), writes=[par.b], sbuf=par.b)
    P.dma(I_dma(cst.ap, cst_d), writes=[cst.b], sbuf=cst.b)
    P.op(DVE, I_copy(identb.ap, cst.ap[:, 0:128]), reads=[cst.b], writes=[identb.b])
    P.op(DVE, I_copy(triub.ap, cst.ap[:, 128:256]), reads=[cst.b], writes=[triub.b])
    P.op(DVE, I_copy(bonesb.ap, cst.ap[:, 256:384]), reads=[cst.b], writes=[bonesb.b])
    P.op(POOL, I_memset(onesb.ap, 1.0), writes=[onesb.b])
    P.op(POOL, I_memset(ones16.ap, 1.0), writes=[ones16.b])
    gbase = A.off

    def rms_stage_a(xg, ss, lnv, rstd, junk):
        for b in range(4):
            P.op(ACT, I_act(junk.ap, xg.ap[:, b, :], AF.Square, accum=ss.ap[:, b:b + 1]),
                 reads=[xg.b], writes=[junk.b, ss.b])
        P.op(ACT, I_act(lnv.ap, ss.ap, AF.Ln, bias=EPS, scale=1.0 / D), reads=[ss.b], writes=[lnv.b])
        P.op(ACT, I_act(rstd.ap, lnv.ap, AF.Exp, scale=-0.5), reads=[lnv.b], writes=[rstd.b])

    def norm_transpose(xg, rstd, xn, tbanks, hst, evac_toggle):
        for b in range(4):
            P.op(DVE, I_ts(xn.ap[:, b, :], xg.ap[:, b, :], rstd.ap[:, b:b + 1], None, ALU.mult),
                 reads=[xg.b, rstd.b], writes=[xn.b])
        for b in range(4):
            bk = tbanks[b % len(tbanks)]
            pv = bf_view(bk.ap, 8)
            for kc in range(8):
                P.op(PE, I_tr(pv[:, kc, :], xn.ap[:, b, kc * 128:(kc + 1) * 128], identb.ap),
                     reads=[xn.b, identb.b], writes=[bk.b])
            eng = ACT if (b + evac_toggle) % 2 == 0 else DVE
            if eng == ACT:
                P.op(ACT, I_act(hst.ap[:, :, b * 128:(b + 1) * 128], pv, AF.Copy), reads=[bk.b], writes=[hst.b])
            else:
                P.op(DVE, I_copy(hst.ap[:, :, b * 128:(b + 1) * 128], pv), reads=[bk.b], writes=[hst.b])

    def load_weight_rows(dst, src_v, nk, c0, c1, gcol_fn, stages, cnt, scale2=1.0, chunk=1024):
        for k in range(nk):
            for cc in range(c0, c1, chunk):
                ce = min(cc + chunk, c1)
                st = stages[cnt[0] % len(stages)]
                eng = DVE if cnt[0] % 2 == 0 else ACT
                cnt[0] += 1
                P.dma(I_dma(st.ap[:, 0:ce - cc], src_v[:, k, cc:ce]), writes=[st.b], sbuf=st.b)
                g = gcol_fn(k)
                d_ap = dst.ap[:, k, cc - c0:ce - c0]
                s_ap = st.ap[:, 0:ce - cc]
                if eng == ACT:
                    if g is None:
                        P.op(ACT, I_act(d_ap, s_ap, AF.Copy), reads=[st.b], writes=[dst.b])
                    else:
                        assert scale2 == 1.0
                        P.op(ACT, I_act(d_ap, s_ap, AF.Identity, scale=g), reads=[st.b, par.b], writes=[dst.b])
                elif g is None:
                    P.op(DVE, I_copy(d_ap, s_ap), reads=[st.b], writes=[dst.b])
                else:
                    P.op(DVE, I_ts(d_ap, s_ap, g, scale2, ALU.mult, ALU.mult), reads=[st.b, par.b], writes=[dst.b])

    try:
        A.off = gbase
        xgs = [A.alloc([128, 4, D], F32, f"xg{i}") for i in range(2)]
        xns = [A.alloc([128, 4, D], BF16, f"xn{i}") for i in range(2)]
        hsts = [A.alloc([128, 8, 512], BF16, f"hst{i}") for i in range(2)]
        junk = A.alloc([128, D], BF16, "junk")
        sss = [A.alloc([128, 4], F32, f"ss{i}") for i in range(2)]
        lnvs = [A.alloc([128, 4], F32, f"lnv{i}") for i in range(2)]
        rstds = [A.alloc([128, 4], F32, f"rstd{i}") for i in range(2)]

        def p1_a(g):
            s = g % 2
            P.dma(I_dma(xgs[s].ap, x_v[g]), writes=[xgs[s].b], sbuf=xgs[s].b)
            rms_stage_a(xgs[s], sss[s], lnvs[s], rstds[s], junk)

        def p1_b(g):
            s = g % 2
            norm_transpose(xgs[s], rstds[s], xns[s], banks[0:4], hsts[s], g)
            P.dma(I_dma(hT_v[:, :, g * 512:(g + 1) * 512], hsts[s].ap), reads=[hsts[s].b], writes=[B_hT_d], sbuf=hsts[s].b, eng=POOL)

        p1_a(0)
        for g in range(NT):
            if g + 1 < NT:
                p1_a(g + 1)
            p1_b(g)

        if UPTO < 2:
            raise StopBuild()
        P.barrier()
        A.off = gbase
        wssd = A.alloc([128, 8, 2560], BF16, "wssd")
        wdt = A.alloc([128, 8, 16], BF16, "wdt")
        wdts = A.alloc([128, 8, 16], F32, "wdts")
        wst = [A.alloc([128, 1024], F32, f"wst{i}") for i in range(4)]
        cwh = A.alloc([128, 48], F32, "cwh")
        cbh = A.alloc([128, 12], F32, "cbh")
        a_b = A.alloc([128, 16], F32, "a_b")
        hTt = [A.alloc([128, 8, 512], BF16, f"hTt{i}") for i in range(2)]
        szT = A.alloc([128, 8, 512], BF16, "szT")
        xpads = [A.alloc([128, 515], F32, f"xpad{i}") for i in range(12)]
        xcT = [A.alloc([128, 512], BF16, f"xcT{i}") for i in range(12)]
        cacc = [A.alloc([128, 512], F32, f"cacc{i}") for i in range(3)]
        thb = [A.alloc([128, 512], F32, f"thb{i}") for i in range(2)]
        dtx = A.alloc([128, 64], F32, "dtx")
        dt_t = A.alloc([128, 4, 16], F32, "dt_t")
        dtA_t = A.alloc([128, 4, 16], F32, "dtA_t")
        nacs = A.alloc([128, 16], F32, "nacs")
        dtmp = A.alloc([128, 16], F32, "dtmp")
        cdl = A.alloc([128, 16], F32, "cdl")
        dte = A.alloc([128, 16], F32, "dte")
        cds = [A.alloc([128, 16], F32, f"cd{i}") for i in range(2)]
        dhi = A.alloc([128, 4, 16], BF16, "dhi")
        dlo = A.alloc([128, 4, 16], BF16, "dlo")
        dres = A.alloc([128, 4, 16], F32, "dres")
        Dhi = A.alloc([128, 8, 128], BF16, "Dhi")
        Dlo = A.alloc([128, 8, 128], BF16, "Dlo")
        w2 = A.alloc([128, 16], F32, "w2")
        ET = A.alloc([128, 16, 128], BF16, "ET")
        ETp = A.alloc([128, 16, 128], F32, "ETp")
        EA = A.alloc([128, 16, 128], BF16, "EA")
        CBTm = A.alloc([128, 2, 128], F32, "CBTm")
        scT = A.alloc([128, 16, 128], BF16, "scT")
        xdt = A.alloc([128, 16, 64], BF16, "xdt")
        xdte = A.alloc([128, 16, 64], BF16, "xdte")
        Btok = A.alloc([128, 2, 128], BF16, "Btok")
        Csc = A.alloc([128, 16, 128], BF16, "Csc")
        Sst = A.alloc([128, 16, 64], F32, "Sst")
        Sbf = A.alloc([128, 16, 64], BF16, "Sbf")
        yt = A.alloc([128, 8, 128], F32, "yt")
        sqb = A.alloc([128, 8, 128], BF16, "sqb")
        lnr = A.alloc([128, 2, 128], F32, "lnr")
        rsr = A.alloc([128, 2, 128], F32, "rsr")
        ystage = A.alloc([128, 8, 512], BF16, "ystage")

        cnt = [0]
        load_weight_rows(wssd, win_v, 8, 0, 1024, lambda k: pc("gmix", k), wst, cnt, scale2=1.0)
        wx = T(wssd.ap[:, :, 1024:2560], "wssd_x")
        wx.b = wssd.b
        load_weight_rows(wx, win_v, 8, 1024, 2560, lambda k: pc("gmix", k), wst, cnt, scale2=1.0, chunk=768)
        P.dma(I_dma(wdts.ap, win_v[:, :, 2560:2576]), writes=[wdts.b], sbuf=wdts.b)
        P.op(DVE, I_tt(wdt.ap, wdts.ap, pc("gmix").unsqueeze(2).to_broadcast([128, 8, 16]), ALU.mult),
             reads=[wdts.b, par.b], writes=[wdt.b])
        P.op(DVE, I_ts(cwh.ap, pc("cws"), 0.5, None, ALU.mult), reads=[par.b], writes=[cwh.b])
        P.op(DVE, I_ts(cbh.ap, pc("cbs"), 0.5, None, ALU.mult), reads=[par.b], writes=[cbh.b])
        P.op(ACT, I_act(a_b.ap, pc("alog"), AF.Exp), reads=[par.b], writes=[a_b.b])
        P.op(DVE, I_ts(a_b.ap, a_b.ap, -1.0, None, ALU.mult), reads=[a_b.b], writes=[a_b.b])
        P.op(POOL, I_memset(Sst.ap, 0.0), writes=[Sst.b])
        P.op(POOL, I_memset(Sbf.ap, 0.0), writes=[Sbf.b])
        P.op(DVE, I_tt(yt.ap, identf.unsqueeze(1).to_broadcast([128, 8, 128]), pc("dcol").unsqueeze(2).to_broadcast([128, 8, 128]), ALU.mult),
             reads=[cst.b, par.b], writes=[yt.b])
        P.op(DVE, I_copy(Dhi.ap, yt.ap), reads=[yt.b], writes=[Dhi.b])
        P.op(DVE, I_tt(yt.ap, yt.ap, Dhi.ap, ALU.subtract), reads=[yt.b, Dhi.b], writes=[yt.b])
        P.op(DVE, I_copy(Dlo.ap, yt.ap), reads=[yt.b], writes=[Dlo.b])
        for i in range(12):
            P.op(POOL, I_memset(xpads[i].ap[:, 0:3], 0.0), writes=[xpads[i].b])

        pb_misc = banks[0]
        psD = T(pb_misc.ap[:, 0:64], "psD")
        psA = T(pb_misc.ap[:, 64:80], "psA")
        psC = T(pb_misc.ap[:, 128:384], "psC")
        psBt = T(pb_misc.ap[:, 384:512], "psBt")
        psSS = T(banks[3].ap[:, 256:512], "psSS")
        psR = [banks[1], banks[2]]
        psTb = T(banks[3].ap[:, 0:256], "psTx")
        psT = banks[7]
        psY = [banks[4], banks[5]]
        psSt = [banks[6], banks[3]]
        psSS.b = banks[3].b
        psD.b = psA.b = psC.b = psBt.b = pb_misc.b
        projbanks = [banks[1], banks[2], banks[4], banks[5], banks[6], banks[7]]
        pcount = [0]

        def ssd_tile_load(t):
            s = t % 2
            P.dma(I_dma(hTt[s].ap, hT_v[:, :, t * 512:(t + 1) * 512]), reads=[B_hT_d], writes=[hTt[s].b], sbuf=hTt[s].b)

        ssd_tile_load(0)
        for t in range(NT):
            hs = hTt[t % 2]
            if t + 1 < NT:
                ssd_tile_load(t + 1)
            for c in range(4):
                for kc in range(8):
                    P.op(PE, I_mm(psD.ap[:, c * 16:(c + 1) * 16], hs.ap[:, kc, c * 128:(c + 1) * 128], wdt.ap[:, kc, :],
                                  start=(kc == 0), stop=(kc == 7)), reads=[hs.b, wdt.b], writes=[psD.b])
            P.op(DVE, I_tt(dtx.ap.rearrange("p (c h) -> p c h", c=4), psD.ap.rearrange("p (c h) -> p c h", c=4),
                           pc("dtb").unsqueeze(1).to_broadcast([128, 4, 16]), ALU.add),
                 reads=[psD.b, par.b], writes=[dtx.b])
            P.op(ACT, I_act(dtx.ap, dtx.ap, AF.Exp), reads=[dtx.b], writes=[dtx.b])
            P.op(ACT, I_act(dt_t.ap.rearrange("p c h -> p (c h)"), dtx.ap, AF.Ln, bias=1.0, scale=1.0), reads=[dtx.b], writes=[dt_t.b])
            P.op(DVE, I_tt(dtA_t.ap, dt_t.ap, a_b.ap.unsqueeze(1).to_broadcast([128, 4, 16]), ALU.mult),
                 reads=[dt_t.b, a_b.b], writes=[dtA_t.b])
            P.op(DVE, I_copy(dhi.ap, dtA_t.ap), reads=[dtA_t.b], writes=[dhi.b])
            P.op(DVE, I_tt(dres.ap, dtA_t.ap, dhi.ap, ALU.subtract), reads=[dtA_t.b, dhi.b], writes=[dres.b])
            P.op(DVE, I_copy(dlo.ap, dres.ap), reads=[dres.b], writes=[dlo.b])
            for zb in range(8):
                bk = projbanks[pcount[0] % len(projbanks)]
                pcount[0] += 1
                for kc in range(8):
                    P.op(PE, I_mm(bk.ap, wssd.ap[:, kc, zb * 128:(zb + 1) * 128], hs.ap[:, kc, :], start=(kc == 0), stop=(kc == 7)),
                         reads=[wssd.b, hs.b], writes=[bk.b])
                P.op(ACT, I_act(szT.ap[:, zb, :], bk.ap, AF.Silu), reads=[bk.b], writes=[szT.b])
            def xbc_A(xb):
                bk = projbanks[pcount[0] % len(projbanks)]
                pcount[0] += 1
                for kc in range(8):
                    P.op(PE, I_mm(bk.ap, wssd.ap[:, kc, 1024 + xb * 128:1024 + (xb + 1) * 128], hs.ap[:, kc, :],
                                  start=(kc == 0), stop=(kc == 7)), reads=[wssd.b, hs.b], writes=[bk.b])
                xp = xpads[xb]
                ac = cacc[xb % 3]
                P.op(ACT, I_act(ac.ap, bk.ap, AF.Identity, bias=pc("cbs", xb), scale=pc("cws", xb * 4 + 3)),
                     reads=[bk.b, par.b], writes=[ac.b])
                P.op(ACT, I_act(xp.ap[:, 3:515], bk.ap, AF.Copy), reads=[bk.b], writes=[xp.b])

            def xbc_B(xb):
                xp = xpads[xb]
                ac = cacc[xb % 3]
                for k in range(3):
                    P.op(DVE, I_stt(ac.ap, xp.ap[:, k:k + 512], pc("cws", xb * 4 + k), ac.ap, ALU.mult, ALU.add),
                         reads=[xp.b, par.b, ac.b], writes=[ac.b])
                P.op(POOL, I_copy(xp.ap[:, 0:3], xp.ap[:, 512:515]), reads=[xp.b], writes=[xp.b])

            def xbc_C(xb):
                ac = cacc[xb % 3]
                P.op(ACT, I_act(xcT[xb].ap, ac.ap, AF.Silu), reads=[ac.b], writes=[xcT[xb].b])

            for step in range(12 + 2):
                if step < 12:
                    xbc_A(step)
                if 0 <= step - 1 < 12:
                    xbc_B(step - 1)
                if 0 <= step - 2 < 12:
                    xbc_C(step - 2)
            def ch_prep(c):
                P.op(PE, I_mm(psA.ap, triuf, dtA_t.ap[:, c, :]), reads=[cst.b, dtA_t.b], writes=[psA.b])
                P.op(DVE, I_ts(nacs.ap, psA.ap, -1.0, None, ALU.mult), reads=[psA.b], writes=[nacs.b])
                for rb in range(4):
                    bk = psR[rb % 2]
                    for hh in range(4):
                        h = rb * 4 + hh
                        o_ap = bk.ap[:, hh * 128:(hh + 1) * 128]
                        P.op(PE, I_mm(o_ap, dhi.ap[:, c, h:h + 1].to_broadcast([128, 128]), triub.ap, start=True, stop=False),
                             reads=[dhi.b, triub.b], writes=[bk.b])
                        P.op(PE, I_mm(o_ap, dlo.ap[:, c, h:h + 1].to_broadcast([128, 128]), triub.ap, start=False, stop=True),
                             reads=[dlo.b, triub.b], writes=[bk.b])
                    P.op(DVE, I_tt(ETp.ap[:, rb * 4:rb * 4 + 4, :], bk.ap.rearrange("p (a b) -> p a b", a=4),
                                   nacs.ap[:, rb * 4:rb * 4 + 4].unsqueeze(2).to_broadcast([128, 4, 128]), ALU.add),
                         reads=[bk.b, nacs.b], writes=[ETp.b])
                    P.op(ACT, I_act(EA.ap[:, rb * 4:rb * 4 + 4, :].rearrange("p a b -> p (a b)"), bk.ap, AF.Exp),
                         reads=[bk.b], writes=[EA.b])
                    last = bk.ap.rearrange("p (a b) -> p a b", a=4)[:, :, 127:128].rearrange("p a b -> p (a b)")
                    P.op(DVE, I_tt(dtmp.ap[:, rb * 4:rb * 4 + 4], last, nacs.ap[:, rb * 4:rb * 4 + 4], ALU.add),
                         reads=[bk.b, nacs.b], writes=[dtmp.b])
                    P.op(DVE, I_copy(cdl.ap[:, rb * 4:rb * 4 + 4], last), reads=[bk.b], writes=[cdl.b])
                P.op(ACT, I_act(ETp.ap.rearrange("p a b -> p (a b)"), ETp.ap.rearrange("p a b -> p (a b)"), AF.Abs),
                     reads=[ETp.b], writes=[ETp.b])
                P.op(ACT, I_act(ET.ap.rearrange("p a b -> p (a b)"), ETp.ap.rearrange("p a b -> p (a b)"), AF.Exp, scale=-1.0),
                     reads=[ETp.b], writes=[ET.b])
                P.op(ACT, I_act(dte.ap, dtmp.ap, AF.Exp), reads=[dtmp.b], writes=[dte.b])
                P.op(ACT, I_act(cds[c % 2].ap, cdl.ap, AF.Exp), reads=[cdl.b], writes=[cds[c % 2].b])

            def ch_mid(c):
                cs = slice(c * 128, (c + 1) * 128)
                for g in range(2):
                    P.op(PE, I_mm(psC.ap[:, g * 128:(g + 1) * 128], xcT[8 + g].ap[:, cs], xcT[10 + g].ap[:, cs]),
                         reads=[xcT[8 + g].b, xcT[10 + g].b], writes=[psC.b])
                P.op(DVE, I_tt(CBTm.ap, psC.ap.rearrange("p (g l) -> p g l", g=2), triuf.unsqueeze(1).to_broadcast([128, 2, 128]), ALU.mult),
                     reads=[psC.b, cst.b], writes=[CBTm.b])
                for g in range(2):
                    P.op(DVE, I_tt(scT.ap[:, 8 * g:8 * g + 8, :], ET.ap[:, 8 * g:8 * g + 8, :],
                                   CBTm.ap[:, g:g + 1, :].to_broadcast([128, 8, 128]), ALU.mult),
                         reads=[ET.b, CBTm.b], writes=[scT.b])
                for g in range(2):
                    P.op(POOL, I_tt(Csc.ap[:, 8 * g:8 * g + 8, :], EA.ap[:, 8 * g:8 * g + 8, :],
                                    xcT[10 + g].ap[:, cs].unsqueeze(1).to_broadcast([128, 8, 128]), ALU.mult),
                         reads=[EA.b, xcT[10 + g].b], writes=[Csc.b])
                pvT = bf_view(psT.ap, 8)
                for blk in range(8):
                    P.op(PE, I_tr(pvT[:, blk, :], xcT[blk].ap[:, cs], identb.ap), reads=[xcT[blk].b, identb.b], writes=[psT.b])
                P.op(DVE, I_tt(w2.ap, dt_t.ap[:, c, :], dte.ap, ALU.mult), reads=[dt_t.b, dte.b], writes=[w2.b])
                pvT16 = psT.ap.bitcast(BF16).rearrange("p (h d) -> p h d", h=16)
                P.op(DVE, I_tt(xdt.ap, pvT16, dt_t.ap[:, c, :].unsqueeze(2).to_broadcast([128, 16, 64]), ALU.mult),
                     reads=[psT.b, dt_t.b], writes=[xdt.b])
                P.op(DVE, I_tt(xdte.ap, pvT16, w2.ap.unsqueeze(2).to_broadcast([128, 16, 64]), ALU.mult),
                     reads=[psT.b, w2.b], writes=[xdte.b])
                pvB = psBt.ap.bitcast(BF16).rearrange("p (g n) -> p g n", g=2)
                for g in range(2):
                    P.op(PE, I_tr(pvB[:, g, :], xcT[8 + g].ap[:, cs], identb.ap), reads=[xcT[8 + g].b, identb.b], writes=[psBt.b])
                P.op(ACT, I_act(Btok.ap, pvB, AF.Copy), reads=[psBt.b], writes=[Btok.b])

            def ch_fin(c):
                cs = slice(c * 128, (c + 1) * 128)
                for hp in range(8):
                    bk = psY[hp // 4]
                    col = (hp % 4) * 128
                    full = bk.ap[:, col:col + 128]
                    P.op(PE, I_mm(full, Dhi.ap[:, hp, :], xcT[hp].ap[:, cs], start=True, stop=False),
                         reads=[Dhi.b, xcT[hp].b], writes=[bk.b])
                    P.op(PE, I_mm(full, Dlo.ap[:, hp, :], xcT[hp].ap[:, cs], start=False, stop=False),
                         reads=[Dlo.b, xcT[hp].b], writes=[bk.b])
                    for half in range(2):
                        h = 2 * hp + half
                        o_ap = bk.ap[64 * half:64 * half + 64, col:col + 128]
                        P.op(PE, I_mm(o_ap, xdt.ap[:, h, :], scT.ap[:, h, :], start=False, stop=False),
                             reads=[xdt.b, scT.b], writes=[bk.b])
                        P.op(PE, I_mm(o_ap, Sbf.ap[:, h, :], Csc.ap[:, h, :], start=False, stop=(half == 1)),
                             reads=[Sbf.b, Csc.b], writes=[bk.b])
                for k in range(2):
                    P.op(DVE, I_tt(yt.ap[:, 4 * k:4 * k + 4, :], psY[k].ap.rearrange("p (a b) -> p a b", a=4),
                                   szT.ap[:, 4 * k:4 * k + 4, cs], ALU.mult),
                         reads=[psY[k].b, szT.b], writes=[yt.b])
                P.op(ACT, I_act(sqb.ap, yt.ap, AF.Square), reads=[yt.b], writes=[sqb.b])
                for g in range(2):
                    for j in range(4):
                        P.op(PE, I_mm(psSS.ap[:, g * 128:(g + 1) * 128], onesb.ap, sqb.ap[:, 4 * g + j, :], start=(j == 0), stop=(j == 3)),
                             reads=[onesb.b, sqb.b], writes=[psSS.b])
                P.op(ACT, I_act(lnr.ap.rearrange("p g l -> p (g l)"), psSS.ap, AF.Ln, bias=EPS, scale=1.0 / 512),
                     reads=[psSS.b], writes=[lnr.b])
                P.op(ACT, I_act(rsr.ap, lnr.ap, AF.Exp, scale=-0.5), reads=[lnr.b], writes=[rsr.b])
                for g in range(2):
                    P.op(POOL, I_tt(ystage.ap[:, 4 * g:4 * g + 4, cs], yt.ap[:, 4 * g:4 * g + 4, :],
                                    rsr.ap[:, g:g + 1, :].to_broadcast([128, 4, 128]), ALU.mult),
                         reads=[yt.b, rsr.b], writes=[ystage.b])
                for g in range(2):
                    bk = psSt[g]
                    P.op(PE, I_mm(bk.ap, Btok.ap[:, g, :], xdte.ap[:, 8 * g:8 * g + 8, :].rearrange("p h d -> p (h d)")),
                         reads=[Btok.b, xdte.b], writes=[bk.b])
                cdc = cds[c % 2]
                for g in range(2):
                    bk = psSt[g]
                    sv = Sst.ap[:, 8 * g:8 * g + 8, :]
                    P.op(DVE, I_tt(sv, sv, cdc.ap[:, 8 * g:8 * g + 8].unsqueeze(2).to_broadcast([128, 8, 64]), ALU.mult),
                         reads=[Sst.b, cdc.b], writes=[Sst.b])
                    P.op(DVE, I_tt(sv, sv, bk.ap.rearrange("p (h d) -> p h d", h=8), ALU.add),
                         reads=[Sst.b, bk.b], writes=[Sst.b])
                P.op(ACT, I_act(Sbf.ap, Sst.ap, AF.Copy), reads=[Sst.b], writes=[Sbf.b])

            ch_prep(0)
            for c in range(4):
                ch_mid(c)
                if c + 1 < 4:
                    ch_prep(c + 1)
                ch_fin(c)
            P.dma(I_dma(mixT_v[:, 0:8, t * 512:(t + 1) * 512], ystage.ap), reads=[ystage.b], writes=[B_mixT_d], sbuf=ystage.b, eng=POOL)

        if UPTO < 3:
            raise StopBuild()
        P.barrier()
        A.off = gbase
        hT = A.alloc([128, 8, S], BF16, "hT")
        hT_k = []
        for kc in range(8):
            t_ = T(hT.ap[:, kc, :], f"hT{kc}")
            hT_k.append(t_)
        cumT = A.alloc([16, S], F32, "cumT")
        ftmp = A.alloc([16, S], F32, "ftmp")
        cumbf = A.alloc([16, S], BF16, "cumbf")
        ncum = A.alloc([128, 32, 16], F32, "ncum")
        nfb = A.alloc([128, 1], F32, "nfb")
        wq8 = A.alloc([128, 1], F32, "wq8")
        wfs = A.alloc([128, 8, 16], F32, "wfs")
        wfb = A.alloc([128, 8, 16], BF16, "wfb")
        wqs = [A.alloc([128, 8, 128], F32, f"wqs{i}") for i in range(3)]
        wqb = [A.alloc([128, 8, 128], BF16, f"wqb{i}") for i in range(3)]
        qA = A.alloc([128, S], BF16, "qA")
        qB = A.alloc([128, S], BF16, "qB")
        kA = A.alloc([128, S], BF16, "kA")
        kB = A.alloc([128, S], BF16, "kB")
        vA = A.alloc([128, 32, 128], BF16, "vA")
        vB = A.alloc([128, 32, 128], BF16, "vB")
        sq2 = [A.alloc([128, 512], BF16, f"sq2{i}") for i in range(3)]
        lnq = [A.alloc([128, 512], F32, f"lnq{i}") for i in range(3)]
        rsq = [A.alloc([128, 512], F32, f"rsq{i}") for i in range(3)]
        NPT = 4
        PT = [A.alloc([128, 512], BF16, f"PT{i}") for i in range(NPT)]
        rr = [A.alloc([128, 512], F32, f"rr{i}") for i in range(2)]
        ost = [A.alloc([128, 512], BF16, f"ost{i}") for i in range(2)]

        for kc in range(8):
            P.dma(I_dma(hT_k[kc].ap, hT_v[:, kc, :]), reads=[B_hT_d], writes=[hT_k[kc].b], sbuf=hT_k[kc].b)
        hT_bufs = [t_.b for t_ in hT_k]

        P.op(DVE, I_ts(nfb.ap, pc("fb"), -1.0, None, ALU.mult), reads=[par.b], writes=[nfb.b])
        P.op(DVE, I_ts(wq8.ap, pc("wq"), 0.125, None, ALU.mult), reads=[par.b], writes=[wq8.b])
        for tl in (qA, qB, kA, kB):
            P.op(POOL, I_memset(tl.ap, 0.0), writes=[tl.b])
        P.op(POOL, I_memset(kA.ap[64:65, :], 1.0), writes=[kA.b])
        P.op(POOL, I_memset(kB.ap[0:1, :], 1.0), writes=[kB.b])
        P.op(POOL, I_memset(vA.ap, 1.0), writes=[vA.b])
        P.op(POOL, I_memset(vB.ap, 1.0), writes=[vB.b])

        P.dma(I_dma(wfs.ap, win_v[:, :, 5648:5664]), writes=[wfs.b], sbuf=wfs.b)
        P.op(DVE, I_tt(wfb.ap, wfs.ap, pc("gmix").unsqueeze(2).to_broadcast([128, 8, 16]), ALU.mult),
             reads=[wfs.b, par.b], writes=[wfb.b])
        for t in range(NT):
            bk = banks[t % 2]
            tsl = slice(t * 512, (t + 1) * 512)
            for kc in range(8):
                P.op(PE, I_mm(bk.ap[0:16, :], wfb.ap[:, kc, :], hT_k[kc].ap[:, tsl], start=(kc == 0), stop=(kc == 7)),
                     reads=[wfb.b, hT_k[kc].b], writes=[bk.b])
            P.op(ACT, I_act(ftmp.ap[:, tsl], bk.ap[0:16, :], AF.Exp, bias=nfb.ap[0:16, :], scale=-1.0),
                 reads=[bk.b, nfb.b], writes=[ftmp.b])
            P.op(ACT, I_act(ftmp.ap[:, tsl], ftmp.ap[:, tsl], AF.Ln, bias=1.0, scale=1.0), reads=[ftmp.b], writes=[ftmp.b])
            init = 0.0 if t == 0 else cumT.ap[:, t * 512 - 1:t * 512]
            P.op(DVE, I_scan(cumT.ap[:, tsl], ones16.ap[0:16, :], ftmp.ap[:, tsl], init, ALU.mult, ALU.subtract),
                 reads=[ones16.b, ftmp.b, cumT.b], writes=[cumT.b])
        P.op(POOL, I_copy(cumbf.ap, cumT.ap), reads=[cumT.b], writes=[cumbf.b])
        pN = banks[2]
        for b in range(32):
            P.op(PE, I_tr(pN.ap[:, b * 16:(b + 1) * 16], cumT.ap[:, b * 128:(b + 1) * 128], identf[0:16, 0:16]),
                 reads=[cumT.b, cst.b], writes=[pN.b])
        P.op(DVE, I_ts(ncum.ap.rearrange("p b h -> p (b h)"), pN.ap, -1.0, None, ALU.mult), reads=[pN.b], writes=[ncum.b])

        psQ2 = [(banks[0], banks[1]), (banks[2], banks[3]), (banks[6], banks[7])]
        psVs = [banks[4], banks[5]]
        psS = [banks[2], banks[3], banks[4]]
        psO = [banks[5], banks[6], banks[7]]
        ocnt = [0]
        qcnt = [0]

        for hp in range(8):
            def w_cols(p_):
                return [2576 + p_ * 128, 3600 + p_ * 128, 4624 + p_ * 128]
            if hp == 0:
                for i in range(3):
                    P.dma(I_dma(wqs[i].ap, win_v[:, :, w_cols(0)[i]:w_cols(0)[i] + 128]), writes=[wqs[i].b], sbuf=wqs[i].b)
            for i in range(3):
                P.op(POOL if i == 1 else DVE, I_tt(wqb[i].ap, wqs[i].ap, pc("gmix").unsqueeze(2).to_broadcast([128, 8, 128]), ALU.mult),
                     reads=[wqs[i].b, par.b], writes=[wqb[i].b])
            if hp + 1 < 8:
                for i in range(3):
                    c_ = w_cols(hp + 1)[i]
                    P.dma(I_dma(wqs[i].ap, win_v[:, :, c_:c_ + 128]), writes=[wqs[i].b], sbuf=wqs[i].b)
            P.dma(I_dma(qA.ap[64:65, :], cumbf.ap[2 * hp:2 * hp + 1, :]), reads=[cumbf.b], writes=[qA.b], sbuf=qA.b)
            P.dma(I_dma(qB.ap[0:1, :], cumbf.ap[2 * hp + 1:2 * hp + 2, :]), reads=[cumbf.b], writes=[qB.b], sbuf=qB.b)
            qk_items = [(which, t) for which in range(2) for t in range(NT)]

            def qk_A(idx):
                which, t = qk_items[idx]
                bk, _ = psQ2[idx % 3]
                tsl = slice(t * 512, (t + 1) * 512)
                for kc in range(8):
                    P.op(PE, I_mm(bk.ap, wqb[which].ap[:, kc, :], hT_k[kc].ap[:, tsl], start=(kc == 0), stop=(kc == 7)),
                         reads=[wqb[which].b, hT_k[kc].b], writes=[bk.b])

            def qk_B(idx):
                which, t = qk_items[idx]
                XA, XB = (qA, qB) if which == 0 else (kA, kB)
                wcol = wq8.ap if which == 0 else pc("wk")
                wcol_b = wq8.b if which == 0 else par.b
                bk, bk2 = psQ2[idx % 3]
                tsl = slice(t * 512, (t + 1) * 512)
                sq = sq2[idx % 3]
                ln_ = lnq[idx % 3]
                rs_ = rsq[idx % 3]
                P.op(ACT, I_act(sq.ap, bk.ap, AF.Square), reads=[bk.b], writes=[sq.b])
                P.op(PE, I_mm(bk2.ap, bonesb.ap, sq.ap), reads=[bonesb.b, sq.b], writes=[bk2.b])
                P.op(ACT, I_act(ln_.ap, bk2.ap, AF.Ln, bias=EPS, scale=1.0 / 64), reads=[bk2.b], writes=[ln_.b])
                P.op(ACT, I_act(rs_.ap, ln_.ap, AF.Exp, scale=-0.5), reads=[ln_.b], writes=[rs_.b])
                P.op(DVE, I_stt(XA.ap[0:64, tsl], bk.ap[0:64, :], wcol[0:64, :], rs_.ap[0:64, :], ALU.mult, ALU.mult),
                     reads=[bk.b, wcol_b, rs_.b], writes=[XA.b])
                P.op(DVE, I_stt(XB.ap[64:128, tsl], bk.ap[64:128, :], wcol[64:128, :], rs_.ap[64:128, :], ALU.mult, ALU.mult),
                     reads=[bk.b, wcol_b, rs_.b], writes=[XB.b])

            qk_A(0)
            for idx in range(len(qk_items)):
                if idx + 1 < len(qk_items):
                    qk_A(idx + 1)
                qk_B(idx)
            for bq in range(8):
                psV = psVs[bq % 2]
                pvv = psV.ap.rearrange("p (j c) -> p j c", j=4)
                for j in range(4):
                    b = 4 * bq + j
                    for kc in range(8):
                        P.op(PE, I_mm(psV.ap[:, j * 128:(j + 1) * 128], hT_k[kc].ap[:, b * 128:(b + 1) * 128], wqb[2].ap[:, kc, :],
                                      start=(kc == 0), stop=(kc == 7)), reads=[hT_k[kc].b, wqb[2].b], writes=[psV.b])
                P.op(ACT, I_act(vA.ap[:, 4 * bq:4 * bq + 4, 0:64], pvv[:, :, 0:64], AF.Copy), reads=[psV.b], writes=[vA.b])
                P.op(DVE, I_copy(vB.ap[:, 4 * bq:4 * bq + 4, 64:128], pvv[:, :, 64:128]), reads=[psV.b], writes=[vB.b])
            seq = []
            for i in range(NT):
                for hd in range(2):
                    nj = 4 * i + 4
                    for j in range(nj):
                        seq.append((i, hd, j, nj))

            def emit_S(n):
                i, hd, j, nj = seq[n]
                r = j - 4 * i
                c0 = max(0, r) * 128
                N = 512 - c0
                Kf = kA if hd == 0 else kB
                Qf = qA if hd == 0 else qB
                bk = psS[n % 3]
                P.op(PE, I_mm(bk.ap[:, 0:N], Kf.ap[:, j * 128:(j + 1) * 128], Qf.ap[:, i * 512 + c0:(i + 1) * 512]),
                     reads=[Kf.b, Qf.b], writes=[bk.b])

            def emit_PV(n):
                i, hd, j, nj = seq[n]
                r = j - 4 * i
                c0 = max(0, r) * 128
                N = 512 - c0
                h = 2 * hp + hd
                bk = psS[n % 3]
                pt = PT[n % NPT]
                if j == 0:
                    ocnt[0] += 1
                ob = psO[ocnt[0] % 3]
                P.op(ACT, I_act(pt.ap[:, 0:N], bk.ap[:, 0:N], AF.Exp, bias=ncum.ap[:, j, h:h + 1], scale=1.0),
                     reads=[bk.b, ncum.b], writes=[pt.b])
                if r >= 0:
                    P.op(POOL, I_tt(pt.ap[:, 0:128], pt.ap[:, 0:128], triub.ap, ALU.mult), reads=[pt.b, triub.b], writes=[pt.b])
                V = vA if hd == 0 else vB
                P.op(PE, I_mm(ob.ap[:, c0:512], V.ap[:, j, :], pt.ap[:, 0:N], start=(j == 0), stop=(j == nj - 1)),
                     reads=[V.b, pt.b], writes=[ob.b])
                if j == nj - 1:
                    os_ = ost[i % 2]
                    rt = rr[hd]
                    if hd == 0:
                        P.op(DVE, I_recip(rt.ap[64:128, :], ob.ap[64:128, :]), reads=[ob.b], writes=[rt.b])
                        P.op(DVE, I_tt(os_.ap[0:64, :], ob.ap[0:64, :], rt.ap[64:128, :], ALU.mult), reads=[ob.b, rt.b], writes=[os_.b])
                    else:
                        P.op(DVE, I_recip(rt.ap[0:64, :], ob.ap[0:64, :]), reads=[ob.b], writes=[rt.b])
                        P.op(DVE, I_tt(os_.ap[64:128, :], ob.ap[64:128, :], rt.ap[0:64, :], ALU.mult), reads=[ob.b, rt.b], writes=[os_.b])
                        P.dma(I_dma(mixT_v[:, 8 + hp, i * 512:(i + 1) * 512], os_.ap), reads=[os_.b], writes=[B_mixT_d], sbuf=os_.b)

            LOOK = 3
            for n in range(min(LOOK, len(seq))):
                emit_S(n)
            for n in range(len(seq)):
                emit_PV(n)
                if n + LOOK < len(seq):
                    emit_S(n + LOOK)

        if UPTO < 4:
            raise StopBuild()
        P.barrier()
        A.off = gbase
        wout = A.alloc([128, 16, D], BF16, "wout")
        wst3 = [A.alloc([128, 1024], F32, f"wst3{i}") for i in range(4)]
        mixt = [A.alloc([128, 16, 512], BF16, f"mixt{i}") for i in range(2)]
        xg3 = [A.alloc([128, 4, D], F32, f"xg3{i}") for i in range(2)]
        xn3 = [A.alloc([128, 4, D], BF16, f"xn3{i}") for i in range(2)]
        hst3 = [A.alloc([128, 8, 512], BF16, f"hst3{i}") for i in range(2)]
        junk3 = A.alloc([128, D], BF16, "junk3")
        ss3 = [A.alloc([128, 4], F32, f"ss3{i}") for i in range(2)]
        lnv3 = [A.alloc([128, 4], F32, f"lnv3{i}") for i in range(2)]
        rstd3 = [A.alloc([128, 4], F32, f"rstd3{i}") for i in range(2)]
        cnt3 = [0]
        load_weight_rows(wout, wout_v, 16, 0, D, lambda k: (pc("gssd", k) if k < 8 else None), wst3, cnt3)

        def p3_load(t):
            s = t % 2
            P.dma(I_dma(mixt[s].ap, mixT_v[:, :, t * 512:(t + 1) * 512]), reads=[B_mixT_d], writes=[mixt[s].b], sbuf=mixt[s].b)
            P.dma(I_dma(xg3[s].ap, x_v[t]), writes=[xg3[s].b], sbuf=xg3[s].b)

        def p3_a(t):
            s = t % 2
            n = 0
            for b in range(4):
                for half in range(2):
                    bk = banks[4 + (n % 4)]
                    n += 1
                    for fc in range(16):
                        P.op(PE, I_mm(bk.ap, mixt[s].ap[:, fc, b * 128:(b + 1) * 128], wout.ap[:, fc, half * 512:(half + 1) * 512],
                                      start=(fc == 0), stop=(fc == 15)), reads=[mixt[s].b, wout.b], writes=[bk.b])
                    xs_ = xg3[s].ap[:, b, half * 512:(half + 1) * 512]
                    P.op(DVE, I_tt(xs_, xs_, bk.ap, ALU.add), reads=[xg3[s].b, bk.b], writes=[xg3[s].b])
            P.dma(I_dma(x1_v[t], xg3[s].ap), reads=[xg3[s].b], writes=[B_x1_d], sbuf=xg3[s].b, eng=POOL)
            rms_stage_a(xg3[s], ss3[s], lnv3[s], rstd3[s], junk3)

        def p3_b(t):
            s = t % 2
            norm_transpose(xg3[s], rstd3[s], xn3[s], banks[0:4], hst3[s], t)
            P.dma(I_dma(h2T_v[:, :, t * 512:(t + 1) * 512], hst3[s].ap), reads=[hst3[s].b], writes=[B_h2T_d], sbuf=hst3[s].b, eng=POOL)

        p3_load(0)
        p3_load(1)
        p3_a(0)
        for t in range(NT):
            if t + 1 < NT:
                p3_a(t + 1)
            p3_b(t)
            if t + 2 < NT:
                p3_load(t + 2)

        if UPTO < 5:
            raise StopBuild()
        P.barrier()
        A.off = gbase
        wup = A.alloc([128, 8, 2 * DFF], BF16, "wup")
        wdn = A.alloc([128, 22, D], BF16, "wdn")
        wst4 = [A.alloc([128, 512], F32, f"wst4{i}") for i in range(4)]
        cwf = A.alloc([128, 132], F32, "cwf")
        cbf = A.alloc([128, 44], F32, "cbf")
        halo = A.alloc([128, 44, 2], F32, "halo")
        h2t = [A.alloc([128, 8, 512], BF16, "h2t0")]
        x1t = A.alloc([128, 2, D], F32, "x1t")
        gpad = [A.alloc([128, 514], F32, f"gpad{i}") for i in range(2)]
        vpad = [A.alloc([128, 514], F32, f"vpad{i}") for i in range(2)]
        gpadh = [Buf(f"gpadh{i}") for i in range(2)]
        vpadh = [Buf(f"vpadh{i}") for i in range(2)]
        gacc = [A.alloc([128, 512], F32, f"gacc{i}") for i in range(3)]
        vacc = [A.alloc([128, 512], F32, f"vacc{i}") for i in range(3)]
        th4 = [A.alloc([128, 512], F32, "th40")] * 2
        gT = A.alloc([128, 22, 512], BF16, "gT")
        cnt4 = [0]
        wup_piece = {}
        wunits = [("up", k, ci) for ci in (0, 5, 6, 1, 7, 2, 8, 3, 9, 4, 10) for k in range(8)]
        wunits += [("dn", k, cc) for k in range(22) for cc in (0, 512)]
        upos = [0]

        def emit_units(n):
            for _ in range(n):
                if upos[0] >= len(wunits):
                    return
                kind, k, x_ = wunits[upos[0]]
                upos[0] += 1
                st = wst4[cnt4[0] % len(wst4)]
                eng = DVE if cnt4[0] % 2 == 0 else ACT
                cnt4[0] += 1
                if kind == "up":
                    cc = x_ * 512
                    pb = Buf(f"wup_{k}_{x_}")
                    wup_piece[(k, x_)] = pb
                    P.dma(I_dma(st.ap[:, 0:512], wup_v[:, k, cc:cc + 512]), writes=[st.b], sbuf=st.b)
                    if eng == ACT:
                        P.op(ACT, I_act(wup.ap[:, k, cc:cc + 512], st.ap[:, 0:512], AF.Identity, scale=pc("gffn", k)),
                             reads=[st.b, par.b], writes=[pb])
                    else:
                        P.op(DVE, I_ts(wup.ap[:, k, cc:cc + 512], st.ap[:, 0:512], pc("gffn", k), 1.0, ALU.mult, ALU.mult),
                             reads=[st.b, par.b], writes=[pb])
                else:
                    cc = x_
                    P.dma(I_dma(st.ap[:, 0:512], wdn_v[:, k, cc:cc + 512]), writes=[st.b], sbuf=st.b)
                    if eng == ACT:
                        P.op(ACT, I_act(wdn.ap[:, k, cc:cc + 512], st.ap[:, 0:512], AF.Copy), reads=[st.b], writes=[wdn.b])
                    else:
                        P.op(DVE, I_copy(wdn.ap[:, k, cc:cc + 512], st.ap[:, 0:512]), reads=[st.b], writes=[wdn.b])

        emit_units(24)
        P.op(DVE, I_ts(cwf.ap[:, 0:66], pc("cwf", 0, 66), 0.5, None, ALU.mult), reads=[par.b], writes=[cwf.b])
        P.op(DVE, I_copy(cwf.ap[:, 66:132], pc("cwf", 66, 132)), reads=[par.b], writes=[cwf.b])
        P.op(DVE, I_ts(cbf.ap[:, 0:22], pc("cbf", 0, 22), 0.5, None, ALU.mult), reads=[par.b], writes=[cbf.b])
        P.op(DVE, I_copy(cbf.ap[:, 22:44], pc("cbf", 22, 44)), reads=[par.b], writes=[cbf.b])
        P.op(POOL, I_memset(halo.ap, 0.0), writes=[halo.b])

        def p4_load(t):
            s = 0
            P.dma(I_dma(h2t[s].ap, h2T_v[:, :, t * 512:(t + 1) * 512]), reads=[B_h2T_d], writes=[h2t[s].b], sbuf=h2t[s].b)

        def conv3(pad, acc, blk, first_eng):
            P.op(first_eng, I_ts(acc.ap, pad.ap[:, 0:512], cwf.ap[:, blk * 3:blk * 3 + 1], cbf.ap[:, blk:blk + 1], ALU.mult, ALU.add),
                 reads=[pad.b, cwf.b, cbf.b], writes=[acc.b])
            for k in range(1, 3):
                P.op(DVE, I_stt(acc.ap, pad.ap[:, k:k + 512], cwf.ap[:, blk * 3 + k:blk * 3 + k + 1], acc.ap, ALU.mult, ALU.add),
                     reads=[pad.b, cwf.b, acc.b], writes=[acc.b])

        p4_load(0)
        gbanks = [banks[0], banks[1]]
        vbanks = [banks[2], banks[3]]
        dbanks = [banks[4], banks[5], banks[6], banks[7]]
        for t in range(NT):
            hs = h2t[0]
            if t > 0:
                p4_load(t)
            def ffn_A(c):
                s = c % 2
                bg = gbanks[s]
                bv = vbanks[s]
                for kc in range(8):
                    P.op(PE, I_mm(bg.ap, wup.ap[:, kc, c * 128:(c + 1) * 128], hs.ap[:, kc, :], start=(kc == 0), stop=(kc == 7)),
                         reads=[wup_piece[(kc, (c * 128) // 512)], hs.b], writes=[bg.b])
                for kc in range(8):
                    P.op(PE, I_mm(bv.ap, wup.ap[:, kc, DFF + c * 128:DFF + (c + 1) * 128], hs.ap[:, kc, :], start=(kc == 0), stop=(kc == 7)),
                         reads=[wup_piece[(kc, (DFF + c * 128) // 512)], hs.b], writes=[bv.b])
                for (pad, padh, acc, bk, blk) in ((gpad[s], gpadh[s], gacc[c % 3], bg, c), (vpad[s], vpadh[s], vacc[c % 3], bv, 22 + c)):
                    P.op(POOL, I_copy(pad.ap[:, 0:2], halo.ap[:, blk, :]), reads=[halo.b], writes=[padh])
                    P.op(ACT, I_act(acc.ap, bk.ap, AF.Identity, bias=pc("cbf", blk), scale=pc("cwf", blk * 3 + 2)),
                         reads=[bk.b, par.b], writes=[acc.b])
                    P.op(ACT, I_act(pad.ap[:, 2:514], bk.ap, AF.Copy), reads=[bk.b], writes=[pad.b])
                    P.op(POOL, I_copy(halo.ap[:, blk, :], pad.ap[:, 512:514]), reads=[pad.b], writes=[halo.b])

            def ffn_B(c):
                s = c % 2
                for (pad, padh, acc, blk) in ((gpad[s], gpadh[s], gacc[c % 3], c), (vpad[s], vpadh[s], vacc[c % 3], 22 + c)):
                    for k in range(2):
                        P.op(DVE, I_stt(acc.ap, pad.ap[:, k:k + 512], pc("cwf", blk * 3 + k), acc.ap, ALU.mult, ALU.add),
                             reads=[pad.b, padh, par.b, acc.b], writes=[acc.b])

            def ffn_C(c):
                s = c % 2
                P.op(ACT, I_act(th4[s].ap, gacc[c % 3].ap, AF.Silu), reads=[gacc[c % 3].b], writes=[th4[s].b])
                P.op(POOL, I_tt(gT.ap[:, c, :], th4[s].ap, vacc[c % 3].ap, ALU.mult), reads=[th4[s].b, vacc[c % 3].b], writes=[gT.b])

            for step in range(22 + 2):
                if t == 0:
                    emit_units(5)
                if step < 22:
                    ffn_A(step)
                if 0 <= step - 1 < 22:
                    ffn_B(step - 1)
                if 0 <= step - 2 < 22:
                    ffn_C(step - 2)
            if t == 0:
                emit_units(len(wunits))
            n = 0
            for hb in range(2):
                P.dma(I_dma(x1t.ap, x1_v[t][:, 2 * hb:2 * hb + 2, :]), reads=[B_x1_d], writes=[x1t.b], sbuf=x1t.b)
                for bb in range(2):
                    b = 2 * hb + bb
                    for half in range(2):
                        bk = dbanks[n % 4]
                        n += 1
                        for c in range(22):
                            P.op(PE, I_mm(bk.ap, gT.ap[:, c, b * 128:(b + 1) * 128], wdn.ap[:, c, half * 512:(half + 1) * 512],
                                          start=(c == 0), stop=(c == 21)), reads=[gT.b, wdn.b], writes=[bk.b])
                        xs_ = x1t.ap[:, bb, half * 512:(half + 1) * 512]
                        P.op(DVE, I_tt(xs_, xs_, bk.ap, ALU.add), reads=[x1t.b, bk.b], writes=[x1t.b])
                P.dma(I_dma(out_v[t][:, 2 * hb:2 * hb + 2, :], x1t.ap), reads=[x1t.b], writes=[B_out_d], sbuf=x1t.b, eng=POOL)


    except StopBuild:
        pass
    finals = [B_out_d]
    if DEBUG:
        finals += [B_hT_d, B_mixT_d, B_x1_d, B_h2T_d]
    nsem = P.emit(nc, final_wait_bufs=finals)
    return nc, nsem, len(P.ops)


def _pack_params(norm_mix_w, ssd_conv_w, ssd_conv_b, ssd_dt_bias, ssd_a_log, ssd_d, ssd_norm_w,
                 fox_f_bias, fox_q_norm_w, fox_k_norm_w, norm_ffn_w, ffn_conv_w, ffn_conv_b):
    par = np.zeros((128, NPAR), np.float32)

    def put(name, arr):
        lo, hi = PCOL[name]
        par[:, lo:hi] = arr

    put("gmix", norm_mix_w[0].reshape(8, 128).T)
    put("gffn", norm_ffn_w[0].reshape(8, 128).T)
    put("gssd", ssd_norm_w[0].reshape(8, 128).T)
    put("cws", ssd_conv_w[0].reshape(4, 12, 128).transpose(2, 1, 0).reshape(128, 48))
    put("cbs", ssd_conv_b[0].reshape(12, 128).T)
    put("dcol", np.repeat(ssd_d[0].reshape(8, 2), 64, axis=1).T)
    put("wq", np.tile(fox_q_norm_w[0], 2)[:, None])
    put("wk", np.tile(fox_k_norm_w[0], 2)[:, None])
    fb = np.zeros((128, 1), np.float32)
    fb[0:16, 0] = fox_f_bias[0]
    put("fb", fb)
    put("dtb", np.tile(ssd_dt_bias[0][None, :], (128, 1)))
    put("alog", np.tile(ssd_a_log[0][None, :], (128, 1)))
    put("cwf", ffn_conv_w[0].reshape(3, 44, 128).transpose(2, 1, 0).reshape(128, 132))
    put("cbf", ffn_conv_b[0].reshape(44, 128).T)
    return par


def _rows_to_pk(w, nk):
    c = w.shape[1]
    return np.ascontiguousarray(w.reshape(nk, 128, c).transpose(1, 0, 2).reshape(128, nk * c))


_CACHE = {}


def kernel(x, norm_mix_w, w_in, ssd_conv_w, ssd_conv_b, ssd_dt_bias, ssd_a_log, ssd_d,
           ssd_norm_w, fox_f_bias, fox_q_norm_w, fox_k_norm_w, w_out, norm_ffn_w,
           w_up, ffn_conv_w, ffn_conv_b, w_down):
    f = lambda a: np.asarray(a, dtype=np.float32)
    x = f(x)
    par = _pack_params(f(norm_mix_w), f(ssd_conv_w), f(ssd_conv_b), f(ssd_dt_bias), f(ssd_a_log), f(ssd_d),
                       f(ssd_norm_w), f(fox_f_bias), f(fox_q_norm_w), f(fox_k_norm_w), f(norm_ffn_w),
                       f(ffn_conv_w), f(ffn_conv_b))
    cst = np.concatenate([np.eye(128, dtype=np.float32), np.triu(np.ones((128, 128), np.float32)),
                          np.kron(np.eye(2, dtype=np.float32), np.ones((64, 64), np.float32))], axis=1)
    win = _rows_to_pk(f(w_in)[0], 8)
    wout = _rows_to_pk(f(w_out)[0], 16)
    wup = _rows_to_pk(f(w_up)[0], 8)
    wdn = _rows_to_pk(f(w_down)[0], 22)
    if "nc" not in _CACHE:
        _CACHE["nc"] = build_program()
    nc, nsem, nops = _CACHE["nc"]
    n = x.shape[0]
    in_maps = []
    for c in range(n):
        in_maps.append({"x": np.ascontiguousarray(x[c]), "win": win, "wout": wout, "wup": wup, "wdn": wdn,
                        "par": par, "cst": cst})
    res = run_bass_kernel_spmd(nc, in_maps, core_ids=list(range(n)))
    if DEBUG:
        _CACHE["dbg"] = res.results
    return np.stack([r["out"] for r in res.results], axis=0).astype(np.float32)
```
